# Optimizing a Trainium2 kernel written in Bass

```python
import functools
import jax, jax.numpy as jnp
from jax import lax
import numpy as np

D_MODEL = 2048
BATCH = 4
SEQ = 4096
DEPTH = 1
DEC_BATCH = 128
DEC_SEQ = 4
PAST_LEN = 16384
PAGE_SIZE = 128

D_RNN = D_MODEL // 2
RNN_BLOCKS = 8
RNN_BLOCK = D_RNN // RNN_BLOCKS
CONV_W = 4
LRU_C = 8.0
HEAD_DIM = 64
N_Q_HEADS = D_MODEL // 2 // HEAD_DIM
N_KV_HEADS = 4
GROUP = N_Q_HEADS // N_KV_HEADS
WINDOW = 128
ROT_DIM = HEAD_DIM // 4
ROPE_THETA = 500000.0
N_KEYS = 128
N_EXPERTS = N_KEYS * N_KEYS
PEER_HEADS = 8
PEER_TOPK = 16
D_KEY = 256
PEER_CHUNK = 128
D_PLE = 256
EPS = 1e-6
NEG_INF = -1e30
IN_SIZES = [D_RNN, D_RNN, N_Q_HEADS * HEAD_DIM, N_KV_HEADS * HEAD_DIM, N_KV_HEADS * HEAD_DIM, D_MODEL, D_MODEL]
D_IN = sum(IN_SIZES)

kernel_name = "hawk_swa_sink_peer_hybrid_step"


def rmsnorm(x, g):
    xf = x.astype(jnp.float32)
    y = xf * lax.rsqrt(jnp.mean(xf * xf, axis=-1, keepdims=True) + EPS) * g.astype(jnp.float32)
    return y.astype(x.dtype)


def rope_partial(x, pos):
    half = ROT_DIM // 2
    inv = ROPE_THETA ** (-jnp.arange(0, ROT_DIM, 2, dtype=jnp.float32) / ROT_DIM)
    ang = pos.astype(jnp.float32)[:, None] * inv[None, :]
    cos = jnp.cos(ang)[:, None, :]
    sin = jnp.sin(ang)[:, None, :]
    xf = x.astype(jnp.float32)
    x1, x2 = xf[..., :half], xf[..., half:ROT_DIM]
    y = jnp.concatenate([x1 * cos - x2 * sin, x2 * cos + x1 * sin, xf[..., ROT_DIM:]], axis=-1)
    return y.astype(x.dtype)


def causal_conv(x, buf, w, b):
    T = x.shape[1]
    xp = jnp.concatenate([buf.astype(x.dtype), x], axis=1)
    y = b + sum(w[k] * xp[:, k:k + T] for k in range(CONV_W))
    return y, xp[:, xp.shape[1] - (CONV_W - 1):]


def linear_recurrence(a, b, h0):
    b = b.at[:, 0].add(a[:, 0] * h0)

    def combine(c1, c2):
        a1, b1 = c1
        a2, b2 = c2
        return a1 * a2, a2 * b1 + b2

    _, h = lax.associative_scan(combine, (a, b), axis=1)
    return h


def rglru(x, h0, pos, w_r, b_r, w_i, b_i, lam):
    B, T, _ = x.shape
    xb = x.reshape(B, T, RNN_BLOCKS, RNN_BLOCK)
    r = jax.nn.sigmoid((jnp.einsum('btnc,ncd->btnd', xb, w_r).reshape(B, T, D_RNN) + b_r).astype(jnp.float32))
    i = jax.nn.sigmoid((jnp.einsum('btnc,ncd->btnd', xb, w_i).reshape(B, T, D_RNN) + b_i).astype(jnp.float32))
    log_a = -LRU_C * r * jax.nn.softplus(-lam.astype(jnp.float32))
    a = jnp.exp(log_a)
    mult = jnp.where(pos[None, :, None] == 0, 1.0, jnp.sqrt(-jnp.expm1(2.0 * log_a)))
    h = linear_recurrence(a, mult * i * x.astype(jnp.float32), h0.astype(jnp.float32))
    return h.astype(x.dtype), h[:, -1]


def sink_attention(q, k, v, mask, sinks):
    s = jnp.einsum('bnqkgd,bnskd->bnkgqs', q.astype(jnp.float32), k.astype(jnp.float32)) * (HEAD_DIM ** -0.5)
    s = jnp.where(mask[None, :, None, None], s, NEG_INF)
    sk = sinks.astype(jnp.float32).reshape(N_KV_HEADS, GROUP)[None, None, :, :, None, None]
    m = jnp.maximum(jnp.max(s, axis=-1, keepdims=True), sk)
    p = jnp.exp(s - m)
    denom = jnp.sum(p, axis=-1, keepdims=True) + jnp.exp(sk - m)
    return jnp.einsum('bnkgqs,bnskd->bnqkgd', p / denom, v.astype(jnp.float32))


def swa_prompt(q, k, v, sinks):
    B, S = q.shape[:2]
    nb = S // WINDOW
    qb = q.reshape(B, nb, WINDOW, N_KV_HEADS, GROUP, HEAD_DIM)
    kb = k.reshape(B, nb, WINDOW, N_KV_HEADS, HEAD_DIM)
    vb = v.reshape(B, nb, WINDOW, N_KV_HEADS, HEAD_DIM)
    kk = jnp.concatenate([jnp.concatenate([jnp.zeros_like(kb[:, :1]), kb[:, :-1]], axis=1), kb], axis=2)
    vv = jnp.concatenate([jnp.concatenate([jnp.zeros_like(vb[:, :1]), vb[:, :-1]], axis=1), vb], axis=2)
    i = jnp.arange(WINDOW)[:, None]
    j = jnp.arange(2 * WINDOW)[None, :]
    d = WINDOW + i - j
    valid = (d >= 0) & (d < WINDOW)
    mask = jnp.where(jnp.arange(nb)[:, None, None] == 0, valid & (j >= WINDOW), valid)
    o = sink_attention(qb, kk, vv, mask, sinks).reshape(B, S, N_Q_HEADS * HEAD_DIM)
    keep = min(WINDOW, S)
    return o, k[:, S - keep:], v[:, S - keep:]


def swa_sample(q, k, v, sinks, cache_k, cache_v, past_len):
    B, T = q.shape[:2]
    w_buf = cache_k.shape[1]
    kk = jnp.concatenate([cache_k.astype(k.dtype), k], axis=1)
    vv = jnp.concatenate([cache_v.astype(v.dtype), v], axis=1)
    qpos = past_len + jnp.arange(T)
    kpos = past_len - w_buf + jnp.arange(w_buf + T)
    d = qpos[:, None] - kpos[None, :]
    mask = (d >= 0) & (d < WINDOW)
    o = sink_attention(q.reshape(B, 1, T, N_KV_HEADS, GROUP, HEAD_DIM), kk[:, None], vv[:, None], mask[None], sinks)
    o = o.reshape(B, T, N_Q_HEADS * HEAD_DIM)
    return o, kk[:, kk.shape[1] - w_buf:], vv[:, vv.shape[1] - w_buf:]


def peer(x, w_q, sub_keys, u, v):
    B, T, D = x.shape
    n = B * T
    xt = x.reshape(n, D)
    q = (xt @ w_q).reshape(n, PEER_HEADS, 2, D_KEY // 2)
    s = jnp.einsum('nhpd,hpkd->nhpk', q.astype(jnp.float32), sub_keys.astype(jnp.float32))
    s1, i1 = lax.top_k(s[:, :, 0], PEER_TOPK)
    s2, i2 = lax.top_k(s[:, :, 1], PEER_TOPK)
    cand = (s1[..., :, None] + s2[..., None, :]).reshape(n, PEER_HEADS, PEER_TOPK * PEER_TOPK)
    cidx = (i1[..., :, None] * N_KEYS + i2[..., None, :]).reshape(n, PEER_HEADS, PEER_TOPK * PEER_TOPK)
    top_s, sel = lax.top_k(cand, PEER_TOPK)
    idx = jnp.take_along_axis(cidx, sel, axis=-1)
    g = jax.nn.softmax(top_s, axis=-1)
    n_pad = -(-n // PEER_CHUNK) * PEER_CHUNK
    pad = n_pad - n
    nc = n_pad // PEER_CHUNK
    xc = jnp.pad(xt, ((0, pad), (0, 0))).reshape(nc, PEER_CHUNK, D)
    ic = jnp.pad(idx, ((0, pad), (0, 0), (0, 0))).reshape(nc, PEER_CHUNK, PEER_HEADS, PEER_TOPK)
    gc = jnp.pad(g, ((0, pad), (0, 0), (0, 0))).reshape(nc, PEER_CHUNK, PEER_HEADS, PEER_TOPK)

    def chunk(args):
        xb, ib, gb = args
        hb = jax.nn.gelu(jnp.einsum('chkd,cd->chk', u[ib], xb).astype(jnp.float32))
        return jnp.einsum('chk,chkd->cd', (hb * gb).astype(v.dtype), v[ib])

    out = lax.map(chunk, (xc, ic, gc))
    return out.reshape(n_pad, D)[:n].reshape(B, T, D).astype(x.dtype)


def decoder_layer(x, ple, pos, conv_buf, h0, attn_fn, norm_mix, w_in, conv_w, conv_b, w_rgate, b_rgate,
                  w_igate, b_igate, lru_lambda, w_proj_rnn, q_norm, k_norm, attn_sinks, w_proj_attn, w_out,
                  norm_ffn, w_peer_q, peer_sub_keys, peer_u, peer_v, w_ple, norm_ple, w_ple_gate):
    B, T, _ = x.shape
    n1 = rmsnorm(x, norm_mix)
    z = n1 @ w_in
    offs = np.cumsum(IN_SIZES)[:-1].tolist()
    xr, gr, q, k, v, ga, gb = jnp.split(z, offs, axis=-1)
    xr, conv_new = causal_conv(xr, conv_buf, conv_w, conv_b)
    hr, h_last = rglru(xr, h0, pos, w_rgate, b_rgate, w_igate, b_igate, lru_lambda)
    branch_a = (hr * jax.nn.gelu(gr)) @ w_proj_rnn
    q = rope_partial(rmsnorm(q.reshape(B, T, N_Q_HEADS, HEAD_DIM), q_norm), pos)
    k = rope_partial(rmsnorm(k.reshape(B, T, N_KV_HEADS, HEAD_DIM), k_norm), pos)
    v = v.reshape(B, T, N_KV_HEADS, HEAD_DIM)
    o, k_buf, v_buf = attn_fn(q, k, v, attn_sinks)
    branch_b = o.astype(x.dtype) @ w_proj_attn
    merged = jax.nn.sigmoid(ga) * branch_a + jax.nn.sigmoid(gb) * branch_b
    x = x + merged @ w_out
    x = x + peer(rmsnorm(x, norm_ffn), w_peer_q, peer_sub_keys, peer_u, peer_v)
    x = x + (ple @ w_ple) * jax.nn.sigmoid(rmsnorm(x, norm_ple) @ w_ple_gate)
    return x, conv_new, h_last, k_buf, v_buf


def setup_inputs(seed: int = 0) -> dict:
    key = jax.random.key(seed)
    ks = jax.random.split(key, 40)
    f32 = jnp.float32
    nrm = lambda k, shape, scale: jax.random.normal(k, shape, f32) * scale
    w_buf = min(WINDOW, PAST_LEN)
    a0 = jax.random.uniform(ks[20], (DEPTH, D_RNN), f32, 0.9, 0.999)
    return {
        "x_prompt": nrm(ks[0], (BATCH, SEQ, D_MODEL), 1.0),
        "x_sample": nrm(ks[1], (DEC_BATCH, DEC_SEQ, D_MODEL), 1.0),
        "p_prompt": nrm(ks[2], (DEPTH, BATCH, SEQ, D_PLE), 1.0),
        "p_sample": nrm(ks[3], (DEPTH, DEC_BATCH, DEC_SEQ, D_PLE), 1.0),
        "state_conv": nrm(ks[4], (DEPTH, DEC_BATCH, CONV_W - 1, D_RNN), 1.0),
        "state_rglru": nrm(ks[5], (DEPTH, DEC_BATCH, D_RNN), 0.5),
        "cache_k": nrm(ks[6], (DEPTH, DEC_BATCH, w_buf, N_KV_HEADS, HEAD_DIM), 1.0),
        "cache_v": nrm(ks[7], (DEPTH, DEC_BATCH, w_buf, N_KV_HEADS, HEAD_DIM), 1.0),
        "norm_mix": 1.0 + nrm(ks[8], (DEPTH, D_MODEL), 0.01),
        "w_in": nrm(ks[9], (DEPTH, D_MODEL, D_IN), D_MODEL ** -0.5),
        "conv_w": nrm(ks[10], (DEPTH, CONV_W, D_RNN), CONV_W ** -0.5),
        "conv_b": nrm(ks[11], (DEPTH, D_RNN), 0.01),
        "w_rgate": nrm(ks[12], (DEPTH, RNN_BLOCKS, RNN_BLOCK, RNN_BLOCK), RNN_BLOCK ** -0.5),
        "b_rgate": nrm(ks[13], (DEPTH, D_RNN), 0.01),
        "w_igate": nrm(ks[14], (DEPTH, RNN_BLOCKS, RNN_BLOCK, RNN_BLOCK), RNN_BLOCK ** -0.5),
        "b_igate": nrm(ks[15], (DEPTH, D_RNN), 0.01),
        "lru_lambda": jnp.log(a0) - jnp.log1p(-a0),
        "w_proj_rnn": nrm(ks[16], (DEPTH, D_RNN, D_MODEL), D_RNN ** -0.5),
        "q_norm": 1.0 + nrm(ks[17], (DEPTH, HEAD_DIM), 0.01),
        "k_norm": 1.0 + nrm(ks[18], (DEPTH, HEAD_DIM), 0.01),
        "attn_sinks": nrm(ks[19], (DEPTH, N_Q_HEADS), 0.5),
        "w_proj_attn": nrm(ks[21], (DEPTH, N_Q_HEADS * HEAD_DIM, D_MODEL), (N_Q_HEADS * HEAD_DIM) ** -0.5),
        "w_out": nrm(ks[22], (DEPTH, D_MODEL, D_MODEL), D_MODEL ** -0.5),
        "norm_ffn": 1.0 + nrm(ks[23], (DEPTH, D_MODEL), 0.01),
        "w_peer_q": nrm(ks[24], (DEPTH, D_MODEL, PEER_HEADS * D_KEY), D_MODEL ** -0.5),
        "peer_sub_keys": nrm(ks[25], (DEPTH, PEER_HEADS, 2, N_KEYS, D_KEY // 2), (D_KEY // 2) ** -0.5),
        "peer_u": nrm(ks[26], (DEPTH, N_EXPERTS, D_MODEL), D_MODEL ** -0.5),
        "peer_v": nrm(ks[27], (DEPTH, N_EXPERTS, D_MODEL), (PEER_HEADS * PEER_TOPK) ** -0.5),
        "w_ple": nrm(ks[28], (DEPTH, D_PLE, D_MODEL), D_PLE ** -0.5),
        "norm_ple": 1.0 + nrm(ks[29], (DEPTH, D_MODEL), 0.01),
        "w_ple_gate": nrm(ks[30], (DEPTH, D_MODEL, D_MODEL), D_MODEL ** -0.5),
    }


def reference(x_prompt, x_sample, p_prompt, p_sample, state_conv, state_rglru, cache_k, cache_v,
              norm_mix, w_in, conv_w, conv_b, w_rgate, b_rgate, w_igate, b_igate, lru_lambda, w_proj_rnn,
              q_norm, k_norm, attn_sinks, w_proj_attn, w_out, norm_ffn, w_peer_q, peer_sub_keys, peer_u,
              peer_v, w_ple, norm_ple, w_ple_gate):
    B, S, _ = x_prompt.shape
    DB, DS, _ = x_sample.shape
    pos_prompt = jnp.arange(S, dtype=jnp.int32)
    pos_sample = PAST_LEN + jnp.arange(DS, dtype=jnp.int32)
    yp, ys = x_prompt, x_sample
    pc, ph, pk, pv, sc, sh, sk, sv = [], [], [], [], [], [], [], []
    for l in range(DEPTH):
        w = (norm_mix[l], w_in[l], conv_w[l], conv_b[l], w_rgate[l], b_rgate[l], w_igate[l], b_igate[l],
             lru_lambda[l], w_proj_rnn[l], q_norm[l], k_norm[l], attn_sinks[l], w_proj_attn[l], w_out[l],
             norm_ffn[l], w_peer_q[l], peer_sub_keys[l], peer_u[l], peer_v[l], w_ple[l], norm_ple[l],
             w_ple_gate[l])
        yp, c_new, h_new, k_new, v_new = decoder_layer(
            yp, p_prompt[l], pos_prompt,
            jnp.zeros((B, CONV_W - 1, D_RNN), x_prompt.dtype), jnp.zeros((B, D_RNN), jnp.float32),
            swa_prompt, *w)
        pc.append(c_new); ph.append(h_new); pk.append(k_new); pv.append(v_new)
        attn_fn = functools.partial(swa_sample, cache_k=cache_k[l], cache_v=cache_v[l], past_len=PAST_LEN)
        ys, c_new, h_new, k_new, v_new = decoder_layer(
            ys, p_sample[l], pos_sample, state_conv[l], state_rglru[l], attn_fn, *w)
        sc.append(c_new); sh.append(h_new); sk.append(k_new); sv.append(v_new)
    prompt_conv, prompt_rglru = jnp.stack(pc), jnp.stack(ph)
    prompt_k, prompt_v = jnp.stack(pk), jnp.stack(pv)
    sample_conv, sample_rglru = jnp.stack(sc), jnp.stack(sh)
    sample_k, sample_v = jnp.stack(sk), jnp.stack(sv)
    return (yp, ys, prompt_conv, prompt_rglru, prompt_k, prompt_v, sample_conv, sample_rglru, sample_k, sample_v)
```

```python
from contextlib import ExitStack
import numpy as np
import concourse.bass as bass
import concourse.mybir as mybir
from concourse.bass_utils import run_bass_kernel_spmd

F32 = mybir.dt.float32
BF16 = mybir.dt.bfloat16
AF = mybir.ActivationFunctionType
ALU = mybir.AluOpType
AX = mybir.AxisListType

D = 2048
NT = 17
NTP = 16
TOK = NT * 128
EPS = 1e-6


class DSem:
    __slots__ = ("sem", "tot", "gen")

    def __init__(self, sem):
        self.sem = sem
        self.tot = 0
        self.gen = -1


class Res:
    __slots__ = ("name", "lw", "rd", "ds", "lw_eng")

    def __init__(self, name):
        self.name = name
        self.lw = None
        self.rd = {}
        self.ds = {}
        self.lw_eng = None


class FW:
    def __init__(self, nc):
        self.nc = nc
        self.eng = {"pe": nc.tensor, "dve": nc.vector, "act": nc.scalar, "pool": nc.gpsimd, "sp": nc.sync}
        self.esem, self.cnt, self.seen, self._ctx = {}, {}, {}, []
        for k in self.eng:
            cm = nc.semaphore("s_" + k)
            self.esem[k] = cm.__enter__()
            self._ctx.append(cm)
            self.cnt[k] = 0
            self.seen[k] = {}
        self.ninst = 0
        self.free_ds, self.used_ds, self.all_ds = {"hw": [], "sw": []}, {"hw": [], "sw": []}, []
        self.gen = 0

    def res(self, name):
        return Res(name)

    def _wait(self, e, ev):
        if ev is None:
            return
        sem, val = ev
        key = id(sem)
        if self.seen[e].get(key, 0) >= val:
            return
        self.eng[e].wait_ge(sem, val)
        self.seen[e][key] = val

    def _deps(self, e, reads, writes):
        for r in reads:
            self._wait(e, r.lw)
        for w in writes:
            self._wait(e, w.lw)
            for ev in w.rd.values():
                self._wait(e, ev)

    def _commit(self, ev, reads, writes):
        k = id(ev[0])
        for r in reads:
            r.rd[k] = ev
        for w in writes:
            w.lw = ev
            w.rd = {}

    def op(self, e, inst_fn, reads=(), writes=()):
        self._deps(e, reads, writes)
        inst = inst_fn()
        self.cnt[e] += 1
        inst.then_inc(self.esem[e], 1)
        ev = (self.esem[e], self.cnt[e])
        if e == "pe":
            self.seen[e][id(self.esem[e])] = self.cnt[e]
        self._commit(ev, reads, writes)
        for w in writes:
            w.lw_eng = e
        self.ninst += 1
        return inst

    def _get_ds(self, owner, kind):
        ds = owner.ds.get(kind)
        if ds is None or ds.gen != self.gen:
            if self.free_ds[kind]:
                ds = self.free_ds[kind].pop()
            else:
                cm = self.nc.semaphore("d%s%d" % (kind, len(self.all_ds)))
                ds = DSem(cm.__enter__())
                self._ctx.append(cm)
                self.all_ds.append(ds)
            ds.gen = self.gen
            owner.ds[kind] = ds
            self.used_ds[kind].append(ds)
        return ds

    def dma(self, q, out, in_, reads=(), writes=(), owner=None, **kw):
        if writes:
            q = "sp"
        else:
            q = "act" if reads[0].lw_eng == "act" else "pool"
        self._deps(q, reads, writes)
        if owner is None:
            owner = writes[0] if writes else reads[0]
        ds = self._get_ds(owner, "sw" if q == "pool" else "hw")
        inst = self.eng[q].dma_start(out=out, in_=in_, **kw)
        ds.tot += 16
        inst.then_inc(ds.sem, 16)
        self._commit((ds.sem, ds.tot), reads, writes)
        for w in writes:
            w.lw_eng = None
        self.ninst += 1
        return inst

    def barrier(self):
        for e in self.eng:
            for k in self.eng:
                if k != e and self.cnt[k] > 0:
                    self._wait(e, (self.esem[k], self.cnt[k]))
            for ds in self.all_ds:
                if ds.tot > 0:
                    self._wait(e, (ds.sem, ds.tot))
        for kind in ("hw", "sw"):
            self.free_ds[kind].extend(self.used_ds[kind])
            self.used_ds[kind] = []
        self.gen += 1

    def finish(self):
        self.barrier()


class Rot:
    def __init__(self, fw, k, es, name, shape, dt, n):
        self.slots = []
        for i in range(n):
            t = es.enter_context(k.sb("%s%d" % (name, i), shape, dt))
            self.slots.append((t, fw.res("%s%d" % (name, i))))
        self.i = 0

    def next(self):
        s = self.slots[self.i % len(self.slots)]
        self.i += 1
        return s


class K:
    pass


def build(dbg=()):
    nc = bass.Bass("TRN2", target_bir_lowering=False)
    fw = FW(nc)
    k = K()
    k.nc, k.fw = nc, fw
    k.evac_i = 0
    k.dq_i = 0
    k.uid = 0

    def _sb(name, shape, dt):
        k.uid += 1
        return nc.sbuf_tensor("%s_%d" % (name, k.uid), list(shape), dt)

    def _pp(name, shape, dt):
        k.uid += 1
        return nc.psum_tensor("%s_%d" % (name, k.uid), list(shape), dt)
    k.sb, k.pp = _sb, _pp

    def din(name, shape, dt=F32):
        return nc.dram_tensor(name, list(shape), dt, kind="ExternalInput").ap()

    def dout(name, shape, dt=F32):
        return nc.dram_tensor(name, list(shape), dt, kind="ExternalOutput").ap()

    def scr(name, shape, dt=F32):
        kind = "ExternalOutput" if name in dbg else "Internal"
        return nc.dram_tensor(name, list(shape), dt, kind=kind).ap()

    I = {}
    I["x_own"] = din("x_own", [TOK, D])
    I["x_prev"] = din("x_prev", [NTP * 128, D])
    I["ident"] = din("ident", [128, 128])
    I["g_mix"] = din("g_mix", [128, D])
    I["w_in"] = din("w_in", [D, 7680])
    S = {}
    S["zT"] = scr("zT", [6144, TOK])
    S["zTp"] = scr("zTp", [1024, NTP * 128])
    S["zqkv"] = scr("zqkv", [TOK, 1536])
    S["zkvp"] = scr("zkvp", [128, 512])

    es0 = ExitStack()
    idf = es0.enter_context(k.sb("idf", [128, 128], F32))
    idb = es0.enter_context(k.sb("idb", [128, 128], BF16))
    r_idf, r_idb = fw.res("idf"), fw.res("idb")
    fw.dma("sp", idf[:], I["ident"], writes=[r_idf])
    fw.op("act", lambda: nc.scalar.copy(out=idb[:], in_=idf[:]), reads=[r_idf], writes=[r_idb])
    k.idb, k.r_idb, k.idf, k.r_idf = idb, r_idb, idf, r_idf

    def evac(out, in_, reads, writes):
        k.evac_i += 1
        if k.evac_i % 2:
            fw.op("dve", lambda: nc.vector.tensor_copy(out=out, in_=in_), reads=reads, writes=writes)
        else:
            fw.op("act", lambda: nc.scalar.copy(out=out, in_=in_), reads=reads, writes=writes)
    k.evac = evac

    def dq():
        k.dq_i += 1
        return "sp" if k.dq_i % 2 else "act"

    def rms_transpose(es, xt, r_xt, gB, r_gB, junk, r_junk, ss, r_ss, xn, r_xn, pst, r_pst, dstT, r_dstT, ncols=128, defer=False):
        fw.op("act", lambda: nc.scalar.activation(out=junk[:], in_=xt[:], func=AF.Square, accum_out=ss[:]),
              reads=[r_xt], writes=[r_junk, r_ss])
        fw.op("act", lambda: nc.scalar.activation(out=ss[:], in_=ss[:], func=AF.Sqrt, scale=1.0 / D, bias=EPS),
              reads=[r_ss], writes=[r_ss])
        fw.op("dve", lambda: nc.vector.reciprocal(out=ss[:], in_=ss[:]), reads=[r_ss], writes=[r_ss])
        fw.op("dve", lambda: nc.vector.scalar_tensor_tensor(out=xn[:], in0=xt[:], scalar=ss[:, 0:1], in1=gB[:],
                                                            op0=ALU.mult, op1=ALU.mult),
              reads=[r_xt, r_ss, r_gB], writes=[r_xn])
        for kk in range(16):
            fw.op("pe", lambda: nc.tensor.transpose(out=pst[:, kk, :], in_=xn[:, kk * 128:(kk + 1) * 128], identity=idb[:]),
                  reads=[r_xn, r_idb], writes=[r_pst])

        def _ev():
            evac(dstT, pst[:], [r_pst], [r_dstT])
        if defer:
            return _ev
        _ev()
    k.rms_transpose = rms_transpose

    def phase_A(x_dram, nt, fm_groups, fm_out, tm_groups, tm_tiles, tm_out, tm_col0):
        with ExitStack() as es:
            n1T = es.enter_context(k.sb("n1T", [128, 16, nt * 128], BF16))
            r_n1T = [fw.res("n1T%d" % i) for i in range(nt)]
            gB = es.enter_context(k.sb("gB", [128, D], F32))
            r_gB = fw.res("gB")
            fw.dma("sp", gB[:], I["g_mix"], writes=[r_gB])
            xs = Rot(fw, k, es, "xs", [128, D], F32, 2)
            junk = es.enter_context(k.sb("junk", [128, D], BF16))
            r_junk = fw.res("junk")
            sss = Rot(fw, k, es, "ss", [128, 1], F32, 2)
            xns = Rot(fw, k, es, "xn", [128, D], BF16, 2)
            pst = [es.enter_context(k.pp("pst%d" % i, [128, 16, 128], BF16)) for i in range(2)]
            r_pst = [fw.res("pst%d" % i) for i in range(2)]
            pend_ev = None
            for i in range(nt):
                xt, r_xt = xs.next()
                fw.dma(dq(), xt[:], x_dram[i * 128:(i + 1) * 128, :], writes=[r_xt])
                ss, r_ss = sss.next()
                xn, r_xn = xns.next()
                ev_ = rms_transpose(es, xt, r_xt, gB, r_gB, junk, r_junk, ss, r_ss, xn, r_xn, pst[i % 2], r_pst[i % 2],
                                    n1T[:, :, i * 128:(i + 1) * 128], r_n1T[i], defer=True)
                if pend_ev is not None:
                    pend_ev()
                pend_ev = ev_
            pend_ev()
            wst = Rot(fw, k, es, "wst", [128, 16, 256], F32, 2)
            wbs = Rot(fw, k, es, "wb", [128, 16, 256], BF16, 2)
            evs = Rot(fw, k, es, "ev", [128, 512], F32, 3)
            pso = [es.enter_context(k.pp("pso%d" % i, [128, 512], F32)) for i in range(4)]
            r_pso = [fw.res("pso%d" % i) for i in range(4)]
            pi = 0
            ntok = nt * 128
            glist = sorted(set(fm_groups) | set(tm_groups))

            def fetch_w(g):
                ws, r_ws = wst.next()
                fw.dma(dq(), ws[:], I["w_in"][:, g * 256:(g + 1) * 256].rearrange("(k p) c -> p k c", p=128), writes=[r_ws])
                wb, r_wb = wbs.next()
                fw.op("pool", lambda: nc.gpsimd.tensor_copy(out=wb[:], in_=ws[:]), reads=[r_ws], writes=[r_wb])
                return wb, r_wb

            nxt_w = fetch_w(glist[0])
            for gi_, g in enumerate(glist):
                wb, r_wb = nxt_w
                if gi_ + 1 < len(glist):
                    nxt_w = fetch_w(glist[gi_ + 1])
                if g in fm_groups:
                    for c2 in range(2):
                        cc = g * 2 + c2
                        row = (cc if cc < 16 else cc - 12) * 128
                        for t0 in range(0, ntok, 512):
                            n = min(512, ntok - t0)
                            ps, r_ps = pso[pi % 4], r_pso[pi % 4]
                            pi += 1
                            tiles = range(t0 // 128, (t0 + n) // 128)
                            for kk in range(16):
                                fw.op("pe", lambda: nc.tensor.matmul(ps[:, 0:n], lhsT=wb[:, kk, c2 * 128:(c2 + 1) * 128],
                                                                     rhs=n1T[:, kk, t0:t0 + n], start=(kk == 0), stop=(kk == 15)),
                                      reads=[r_wb] + [r_n1T[t] for t in tiles], writes=[r_ps])
                            ev, r_ev = evs.next()
                            evac(ev[:, 0:n], ps[:, 0:n], [r_ps], [r_ev])
                            fw.dma(dq(), fm_out[row:row + 128, t0:t0 + n], ev[:, 0:n], reads=[r_ev])
                            k.bg()
                if g in tm_groups:
                    for t in tm_tiles:
                        ps, r_ps = pso[pi % 4], r_pso[pi % 4]
                        pi += 1
                        for kk in range(16):
                            fw.op("pe", lambda: nc.tensor.matmul(ps[:, 0:256], lhsT=n1T[:, kk, t * 128:(t + 1) * 128],
                                                                 rhs=wb[:, kk, :], start=(kk == 0), stop=(kk == 15)),
                                  reads=[r_wb, r_n1T[t]], writes=[r_ps])
                        ev, r_ev = evs.next()
                        evac(ev[:, 0:256], ps[:, 0:256], [r_ps], [r_ev])
                        k.bg()
                        c0 = g * 256 - tm_col0
                        ti = tm_tiles.index(t) if tm_out is S["zkvp"] else t
                        fw.dma(dq(), tm_out[ti * 128:(ti + 1) * 128, c0:c0 + 256], ev[:, 0:256], reads=[r_ev])
        fw.barrier()


    I["rnn_vec"] = din("rnn_vec", [128, 8, 4])
    I["conv_wT"] = din("conv_wT", [128, 8, 4])
    I["w_gates"] = din("w_gates", [128, 2, 8, 128])
    I["state_convT"] = din("state_convT", [128, 8, 16, 3])
    I["state_hT"] = din("state_hT", [128, 8, 16])
    I["flags"] = din("flags", [128, 2])
    S["hgT"] = scr("hgT", [1024, TOK], BF16)
    O = {}
    O["o_pconv"] = dout("o_pconv", [1024, 3])
    O["o_prglru"] = dout("o_prglru", [1024, 1])
    O["o_sconv"] = dout("o_sconv", [1024, 16, 3])
    O["o_srglru"] = dout("o_srglru", [1024, 16])
    k.O = O

    def phase_B():
        TP = 4096
        with ExitStack() as es:
            def T(name, shape, dt=F32):
                return es.enter_context(k.sb(name, shape, dt)), fw.res(name)
            rv, r_rv = T("rv", [128, 8, 4])
            cw, r_cw = T("cw", [128, 8, 4])
            wgf, r_wgf = T("wgf", [128, 2, 8, 128])
            wg, r_wg = T("wg", [128, 2, 8, 128], BF16)
            flg, r_flg = T("flg", [128, 2])
            sp_, r_sp = T("sp_", [128, 8])
            c8, r_c8 = T("c8", [128, 8])
            c16, r_c16 = T("c16", [128, 8])
            fw.dma("sp", rv[:], I["rnn_vec"], writes=[r_rv])
            fw.dma("sp", cw[:], I["conv_wT"], writes=[r_cw])
            fw.dma("sp", wgf[:], I["w_gates"], writes=[r_wgf])
            fw.dma("sp", flg[:], I["flags"], writes=[r_flg])
            fw.op("act", lambda: nc.scalar.copy(out=wg[:], in_=wgf[:]), reads=[r_wgf], writes=[r_wg])
            fw.op("act", lambda: nc.scalar.activation(out=sp_[:], in_=rv[:, :, 3], func=AF.Exp, scale=-1.0), reads=[r_rv], writes=[r_sp])
            fw.op("act", lambda: nc.scalar.activation(out=sp_[:], in_=sp_[:], func=AF.Ln, bias=1.0), reads=[r_sp], writes=[r_sp])
            fw.op("dve", lambda: nc.vector.tensor_scalar(out=c8[:], in0=sp_[:], scalar1=-8.0, scalar2=None, op0=ALU.mult), reads=[r_sp], writes=[r_c8])
            fw.op("dve", lambda: nc.vector.tensor_scalar(out=c16[:], in0=sp_[:], scalar1=-16.0, scalar2=None, op0=ALU.mult), reads=[r_sp], writes=[r_c16])
            XB2 = [T("XB%d" % i, [128, TP + 3]) for i in range(2)]
            xc2 = [T("xc%d" % i, [128, TP]) for i in range(2)]
            xcb2 = [T("xcb%d" % i, [128, TP], BF16) for i in range(2)]
            rr, r_rr = T("rr", [128, TP])
            ii, r_ii = T("ii", [128, TP])
            aa, r_aa = T("aa", [128, TP])
            mt, r_mt = T("mt", [128, TP])
            hh, r_hh = T("hh", [128, TP])
            h0, r_h0 = T("h0", [128, 1])
            gr, r_gr = T("gr", [128, 2048 + 64])
            hgb, r_hgb = T("hgb", [128, 2048 + 128], BF16)
            XS2 = [T("XS%d" % i, [128, 16, 7]) for i in range(2)]
            xs2 = [T("xs_%d" % i, [128, 16, 4]) for i in range(2)]
            xsb2 = [T("xsb%d" % i, [128, 64], BF16) for i in range(2)]
            hs2 = [T("hs%d" % i, [128, 16]) for i in range(2)]
            rs, r_rs = T("rs", [128, 64])
            is_, r_is = T("is_", [128, 64])
            as_, r_as = T("as_", [128, 64])
            ms, r_ms = T("ms", [128, 64])
            ps = [es.enter_context(k.pp("psB%d" % i, [128, 512], F32)) for i in range(4)]
            r_ps = [fw.res("psB%d" % i) for i in range(4)]
            pi = 0
            for (XB_, r_XB_) in XB2:
                fw.op("dve", lambda: nc.vector.memset(XB_[:, 0:3], 0.0), writes=[r_XB_])
            fw.op("dve", lambda: nc.vector.memset(hgb[:, 2048 + 64:], 0.0), writes=[r_hgb])

            def stage1(b):
                cs = slice(b * 128, (b + 1) * 128)
                XB, r_XB = XB2[b % 2]
                xc, r_xc = xc2[b % 2]
                xcb, r_xcb = xcb2[b % 2]
                XS, r_XS = XS2[b % 2]
                xs_, r_xs = xs2[b % 2]
                xsb, r_xsb = xsb2[b % 2]
                hs, r_hs = hs2[b % 2]
                fw.dma("sp", XB[:, 3:3 + 2048], S["zTp"][cs, :], writes=[r_XB])
                fw.dma("act", XB[:, 3 + 2048:], S["zT"][cs, 0:2048], writes=[r_XB])
                fw.dma("act", XS[:, :, 0:3], I["state_convT"][:, b], writes=[r_XS])
                fw.dma("act", XS[:, :, 3:7], S["zT"][cs, 2048:2048 + 64].rearrange("p (s i) -> p s i", i=4), writes=[r_XS])
                fw.dma("sp", hs[:], I["state_hT"][:, b], writes=[r_hs])

                def conv(dst, r_dst, src_fn, r_src):
                    fw.op("dve", lambda: nc.vector.tensor_scalar(out=dst, in0=src_fn(3), scalar1=cw[:, b, 3:4], scalar2=rv[:, b, 0:1],
                                                                 op0=ALU.mult, op1=ALU.add), reads=[r_src, r_cw, r_rv], writes=[r_dst])
                    for kk in (2, 1, 0):
                        fw.op("dve", lambda: nc.vector.scalar_tensor_tensor(out=dst, in0=src_fn(kk), scalar=cw[:, b, kk:kk + 1], in1=dst,
                                                                            op0=ALU.mult, op1=ALU.add), reads=[r_src, r_cw, r_dst], writes=[r_dst])
                conv(xc[:], r_xc, lambda kk: XB[:, kk:kk + TP], r_XB)
                conv(xs_[:], r_xs, lambda kk: XS[:, :, kk:kk + 4], r_XS)

            def cast1(b):
                xc, r_xc = xc2[b % 2]
                xcb, r_xcb = xcb2[b % 2]
                xs_, r_xs = xs2[b % 2]
                xsb, r_xsb = xsb2[b % 2]
                fw.op("act", lambda: nc.scalar.copy(out=xcb[:], in_=xc[:]), reads=[r_xc], writes=[r_xcb])
                fw.op("act", lambda: nc.scalar.copy(out=xsb[:], in_=xs_[:].rearrange("p s i -> p (s i)")), reads=[r_xs], writes=[r_xsb])

            stage1(0)
            cast1(0)
            for b in range(8):
                cs = slice(b * 128, (b + 1) * 128)
                XB, r_XB = XB2[b % 2]
                xc, r_xc = xc2[b % 2]
                xcb, r_xcb = xcb2[b % 2]
                XS, r_XS = XS2[b % 2]
                xs_, r_xs = xs2[b % 2]
                xsb, r_xsb = xsb2[b % 2]
                hs, r_hs = hs2[b % 2]
                fw.dma("sp", gr[:], S["zT"][1024 + b * 128:1024 + (b + 1) * 128, 0:2048 + 64], writes=[r_gr])
                if b + 1 < 8:
                    stage1(b + 1)
                for gi, (dst, r_dst, dsts, r_dsts) in enumerate(((rr, r_rr, rs, r_rs), (ii, r_ii, is_, r_is))):
                    for c in range(TP // 512):
                        p_, r_p = ps[pi % 4], r_ps[pi % 4]
                        pi += 1
                        fw.op("pe", lambda: nc.tensor.matmul(p_[:], lhsT=wg[:, gi, b, :], rhs=xcb[:, c * 512:(c + 1) * 512], start=True, stop=True),
                              reads=[r_wg, r_xcb], writes=[r_p])
                        fw.op("act", lambda: nc.scalar.activation(out=dst[:, c * 512:(c + 1) * 512], in_=p_[:], func=AF.Sigmoid, bias=rv[:, b, 1 + gi:2 + gi]),
                              reads=[r_p, r_rv], writes=[r_dst])
                    p_, r_p = ps[pi % 4], r_ps[pi % 4]
                    pi += 1
                    fw.op("pe", lambda: nc.tensor.matmul(p_[:, 0:64], lhsT=wg[:, gi, b, :], rhs=xsb[:], start=True, stop=True),
                          reads=[r_wg, r_xsb], writes=[r_p])
                    fw.op("act", lambda: nc.scalar.activation(out=dsts[:], in_=p_[:, 0:64], func=AF.Sigmoid, bias=rv[:, b, 1 + gi:2 + gi]),
                          reads=[r_p, r_rv], writes=[r_dsts])
                for (a_, r_a, m_, r_m, r__, r_r) in ((aa, r_aa, mt, r_mt, rr, r_rr), (as_, r_as, ms, r_ms, rs, r_rs)):
                    fw.op("act", lambda: nc.scalar.activation(out=a_[:], in_=r__[:], func=AF.Exp, scale=c8[:, b:b + 1]), reads=[r_r, r_c8], writes=[r_a])
                    fw.op("act", lambda: nc.scalar.activation(out=m_[:], in_=r__[:], func=AF.Exp, scale=c16[:, b:b + 1]), reads=[r_r, r_c16], writes=[r_m])
                    fw.op("act", lambda: nc.scalar.activation(out=m_[:], in_=m_[:], func=AF.Sqrt, scale=-1.0, bias=1.0), reads=[r_m], writes=[r_m])
                fw.op("dve", lambda: nc.vector.memset(mt[:, 0:1], 1.0), writes=[r_mt])
                fw.op("dve", lambda: nc.vector.tensor_scalar(out=mt[:, 2048:2049], in0=mt[:, 2048:2049], scalar1=flg[:, 0:1], scalar2=flg[:, 1:2],
                                                             op0=ALU.mult, op1=ALU.add), reads=[r_mt, r_flg], writes=[r_mt])
                if b + 1 < 8:
                    cast1(b + 1)
                fw.op("dve", lambda: nc.vector.tensor_tensor(out=mt[:], in0=mt[:], in1=ii[:], op=ALU.mult), reads=[r_mt, r_ii], writes=[r_mt])
                fw.op("dve", lambda: nc.vector.tensor_tensor(out=mt[:], in0=mt[:], in1=xc[:], op=ALU.mult), reads=[r_mt, r_xc], writes=[r_mt])
                fw.op("dve", lambda: nc.vector.tensor_tensor(out=ms[:], in0=ms[:], in1=is_[:], op=ALU.mult), reads=[r_ms, r_is], writes=[r_ms])
                fw.op("dve", lambda: nc.vector.tensor_tensor(out=ms[:], in0=ms[:], in1=xs_[:].rearrange("p s i -> p (s i)"), op=ALU.mult), reads=[r_ms, r_xs], writes=[r_ms])
                fw.op("dve", lambda: nc.vector.tensor_tensor_scan(out=hh[:, 0:2048], data0=aa[:, 0:2048], data1=mt[:, 0:2048], initial=0.0,
                                                                  op0=ALU.mult, op1=ALU.add), reads=[r_aa, r_mt], writes=[r_hh])
                fw.op("dve", lambda: nc.vector.tensor_scalar(out=h0[:], in0=hh[:, 2047:2048], scalar1=flg[:, 0:1], scalar2=None, op0=ALU.mult),
                      reads=[r_hh, r_flg], writes=[r_h0])
                fw.op("dve", lambda: nc.vector.tensor_tensor_scan(out=hh[:, 2048:], data0=aa[:, 2048:], data1=mt[:, 2048:], initial=h0[:, 0:1],
                                                                  op0=ALU.mult, op1=ALU.add), reads=[r_aa, r_mt, r_h0], writes=[r_hh])
                a3 = as_[:].rearrange("p (s i) -> p s i", i=4)
                b3 = ms[:].rearrange("p (s i) -> p s i", i=4)
                hq = rs[:].rearrange("p (s i) -> p s i", i=4)
                for i4 in range(4):
                    prev = hs[:] if i4 == 0 else hq[:, :, i4 - 1]
                    fw.op("dve", lambda: nc.vector.tensor_tensor(out=hq[:, :, i4], in0=a3[:, :, i4], in1=prev, op=ALU.mult),
                          reads=[r_as, r_hs, r_rs], writes=[r_rs])
                    fw.op("dve", lambda: nc.vector.tensor_tensor(out=hq[:, :, i4], in0=hq[:, :, i4], in1=b3[:, :, i4], op=ALU.add),
                          reads=[r_ms, r_rs], writes=[r_rs])
                fw.op("act", lambda: nc.scalar.activation(out=gr[:], in_=gr[:], func=AF.Gelu_apprx_tanh), reads=[r_gr], writes=[r_gr])
                fw.op("dve", lambda: nc.vector.tensor_tensor(out=hgb[:, 0:2048], in0=hh[:, 2048:], in1=gr[:, 0:2048], op=ALU.mult),
                      reads=[r_hh, r_gr], writes=[r_hgb])
                fw.op("dve", lambda: nc.vector.tensor_tensor(out=hgb[:, 2048:2048 + 64], in0=rs[:], in1=gr[:, 2048:], op=ALU.mult),
                      reads=[r_rs, r_gr], writes=[r_hgb])
                fw.dma("sp", S["hgT"][cs, :], hgb[:], reads=[r_hgb])
                fw.dma("act", O["o_pconv"][cs, :], XB[:, TP:TP + 3], reads=[r_XB])
                fw.dma("act", O["o_prglru"][cs, :], hh[:, TP - 1:TP], reads=[r_hh])
                fw.dma("act", O["o_sconv"][cs], XS[:, :, 4:7], reads=[r_XS])
                fw.op("dve", lambda: nc.vector.tensor_copy(out=hs[:], in_=hq[:, :, 3]), reads=[r_rs], writes=[r_hs])
                fw.dma("act", O["o_srglru"][cs, :], hs[:], reads=[r_hs])
        fw.barrier()
    k.phase_B = phase_B

    I["gqk"] = din("gqk", [128, 1280])
    I["cs_own"] = din("cs_own", [TOK, 16])
    I["cs_prev"] = din("cs_prev", [128, 16])
    I["masks"] = din("masks", [128, 3, 128])
    I["sinkB"] = din("sinkB", [128, 16])
    I["sinkT"] = din("sinkT", [64, 4, 16])
    I["cache_kT"] = din("cache_kT", [64, 16, 4, 128])
    I["cache_vT"] = din("cache_vT", [128, 16, 4, 64])
    I["cache_k_nat"] = din("cache_k_nat", [16, 128, 256])
    I["cache_v_nat"] = din("cache_v_nat", [16, 128, 256])
    I["smask_c"] = din("smask_c", [128, 16])
    I["smask_n"] = din("smask_n", [64, 64])
    S["oT"] = scr("oT", [1024, TOK], BF16)
    O["o_pk"] = dout("o_pk", [128, 256])
    O["o_pv"] = dout("o_pv", [128, 256])
    O["o_sk"] = dout("o_sk", [16, 124, 256])
    O["o_sv"] = dout("o_sv", [16, 124, 256])
    O["o_skn"] = dout("o_skn", [64, 256])
    O["o_svn"] = dout("o_svn", [64, 256])

    def phase_C():
        with ExitStack() as es:
            def T(name, shape, dt=F32):
                return es.enter_context(k.sb(name, shape, dt)), fw.res(name)
            gqk, r_gqk = T("gqk", [128, 1280])
            mkf, r_mkf = T("mkf", [128, 3, 128])
            mk, r_mk = T("mk", [128, 3, 128], BF16)
            esk, r_esk = T("esk", [128, 16])
            eskT, r_eskT = T("eskT", [64, 4, 16])
            smc_f, r_smcf = T("smc_f", [128, 16])
            smc, r_smc = T("smc", [128, 16], BF16)
            smn_f, r_smnf = T("smn_f", [64, 64])
            smn, r_smn = T("smn", [64, 64], BF16)
            ones, r_ones = T("ones", [128, 64], BF16)
            fw.dma("sp", gqk[:], I["gqk"], writes=[r_gqk])
            fw.dma("sp", mkf[:], I["masks"], writes=[r_mkf])
            fw.dma("sp", esk[:], I["sinkB"], writes=[r_esk])
            fw.dma("sp", eskT[:], I["sinkT"], writes=[r_eskT])
            fw.dma("sp", smc_f[:], I["smask_c"], writes=[r_smcf])
            fw.dma("sp", smn_f[:], I["smask_n"], writes=[r_smnf])
            fw.op("act", lambda: nc.scalar.copy(out=mk[:], in_=mkf[:]), reads=[r_mkf], writes=[r_mk])
            fw.op("act", lambda: nc.scalar.copy(out=smc[:], in_=smc_f[:]), reads=[r_smcf], writes=[r_smc])
            fw.op("act", lambda: nc.scalar.copy(out=smn[:], in_=smn_f[:]), reads=[r_smnf], writes=[r_smn])
            fw.op("act", lambda: nc.scalar.activation(out=esk[:], in_=esk[:], func=AF.Exp), reads=[r_esk], writes=[r_esk])
            fw.op("act", lambda: nc.scalar.activation(out=eskT[:], in_=eskT[:], func=AF.Exp), reads=[r_eskT], writes=[r_eskT])
            fw.op("dve", lambda: nc.vector.memset(ones[:], 1.0), writes=[r_ones])
            ckf, r_ckf = T("ckf", [64, 16, 4, 128])
            ck, r_ck = T("ck", [64, 16, 4, 128], BF16)
            cvf, r_cvf = T("cvf", [128, 16, 4, 64])
            cv, r_cv = T("cv", [128, 16, 4, 64], BF16)
            fw.dma("sp", ckf[:], I["cache_kT"], writes=[r_ckf])
            fw.dma("act", cvf[:], I["cache_vT"], writes=[r_cvf])
            fw.op("pool", lambda: nc.gpsimd.tensor_copy(out=ck[:], in_=ckf[:]), reads=[r_ckf], writes=[r_ck])
            fw.op("pool", lambda: nc.gpsimd.tensor_copy(out=cv[:], in_=cvf[:]), reads=[r_cvf], writes=[r_cv])
            r_cp = fw.res("cachecp")
            fw.dma("sp", O["o_sk"], I["cache_k_nat"][:, 4:128, :], writes=[r_cp])
            fw.dma("act", O["o_sv"], I["cache_v_nat"][:, 4:128, :], writes=[r_cp])

            qkvs = Rot(fw, k, es, "qkv", [128, 1536], F32, 2)
            css = Rot(fw, k, es, "cs", [128, 16], F32, 2)
            sq, r_sq = T("sq", [128, 1280])
            ssq, r_ssq = T("ssq", [128, 20])
            tmp, r_tmp = T("tmp", [128, 4, 20, 8])
            qkb, r_qkb = T("qkb", [128, 1280], BF16)
            QTs = Rot(fw, k, es, "QT", [64, 16, 128], BF16, 2)
            KTs = Rot(fw, k, es, "KT", [64, 4, 128], BF16, 3)
            VAs = Rot(fw, k, es, "VA", [128, 4, 65], BF16, 3)
            for (va, r_va) in VAs.slots:
                fw.op("dve", lambda: nc.vector.memset(va[:, :, 64:65], 1.0), writes=[r_va])
            PTs = Rot(fw, k, es, "PT", [128, 512], BF16, 4)
            ots = Rot(fw, k, es, "ot", [128, 16, 64], BF16, 2)
            oTts = Rot(fw, k, es, "oTt", [128, 8, 128], BF16, 2)
            den, r_den = T("den", [128, 4])
            psT = es.enter_context(k.pp("psT", [64, 24, 128], BF16))
            r_psT = fw.res("psT")
            psS = [es.enter_context(k.pp("psS%d" % i, [128, 512], F32)) for i in range(2)]
            r_psS = [fw.res("psS%d" % i) for i in range(2)]
            psO = [es.enter_context(k.pp("psO%d" % i, [128, 4, 128], F32)) for i in range(2)]
            r_psO = [fw.res("psO%d" % i) for i in range(2)]
            psOT = es.enter_context(k.pp("psOT", [128, 8, 128], BF16))
            r_psOT = fw.res("psOT")
            si = 0
            oi = 0

            def prep(qkv, r_qkv, cs, r_cs, nheads0):
                hs = slice(nheads0 * 64, 1280)
                nh = 20 - nheads0
                v3 = qkv[:, hs].rearrange("p (h d) -> p h d", d=64)
                fw.op("dve", lambda: nc.vector.tensor_tensor(out=sq[:, hs], in0=qkv[:, hs], in1=qkv[:, hs], op=ALU.mult), reads=[r_qkv], writes=[r_sq])
                fw.op("dve", lambda: nc.vector.tensor_reduce(out=ssq[:, nheads0:], in_=sq[:, hs].rearrange("p (h d) -> p h d", d=64), axis=AX.X, op=ALU.add),
                      reads=[r_sq], writes=[r_ssq])
                fw.op("act", lambda: nc.scalar.activation(out=ssq[:, nheads0:], in_=ssq[:, nheads0:], func=AF.Sqrt, scale=1.0 / 64, bias=EPS), reads=[r_ssq], writes=[r_ssq])
                fw.op("dve", lambda: nc.vector.reciprocal(out=ssq[:, nheads0:], in_=ssq[:, nheads0:]), reads=[r_ssq], writes=[r_ssq])
                fw.op("dve", lambda: nc.vector.tensor_tensor(out=v3, in0=v3, in1=ssq[:, nheads0:].unsqueeze(2).to_broadcast([128, nh, 64]), op=ALU.mult),
                      reads=[r_qkv, r_ssq], writes=[r_qkv])
                fw.op("dve", lambda: nc.vector.tensor_tensor(out=qkv[:, hs], in0=qkv[:, hs], in1=gqk[:, hs], op=ALU.mult), reads=[r_qkv, r_gqk], writes=[r_qkv])
                x1, x2 = v3[:, :, 0:8], v3[:, :, 8:16]
                cosb = cs[:, 0:8].unsqueeze(1).to_broadcast([128, nh, 8])
                sinb = cs[:, 8:16].unsqueeze(1).to_broadcast([128, nh, 8])
                tt = [tmp[:, j, nheads0:, :] for j in range(4)]
                for (o_, a_, b_) in ((tt[0], x1, cosb), (tt[1], x2, sinb), (tt[2], x2, cosb), (tt[3], x1, sinb)):
                    fw.op("dve", lambda: nc.vector.tensor_tensor(out=o_, in0=a_, in1=b_, op=ALU.mult), reads=[r_qkv, r_cs], writes=[r_tmp])
                fw.op("dve", lambda: nc.vector.tensor_tensor(out=x1, in0=tt[0], in1=tt[1], op=ALU.subtract), reads=[r_tmp], writes=[r_qkv])
                fw.op("dve", lambda: nc.vector.tensor_tensor(out=x2, in0=tt[2], in1=tt[3], op=ALU.add), reads=[r_tmp], writes=[r_qkv])
                fw.op("act", lambda: nc.scalar.copy(out=qkb[:, hs], in_=qkv[:, hs]), reads=[r_qkv], writes=[r_qkb])
                for h in range(nheads0, 20):
                    fw.op("pe", lambda: nc.tensor.transpose(out=psT[:, h, :], in_=qkb[:, h * 64:(h + 1) * 64], identity=idb[:]),
                          reads=[r_qkb, r_idb], writes=[r_psT])
                QT = r_QT = None
                if nheads0 == 0:
                    QT, r_QT = QTs.next()
                    evac(QT[:], psT[:, 0:16, :], [r_psT], [r_QT])
                KT, r_KT = KTs.next()
                evac(KT[:], psT[:, 16:20, :], [r_psT], [r_KT])
                VA, r_VA = VAs.next()
                fw.op("act", lambda: nc.scalar.copy(out=VA[:, :, 0:64], in_=qkv[:, 1280:1536].rearrange("p (g d) -> p g d", d=64)), reads=[r_qkv], writes=[r_VA])
                return QT, r_QT, KT, r_KT, VA, r_VA

            qkv, r_qkv = qkvs.next()
            cs, r_cs = css.next()
            fw.op("dve", lambda: nc.vector.memset(qkv[:, 0:1024], 0.0), writes=[r_qkv])
            fw.dma("sp", qkv[:, 1024:1536], S["zkvp"], writes=[r_qkv])
            fw.dma("act", cs[:], I["cs_prev"], writes=[r_cs])
            _, _, KTp, r_KTp, VAp, r_VAp = prep(qkv, r_qkv, cs, r_cs, 16)
            for t in range(NT):
                qkv, r_qkv = qkvs.next()
                cs, r_cs = css.next()
                fw.dma("sp", qkv[:], S["zqkv"][t * 128:(t + 1) * 128, :], writes=[r_qkv])
                fw.dma("act", cs[:], I["cs_own"][t * 128:(t + 1) * 128, :], writes=[r_cs])
                QT, r_QT, KT, r_KT, VA, r_VA = prep(qkv, r_qkv, cs, r_cs, 0)
                if t == NT - 2:
                    fw.dma("sp", O["o_pk"], qkv[:, 1024:1280], reads=[r_qkv])
                    fw.dma("sp", O["o_pv"], qkv[:, 1280:1536], reads=[r_qkv])
                if t == NT - 1:
                    fw.dma("sp", O["o_skn"], qkv[0:64, 1024:1280], reads=[r_qkv])
                    fw.dma("sp", O["o_svn"], qkv[0:64, 1280:1536], reads=[r_qkv])
                oTt, r_oTt = oTts.next()
                if t < NT - 1:
                    ot, r_ot = ots.next()
                    for g in range(4):
                        pts = []
                        for (KTx, r_KTx, mi) in ((KTp, r_KTp, 2 if t == 0 else 1), (KT, r_KT, 0)):
                            pS, r_pS = psS[si % 2], r_psS[si % 2]
                            si += 1
                            fw.op("pe", lambda: nc.tensor.matmul(pS[:], lhsT=KTx[:, g, :], rhs=QT[:, 4 * g:4 * g + 4, :], start=True, stop=True),
                                  reads=[r_KTx, r_QT], writes=[r_pS])
                            PT, r_PT = PTs.next()
                            fw.op("act", lambda: nc.scalar.activation(out=PT[:], in_=pS[:], func=AF.Exp, scale=0.125), reads=[r_pS], writes=[r_PT])
                            pv = PT[:].rearrange("p (h q) -> p h q", q=128)
                            fw.op("dve", lambda: nc.vector.tensor_tensor(out=pv, in0=pv, in1=mk[:, mi, :].unsqueeze(1).to_broadcast([128, 4, 128]), op=ALU.mult),
                                  reads=[r_PT, r_mk], writes=[r_PT])
                            pts.append((PT, r_PT))
                        pO, r_pO = psO[oi % 2], r_psO[oi % 2]
                        oi += 1
                        for h in range(4):
                            for j, (VAx, r_VAx) in enumerate(((VAp, r_VAp), (VA, r_VA))):
                                PT, r_PT = pts[j]
                                fw.op("pe", lambda: nc.tensor.matmul(pO[:, h, 0:65], lhsT=PT[:, h * 128:(h + 1) * 128], rhs=VAx[:, g, :], start=(j == 0), stop=(j == 1)),
                                      reads=[r_PT, r_VAx], writes=[r_pO])
                        fw.op("dve", lambda: nc.vector.tensor_tensor(out=den[:], in0=pO[:, :, 64], in1=esk[:, 4 * g:4 * g + 4], op=ALU.add), reads=[r_pO, r_esk], writes=[r_den])
                        fw.op("dve", lambda: nc.vector.reciprocal(out=den[:], in_=den[:]), reads=[r_den], writes=[r_den])
                        fw.op("dve", lambda: nc.vector.tensor_tensor(out=ot[:, 4 * g:4 * g + 4, :], in0=pO[:, :, 0:64], in1=den[:].unsqueeze(2).to_broadcast([128, 4, 64]), op=ALU.mult),
                              reads=[r_pO, r_den], writes=[r_ot])
                    otf = ot[:].rearrange("p h d -> p (h d)")
                    for kc in range(8):
                        fw.op("pe", lambda: nc.tensor.transpose(out=psOT[:, kc, :], in_=otf[:, kc * 128:(kc + 1) * 128], identity=idb[:]), reads=[r_ot, r_idb], writes=[r_psOT])
                    evac(oTt[:], psOT[:], [r_psOT], [r_oTt])
                    fw.dma("sp", S["oT"][:, t * 128:(t + 1) * 128].rearrange("(k p) t -> p k t", p=128), oTt[:], reads=[r_oTt])
                    KTp, r_KTp, VAp, r_VAp = KT, r_KT, VA, r_VA
                else:
                    osm, r_osm = T("osm", [64, 4, 4, 16, 4], BF16)
                    dn, r_dn = T("dn", [64, 16, 16])
                    PN, r_PN = T("PN", [64, 4, 64], BF16)
                    fw.op("dve", lambda: nc.vector.memset(oTt[:], 0.0), writes=[r_oTt])
                    fw.dma("sp", S["oT"][:, t * 128 + 64:(t + 1) * 128].rearrange("(k p) t -> p k t", p=128), oTt[:, :, 0:64], reads=[r_oTt])
                    for g in range(4):
                        pS, r_pS = psS[si % 2], r_psS[si % 2]
                        si += 1
                        fw.op("pe", lambda: nc.tensor.matmul(pS[0:64, 0:256], lhsT=KT[:, g, 0:64], rhs=QT[:, 4 * g:4 * g + 4, 0:64], start=True, stop=True),
                              reads=[r_KT, r_QT], writes=[r_pS])
                        fw.op("act", lambda: nc.scalar.activation(out=PN[:].rearrange("p h q -> p (h q)"), in_=pS[0:64, 0:256], func=AF.Exp, scale=0.125), reads=[r_pS], writes=[r_PN])
                        fw.op("dve", lambda: nc.vector.tensor_tensor(out=PN[:], in0=PN[:], in1=smn[:].unsqueeze(1).to_broadcast([64, 4, 64]), op=ALU.mult), reads=[r_PN, r_smn], writes=[r_PN])
                        pS, r_pS = psS[si % 2], r_psS[si % 2]
                        si += 1
                        for s_ in range(16):
                            fw.op("pe", lambda: nc.tensor.matmul(pS[:, s_ * 16:(s_ + 1) * 16], lhsT=ck[:, s_, g, :], rhs=QT[:, 4 * g:4 * g + 4, 4 * s_:4 * s_ + 4], start=True, stop=True),
                                  reads=[r_ck, r_QT], writes=[r_pS])
                        PT, r_PT = PTs.next()
                        fw.op("act", lambda: nc.scalar.activation(out=PT[:, 0:256], in_=pS[:, 0:256], func=AF.Exp, scale=0.125), reads=[r_pS], writes=[r_PT])
                        pv = PT[:, 0:256].rearrange("p (s q) -> p s q", q=16)
                        fw.op("dve", lambda: nc.vector.tensor_tensor(out=pv, in0=pv, in1=smc[:].unsqueeze(1).to_broadcast([128, 16, 16]), op=ALU.mult), reads=[r_PT, r_smc], writes=[r_PT])
                        pO, r_pO = psO[oi % 2], r_psO[oi % 2]
                        oi += 1
                        pOv = pO[0:64].rearrange("p a b -> p (a b)")
                        for s_ in range(16):
                            pn = PN[:, :, 4 * s_:4 * s_ + 4]
                            fw.op("pe", lambda: nc.tensor.matmul(pOv[:, s_ * 16:(s_ + 1) * 16], lhsT=cv[:, s_, g, :], rhs=PT[:, s_ * 16:(s_ + 1) * 16], start=True, stop=False),
                                  reads=[r_cv, r_PT], writes=[r_pO])
                            fw.op("pe", lambda: nc.tensor.matmul(pOv[:, s_ * 16:(s_ + 1) * 16], lhsT=VA[0:64, g, 0:64], rhs=pn, start=False, stop=True),
                                  reads=[r_VA, r_PN], writes=[r_pO])
                        for s_ in range(16):
                            pn = PN[:, :, 4 * s_:4 * s_ + 4]
                            fw.op("pe", lambda: nc.tensor.matmul(pOv[:, 256 + s_ * 16:256 + (s_ + 1) * 16], lhsT=ones[:, 0:64], rhs=PT[:, s_ * 16:(s_ + 1) * 16], start=True, stop=False),
                                  reads=[r_ones, r_PT], writes=[r_pO])
                            fw.op("pe", lambda: nc.tensor.matmul(pOv[:, 256 + s_ * 16:256 + (s_ + 1) * 16], lhsT=ones[0:64, 0:64], rhs=pn, start=False, stop=True),
                                  reads=[r_ones, r_PN], writes=[r_pO])
                        fw.op("dve", lambda: nc.vector.tensor_tensor(out=dn[:], in0=pOv[:, 256:512].rearrange("p (s q) -> p s q", q=16),
                                                                     in1=eskT[:, g, :].unsqueeze(1).to_broadcast([64, 16, 16]), op=ALU.add), reads=[r_pO, r_eskT], writes=[r_dn])
                        fw.op("dve", lambda: nc.vector.reciprocal(out=dn[:], in_=dn[:]), reads=[r_dn], writes=[r_dn])
                        fw.op("dve", lambda: nc.vector.tensor_tensor(out=osm[:, g].rearrange("p h s i -> p s h i"),
                                                                     in0=pOv[:, 0:256].rearrange("p (s h i) -> p s h i", h=4, i=4),
                                                                     in1=dn[:].rearrange("p s (h i) -> p s h i", i=4), op=ALU.mult), reads=[r_pO, r_dn], writes=[r_osm])
                    for g in range(4):
                        for h in range(4):
                            hd = 4 * g + h
                            fw.dma("sp" if h % 2 else "act", S["oT"][hd * 64:(hd + 1) * 64, t * 128:t * 128 + 64], osm[:, g, h].rearrange("p s i -> p (s i)"), reads=[r_osm], owner=r_osm)
        fw.barrier()
    k.phase_C = phase_C

    for nm_, shp in (("w_proj_rnn", [1024, D]), ("w_proj_attn", [1024, D]), ("w_out", [D, D]), ("w_peer_q", [D, D]), ("w_ple_gate", [D, D]), ("w_ple", [256, D])):
        I[nm_] = din(nm_, shp)
    I["g_ffn"] = din("g_ffn", [128, D])
    I["g_ple"] = din("g_ple", [128, D])
    I["skT"] = din("skT", [128, 16, 128])
    I["pleT"] = din("pleT", [256, TOK])
    I["peer_uT"] = din("peer_uT", [128, 128, 2048])
    I["peer_v"] = din("peer_v", [16384, D])
    S["mT"] = scr("mT", [D, TOK], BF16)
    S["x2"] = scr("x2", [TOK, D])
    S["x3"] = scr("x3", [TOK, D])
    S["xnT"] = scr("xnT", [D, TOK], BF16)
    S["s12"] = scr("s12", [TOK, 2048])
    S["ub"] = scr("ub", [128, 128, 2048], BF16)
    S["Hg"] = scr("Hg", [NT, 128, 128, 128], BF16)
    S["vb"] = scr("vb", [16384, D], BF16)
    O["y"] = dout("y", [TOK, D])

    def load_w(es, name, dram, K, N, stg, w=None):
        if w is None:
            w = es.enter_context(k.sb(name, [128, K, N], BF16))
        r_w = []
        for ci, c0 in enumerate(range(0, N, 256)):
            r_c = fw.res("%s_%d" % (name, ci))
            st, r_st = stg.next()
            fw.dma(dq(), st[:, 0:K, :], dram[:, c0:c0 + 256].rearrange("(k p) c -> p k c", p=128), writes=[r_st])
            k.evac_i += 1
            if k.evac_i % 2:
                fw.op("act", lambda: nc.scalar.copy(out=w[:, :, c0:c0 + 256], in_=st[:, 0:K, :]), reads=[r_st], writes=[r_c])
            else:
                fw.op("dve", lambda: nc.vector.tensor_copy(out=w[:, :, c0:c0 + 256], in_=st[:, 0:K, :]), reads=[r_st], writes=[r_c])
            r_w.append(r_c)
        return w, r_w

    def phase_D():
        with ExitStack() as es:
            stg = Rot(fw, k, es, "stg", [128, 16, 256], F32, 2)
            wpr, r_wpr = load_w(es, "wpr", I["w_proj_rnn"], 8, D, stg)
            wpa, r_wpa = load_w(es, "wpa", I["w_proj_attn"], 8, D, stg)
            hgs = Rot(fw, k, es, "hgs", [128, 8, 512], BF16, 2)
            obs = Rot(fw, k, es, "obs", [128, 8, 512], BF16, 2)
            gas = Rot(fw, k, es, "gas", [128, 512], F32, 2)
            gbs = Rot(fw, k, es, "gbs", [128, 512], F32, 2)
            mTs = Rot(fw, k, es, "mTs", [128, 16, 512], BF16, 2)
            t1s = Rot(fw, k, es, "t1s", [128, 512], F32, 2)
            ps = [es.enter_context(k.pp("psD%d" % i, [128, 512], F32)) for i in range(4)]
            r_ps = [fw.res("psD%d" % i) for i in range(4)]
            pi = 0
            for t0 in range(0, TOK, 512):
                n = min(512, TOK - t0)
                hg, r_hg = hgs.next()
                ob, r_ob = obs.next()
                fw.dma("sp", hg[:, :, 0:n], S["hgT"][:, t0:t0 + n].rearrange("(k p) t -> p k t", p=128), writes=[r_hg])
                fw.dma("act", ob[:, :, 0:n], S["oT"][:, t0:t0 + n].rearrange("(k p) t -> p k t", p=128), writes=[r_ob])
                mT, r_mT = mTs.next()
                for cc in range(16):
                    ga, r_ga = gas.next()
                    gb, r_gb = gbs.next()
                    fw.dma("sp", ga[:, 0:n], S["zT"][2048 + cc * 128:2048 + (cc + 1) * 128, t0:t0 + n], writes=[r_ga])
                    fw.dma("act", gb[:, 0:n], S["zT"][4096 + cc * 128:4096 + (cc + 1) * 128, t0:t0 + n], writes=[r_gb])
                    fw.op("act", lambda: nc.scalar.activation(out=ga[:, 0:n], in_=ga[:, 0:n], func=AF.Sigmoid), reads=[r_ga], writes=[r_ga])
                    fw.op("act", lambda: nc.scalar.activation(out=gb[:, 0:n], in_=gb[:, 0:n], func=AF.Sigmoid), reads=[r_gb], writes=[r_gb])
                    pa, r_pa = ps[pi % 4], r_ps[pi % 4]
                    pb, r_pb = ps[(pi + 1) % 4], r_ps[(pi + 1) % 4]
                    pi += 2
                    for kk in range(8):
                        fw.op("pe", lambda: nc.tensor.matmul(pa[:, 0:n], lhsT=wpr[:, kk, cc * 128:(cc + 1) * 128], rhs=hg[:, kk, 0:n], start=(kk == 0), stop=(kk == 7)),
                              reads=[r_wpr[cc // 2], r_hg], writes=[r_pa])
                    for kk in range(8):
                        fw.op("pe", lambda: nc.tensor.matmul(pb[:, 0:n], lhsT=wpa[:, kk, cc * 128:(cc + 1) * 128], rhs=ob[:, kk, 0:n], start=(kk == 0), stop=(kk == 7)),
                              reads=[r_wpa[cc // 2], r_ob], writes=[r_pb])
                    t1, r_t1 = t1s.next()
                    fw.op("dve", lambda: nc.vector.tensor_tensor(out=t1[:, 0:n], in0=pa[:, 0:n], in1=ga[:, 0:n], op=ALU.mult), reads=[r_pa, r_ga], writes=[r_t1])
                    fw.op("dve", lambda: nc.vector.tensor_tensor(out=gb[:, 0:n], in0=pb[:, 0:n], in1=gb[:, 0:n], op=ALU.mult), reads=[r_pb, r_gb], writes=[r_gb])
                    fw.op("dve", lambda: nc.vector.tensor_tensor(out=mT[:, cc, 0:n], in0=t1[:, 0:n], in1=gb[:, 0:n], op=ALU.add), reads=[r_t1, r_gb], writes=[r_mT])
                    k.bg()
                fw.dma("sp", S["mT"][:, t0:t0 + n].rearrange("(k p) t -> p k t", p=128), mT[:, :, 0:n], reads=[r_mT])
        fw.barrier()
        k.es_wq = ExitStack()
        k.wq_hold = k.es_wq.enter_context(k.sb("wq_pre", [128, 16, D], BF16))
        with ExitStack() as es:
            stg = Rot(fw, k, es, "stg", [128, 16, 256], F32, 2)
            wo, r_wo = load_w(es, "wo", I["w_out"], 16, D, stg)
            k.wq, k.r_wq = load_w(None, "wq", I["w_peer_q"], 16, D, stg, w=k.wq_hold)
            mts = Rot(fw, k, es, "mts", [128, 16, 128], BF16, 2)
            xs = Rot(fw, k, es, "xs", [128, D], F32, 2)
            x2s = Rot(fw, k, es, "x2s", [128, D], F32, 2)
            ps = [es.enter_context(k.pp("psE%d" % i, [128, 512], F32)) for i in range(4)]
            r_ps = [fw.res("psE%d" % i) for i in range(4)]
            pi = 0
            for t in range(NT):
                mt, r_mt = mts.next()
                xt, r_xt = xs.next()
                x2, r_x2 = x2s.next()
                fw.dma("sp", mt[:], S["mT"][:, t * 128:(t + 1) * 128].rearrange("(k p) t -> p k t", p=128), writes=[r_mt])
                fw.dma("act", xt[:], I["x_own"][t * 128:(t + 1) * 128, :], writes=[r_xt])
                for cg in range(4):
                    p_, r_p = ps[pi % 4], r_ps[pi % 4]
                    pi += 1
                    for kk in range(16):
                        fw.op("pe", lambda: nc.tensor.matmul(p_[:], lhsT=mt[:, kk, :], rhs=wo[:, kk, cg * 512:(cg + 1) * 512], start=(kk == 0), stop=(kk == 15)),
                              reads=[r_mt, r_wo[2 * cg], r_wo[2 * cg + 1]], writes=[r_p])
                    fw.op("dve", lambda: nc.vector.tensor_tensor(out=x2[:, cg * 512:(cg + 1) * 512], in0=p_[:], in1=xt[:, cg * 512:(cg + 1) * 512], op=ALU.add),
                          reads=[r_p, r_xt], writes=[r_x2])
                fw.dma("sp", S["x2"][t * 128:(t + 1) * 128, :], x2[:], reads=[r_x2])
        fw.barrier()
    k.phase_D = phase_D

    def phase_P_gen(es):
        sts = Rot(fw, k, es, "pst", [128, 2048], F32, 2)
        sbs = Rot(fw, k, es, "psb", [128, 2048], BF16, 2)
        return phase_P_run(sts, sbs)

    def phase_P_run(sts, sbs):
        ei = 0
        for c in range(128):
            for (src, dst) in ((I["peer_uT"][c], S["ub"][c]), (I["peer_v"][c * 128:(c + 1) * 128, :], S["vb"][c * 128:(c + 1) * 128, :])):
                st_, r_st = sts.next()
                sb_, r_sb = sbs.next()
                fw.dma("sp", st_[:], src, writes=[r_st])
                ei += 1
                if ei % 2:
                    fw.op("act", lambda: nc.scalar.copy(out=sb_[:], in_=st_[:]), reads=[r_st], writes=[r_sb])
                else:
                    fw.op("dve", lambda: nc.vector.tensor_copy(out=sb_[:], in_=st_[:]), reads=[r_st], writes=[r_sb])
                fw.dma("sp", dst, sb_[:], reads=[r_sb])
                yield
    k.bg_gen = None

    def bg():
        if k.bg_gen is not None:
            try:
                next(k.bg_gen)
            except StopIteration:
                k.bg_gen = None
    k.bg = bg

    def phase_P():
        pass
    k.phase_P = phase_P
    k.phase_P_gen = phase_P_gen

    def norm_tiles(es, pfx):
        gB = es.enter_context(k.sb(pfx + "gB", [128, D], F32))
        junk = es.enter_context(k.sb(pfx + "junk", [128, D], BF16))
        d_ = dict(gB=gB, r_gB=fw.res("gB"), junk=junk, r_junk=fw.res("junk"),
                  sss=Rot(fw, k, es, pfx + "ss", [128, 1], F32, 2), xns=Rot(fw, k, es, pfx + "xn", [128, D], BF16, 2),
                  pst=[es.enter_context(k.pp(pfx + "pst%d" % i, [128, 16, 128], BF16)) for i in range(1)],
                  r_pst=[fw.res("pst%d" % i) for i in range(1)])
        return d_

    def do_norm(es, nt_, xt, r_xt, dstT, r_dstT, i):
        ss, r_ss = nt_["sss"].next()
        xn, r_xn = nt_["xns"].next()
        rms_transpose(es, xt, r_xt, nt_["gB"], nt_["r_gB"], nt_["junk"], nt_["r_junk"], ss, r_ss, xn, r_xn, nt_["pst"][0], nt_["r_pst"][0], dstT, r_dstT)

    def phase_E0():
        with ExitStack() as es:
            wq, r_wq = k.wq, k.r_wq
            skf = es.enter_context(k.sb("skf", [128, 16, 128], F32))
            sk = es.enter_context(k.sb("sk", [128, 16, 128], BF16))
            r_skf, r_sk = fw.res("skf"), fw.res("sk")
            fw.dma("sp", skf[:], I["skT"], writes=[r_skf])
            fw.op("act", lambda: nc.scalar.copy(out=sk[:], in_=skf[:]), reads=[r_skf], writes=[r_sk])
            nt_ = norm_tiles(es, "e0")
            fw.dma("sp", nt_["gB"][:], I["g_ffn"], writes=[nt_["r_gB"]])
            xs = Rot(fw, k, es, "xs", [128, D], F32, 3)
            xnTs = Rot(fw, k, es, "xnTt", [128, 16, 128], BF16, 3)
            qTs = Rot(fw, k, es, "qT", [128, 16, 128], BF16, 2)
            s12s = Rot(fw, k, es, "s12", [128, 2048], F32, 2)
            psq = [es.enter_context(k.pp("psq%d" % i, [128, 4, 128], F32)) for i in range(2)]
            r_psq = [fw.res("psq%d" % i) for i in range(2)]
            pss = [es.enter_context(k.pp("pss%d" % i, [128, 512], F32)) for i in range(4)]
            r_pss = [fw.res("pss%d" % i) for i in range(4)]
            qi = 0

            def e0_prep(t):
                xt, r_xt = xs.next()
                fw.dma("sp", xt[:], S["x2"][t * 128:(t + 1) * 128, :], writes=[r_xt])
                xnT, r_xnT = xnTs.next()
                do_norm(es, nt_, xt, r_xt, xnT[:], r_xnT, t)
                fw.dma("act", S["xnT"][:, t * 128:(t + 1) * 128].rearrange("(k p) t -> p k t", p=128), xnT[:], reads=[r_xnT])
                return xnT, r_xnT

            xq = [e0_prep(0), e0_prep(1)]
            for t in range(NT):
                xnT, r_xnT = xq.pop(0)
                if t + 2 < NT:
                    xq.append(e0_prep(t + 2))
                qT, r_qT = qTs.next()
                for h4 in range(4):
                    pq, r_pq = psq[qi % 2], r_psq[qi % 2]
                    qi += 1
                    for j in range(4):
                        hp = h4 * 4 + j
                        for kk in range(16):
                            fw.op("pe", lambda: nc.tensor.matmul(pq[:, j, :], lhsT=wq[:, kk, hp * 128:(hp + 1) * 128], rhs=xnT[:, kk, :], start=(kk == 0), stop=(kk == 15)),
                                  reads=[r_wq[hp // 2], r_xnT], writes=[r_pq])
                    evac(qT[:, h4 * 4:(h4 + 1) * 4, :], pq[:], [r_pq], [r_qT])
                s12, r_s12 = s12s.next()
                for h4 in range(4):
                    p_, r_p = pss[h4], r_pss[h4]
                    for j in range(4):
                        hp = h4 * 4 + j
                        fw.op("pe", lambda: nc.tensor.matmul(p_[:, j * 128:(j + 1) * 128], lhsT=qT[:, hp, :], rhs=sk[:, hp, :], start=True, stop=True),
                              reads=[r_qT, r_sk], writes=[r_p])
                    evac(s12[:, h4 * 512:(h4 + 1) * 512], p_[:], [r_p], [r_s12])
                fw.dma("sp", S["s12"][t * 128:(t + 1) * 128, :], s12[:], reads=[r_s12])
        fw.barrier()
        k.es_wq.close()
    k.phase_E0 = phase_E0

    def phase_E1a():
        with ExitStack() as es:
            xa = es.enter_context(k.sb("xnTall", [128, 16, TOK], BF16))
            r_xa = fw.res("xnTall")
            for k0 in range(0, 16, 4):
                fw.dma("sp", xa[:, k0:k0 + 4, :], S["xnT"][k0 * 128:(k0 + 4) * 128, :].rearrange("(k p) t -> p k t", p=128), writes=[r_xa])
            ufs = Rot(fw, k, es, "uft", [128, 2048], F32, 2)
            ubs = Rot(fw, k, es, "ubt", [128, 16, 128], BF16, 3)
            vfs = Rot(fw, k, es, "vft", [128, 2048], F32, 2)
            vbs = Rot(fw, k, es, "vbt", [128, 2048], BF16, 2)
            gbs = Rot(fw, k, es, "gbt", [128, TOK], BF16, 3)
            ps = [es.enter_context(k.pp("psG%d" % i, [128, 512], F32)) for i in range(6)]
            r_ps = [fw.res("psG%d" % i) for i in range(6)]
            pi = 0
            def fetch_u(c):
                uf, r_uf = ufs.next()
                ub, r_ub = ubs.next()
                fw.dma("sp", uf[:], I["peer_uT"][c], writes=[r_uf])
                fw.op("act", lambda: nc.scalar.copy(out=ub[:].rearrange("p k e -> p (k e)"), in_=uf[:]), reads=[r_uf], writes=[r_ub])
                return ub, r_ub

            nxt_u = fetch_u(0)
            for c in range(128):
                ub, r_ub = nxt_u
                if c + 1 < 128:
                    nxt_u = fetch_u(c + 1)
                vf, r_vf = vfs.next()
                vb, r_vb = vbs.next()
                fw.dma("sp", vf[:], I["peer_v"][c * 128:(c + 1) * 128, :], writes=[r_vf])
                fw.op("dve", lambda: nc.vector.tensor_copy(out=vb[:], in_=vf[:]), reads=[r_vf], writes=[r_vb])
                fw.dma("sp", S["vb"][c * 128:(c + 1) * 128, :], vb[:], reads=[r_vb])
                gb, r_gb = gbs.next()
                for t0 in range(0, TOK, 512):
                    n = min(512, TOK - t0)
                    p_, r_p = ps[pi % 6], r_ps[pi % 6]
                    pi += 1
                    for kk in range(16):
                        fw.op("pe", lambda: nc.tensor.matmul(p_[:, 0:n], lhsT=ub[:, kk, :], rhs=xa[:, kk, t0:t0 + n], start=(kk == 0), stop=(kk == 15)),
                              reads=[r_ub, r_xa], writes=[r_p])
                    fw.op("act", lambda: nc.scalar.activation(out=gb[:, t0:t0 + n], in_=p_[:, 0:n], func=AF.Gelu_apprx_tanh), reads=[r_p], writes=[r_gb])
                fw.dma("sp", S["Hg"][:, :, c, :].rearrange("n e t -> e n t"), gb[:].rearrange("e (n t) -> e n t", t=128), reads=[r_gb])
        fw.barrier()
    k.phase_E1a = phase_E1a

    def phase_E1():
        NEG = -3.0e38
        DELTA = 4.0e-6
        with ExitStack() as es:
            def T(name, shape, dt=F32):
                return es.enter_context(k.sb(name, shape, dt)), fw.res(name)
            s12, r_s12 = T("s12m", [128, 8, 2, 128])
            Hgt = es.enter_context(k.sb("Hgt", [128, 128, 128], BF16))
            r_Hg = [fw.res("Hgt%d" % i) for i in range(8)]
            x2, r_x2 = T("x2m", [128, D])
            wrk, r_wrk = T("wrk", [128, 256])
            tops, r_tops = T("tops", [128, 8, 2, 16])
            cand, r_cand = T("cand", [128, 16, 16])
            c16, r_c16 = T("c16m", [128, 8, 16])
            e16, r_e16 = T("e16", [128, 8, 16])
            Z, r_Z = T("Z", [128, 8])
            tauD, r_tauD = T("tauD", [128, 8])
            thp, r_thp = T("thp", [128, 8, 16])
            e1n, r_e1n = T("e1n", [128, 8, 16])
            s2m, r_s2m = T("s2m", [128, 8, 128])
            E2b, r_E2b = T("E2b", [128, 8, 128], BF16)
            Rms = Rot(fw, k, es, "Rm", [128, 8, 128], BF16, 4)
            e1b, r_e1b = T("e1b", [128, 8, 16], BF16)
            As_ = Rot(fw, k, es, "As", [128, 8, 128], BF16, 4)
            AT, r_AT = T("ATf", [128, 128, 128], BF16)
            RT, r_RT = T("RT", [128, 128, 128], BF16)
            Wall = [T("Wall%d" % i, [128, 128, 128], BF16) for i in range(1)]
            vbs = Rot(fw, k, es, "vbt", [128, D], BF16, 8)
            acc = [es.enter_context(k.pp("acc%d" % i, [128, 512], F32)) for i in range(4)]
            r_acc = [fw.res("acc%d" % i) for i in range(4)]
            psX = [es.enter_context(k.pp("psX%d" % i, [128, 8, 128], BF16)) for i in range(2)]
            r_psX = [fw.res("psX%d" % i) for i in range(2)]
            psW = [es.enter_context(k.pp("psW%d" % i, [128, 4, 128], F32)) for i in range(2)]
            r_psW = [fw.res("psW%d" % i) for i in range(2)]
            st = {"x": 0, "w": 0}

            def prep(t):
                W, r_W = Wall[0]
                fw.dma("sp", s12[:].rearrange("p h q j -> p (h q j)"), S["s12"][t * 128:(t + 1) * 128, :], writes=[r_s12])
                for h in range(8):
                    for q in range(2):
                        fw.op("dve", lambda: nc.vector.max(out=tops[:, h, q, 0:8], in_=s12[:, h, q, :]), reads=[r_s12], writes=[r_tops])
                        fw.op("dve", lambda: nc.vector.match_replace(out=wrk[:, 0:128], in_to_replace=tops[:, h, q, 0:8], in_values=s12[:, h, q, :], imm_value=NEG),
                              reads=[r_s12, r_tops], writes=[r_wrk])
                        fw.op("dve", lambda: nc.vector.max(out=tops[:, h, q, 8:16], in_=wrk[:, 0:128]), reads=[r_wrk], writes=[r_tops])
                        yield 0.5
                for h in range(8):
                    fw.op("dve", lambda: nc.vector.tensor_tensor(out=cand[:], in0=tops[:, h, 0, :].unsqueeze(2).to_broadcast([128, 16, 16]),
                                                                 in1=tops[:, h, 1, :].unsqueeze(1).to_broadcast([128, 16, 16]), op=ALU.add), reads=[r_tops], writes=[r_cand])
                    cv_ = cand[:].rearrange("p a b -> p (a b)")
                    fw.op("dve", lambda: nc.vector.max(out=c16[:, h, 0:8], in_=cv_), reads=[r_cand], writes=[r_c16])
                    fw.op("dve", lambda: nc.vector.match_replace(out=wrk[:], in_to_replace=c16[:, h, 0:8], in_values=cv_, imm_value=NEG), reads=[r_cand, r_c16], writes=[r_wrk])
                    fw.op("dve", lambda: nc.vector.max(out=c16[:, h, 8:16], in_=wrk[:]), reads=[r_wrk], writes=[r_c16])
                    yield 1.0
                fw.op("dve", lambda: nc.vector.tensor_tensor(out=e16[:], in0=c16[:], in1=c16[:, :, 0:1].to_broadcast([128, 8, 16]), op=ALU.subtract), reads=[r_c16], writes=[r_e16])
                fw.op("act", lambda: nc.scalar.activation(out=e16[:], in_=e16[:], func=AF.Exp), reads=[r_e16], writes=[r_e16])
                fw.op("dve", lambda: nc.vector.tensor_reduce(out=Z[:], in_=e16[:], axis=AX.X, op=ALU.add), reads=[r_e16], writes=[r_Z])
                fw.op("dve", lambda: nc.vector.reciprocal(out=Z[:], in_=Z[:]), reads=[r_Z], writes=[r_Z])
                yield 1.0
                fw.op("dve", lambda: nc.vector.tensor_tensor(out=e1n[:], in0=tops[:, :, 0, :], in1=tops[:, :, 0, 0:1].to_broadcast([128, 8, 16]), op=ALU.subtract), reads=[r_tops], writes=[r_e1n])
                fw.op("act", lambda: nc.scalar.activation(out=e1n[:], in_=e1n[:], func=AF.Exp), reads=[r_e1n], writes=[r_e1n])
                fw.op("dve", lambda: nc.vector.tensor_tensor(out=e1b[:], in0=e1n[:], in1=Z[:].unsqueeze(2).to_broadcast([128, 8, 16]), op=ALU.mult), reads=[r_e1n, r_Z], writes=[r_e1b])
                fw.op("dve", lambda: nc.vector.tensor_scalar(out=tauD[:], in0=c16[:, :, 15], scalar1=-DELTA, scalar2=None, op0=ALU.add), reads=[r_c16], writes=[r_tauD])
                fw.op("dve", lambda: nc.vector.tensor_tensor(out=thp[:], in0=tauD[:].unsqueeze(2).to_broadcast([128, 8, 16]), in1=tops[:, :, 0, :], op=ALU.subtract), reads=[r_tauD, r_tops], writes=[r_thp])
                yield 1.0
                fw.op("dve", lambda: nc.vector.tensor_tensor(out=s2m[:], in0=s12[:, :, 1, :], in1=tops[:, :, 1, 0:1].to_broadcast([128, 8, 128]), op=ALU.subtract), reads=[r_s12, r_tops], writes=[r_s2m])
                fw.op("act", lambda: nc.scalar.activation(out=E2b[:], in_=s2m[:], func=AF.Exp), reads=[r_s2m], writes=[r_E2b])
                yield 1.0

                def r_front(j0):
                    Rm, r_Rm = Rms.next()
                    o_m = Rm[:].rearrange("p j (h k) -> p j h k", k=16)
                    s2b = s12[:, :, 1, j0:j0 + 8].rearrange("p h j -> p j h").unsqueeze(3).to_broadcast([128, 8, 8, 16])
                    e2b = E2b[:, :, j0:j0 + 8].rearrange("p h j -> p j h").unsqueeze(3).to_broadcast([128, 8, 8, 16])
                    fw.op("dve", lambda: nc.vector.tensor_tensor(out=o_m, in0=s2b, in1=thp[:].unsqueeze(1).to_broadcast([128, 8, 8, 16]), op=ALU.is_ge),
                          reads=[r_s12, r_thp], writes=[r_Rm])
                    fw.op("dve", lambda: nc.vector.tensor_tensor(out=o_m, in0=o_m, in1=e2b, op=ALU.mult), reads=[r_Rm, r_E2b], writes=[r_Rm])
                    return (j0, Rm, r_Rm)

                def r_back(a):
                    j0, Rm, r_Rm = a
                    pX, r_pX = psX[st["x"] % 2], r_psX[st["x"] % 2]
                    st["x"] += 1
                    for jj in range(8):
                        fw.op("pe", lambda: nc.tensor.transpose(out=pX[:, jj, :], in_=Rm[:, jj, :], identity=idb[:]), reads=[r_Rm, r_idb], writes=[r_pX])
                    fw.op("act", lambda: nc.scalar.copy(out=RT[:, :, j0:j0 + 8], in_=pX[:].rearrange("p j t -> p t j")), reads=[r_pX], writes=[r_RT])

                pend_r = []
                for j0 in range(0, 128, 8):
                    pend_r.append(r_front(j0))
                    if len(pend_r) > 2:
                        r_back(pend_r.pop(0))
                    yield 1.9
                while pend_r:
                    r_back(pend_r.pop(0))
                    yield 0.5

                def a_front(c1):
                    A_, r_A = As_.next()
                    o_a = A_[:].rearrange("p c (h k) -> p c h k", k=16)
                    fw.op("dve", lambda: nc.vector.tensor_tensor(out=o_a, in0=s12[:, :, 0, c1:c1 + 8].rearrange("p h c -> p c h").unsqueeze(3).to_broadcast([128, 8, 8, 16]),
                                                                 in1=tops[:, :, 0, :].unsqueeze(1).to_broadcast([128, 8, 8, 16]), op=ALU.is_equal),
                          reads=[r_s12, r_tops], writes=[r_A])
                    fw.op("dve", lambda: nc.vector.tensor_tensor(out=o_a, in0=o_a, in1=e1b[:].unsqueeze(1).to_broadcast([128, 8, 8, 16]), op=ALU.mult),
                          reads=[r_A, r_e1b], writes=[r_A])
                    return (c1, A_, r_A)

                def a_back(a):
                    c1, A_, r_A = a
                    pX, r_pX = psX[st["x"] % 2], r_psX[st["x"] % 2]
                    st["x"] += 1
                    for cc in range(8):
                        fw.op("pe", lambda: nc.tensor.transpose(out=pX[:, cc, :], in_=A_[:, cc, :], identity=idb[:]), reads=[r_A, r_idb], writes=[r_pX])
                    fw.op("act", lambda: nc.scalar.copy(out=AT[:, :, c1:c1 + 8], in_=pX[:].rearrange("p c t -> p t c")), reads=[r_pX], writes=[r_AT])

                pend_a = []
                for c1 in range(0, 128, 8):
                    pend_a.append(a_front(c1))
                    if len(pend_a) > 2:
                        a_back(pend_a.pop(0))
                    yield 1.9
                while pend_a:
                    a_back(pend_a.pop(0))
                    yield 0.5
                for t0 in range(0, 128, 4):
                    pW, r_pW = psW[st["w"] % 2], r_psW[st["w"] % 2]
                    st["w"] += 1
                    for tt in range(4):
                        tk = t0 + tt
                        fw.op("pe", lambda: nc.tensor.matmul(pW[:, tt, :], lhsT=RT[:, tk, :], rhs=AT[:, tk, :], start=True, stop=True),
                              reads=[r_RT, r_AT], writes=[r_pW])
                    fw.op("act", lambda: nc.scalar.copy(out=W[:, t0:t0 + 4, :], in_=pW[:]), reads=[r_pW], writes=[r_W])
                    yield 0.6

            def drain(g):
                for _ in g:
                    pass

            def load_hg(t, g):
                fw.dma("sp", Hgt[:, g * 16:(g + 1) * 16, :], S["Hg"][t, :, g * 16:(g + 1) * 16, :], writes=[r_Hg[g]])

            def gate(t, g):
                W, r_W = Wall[0]
                fw.op("dve", lambda: nc.vector.tensor_tensor(out=Hgt[:, g * 16:(g + 1) * 16, :], in0=Hgt[:, g * 16:(g + 1) * 16, :],
                                                             in1=W[:, :, g * 16:(g + 1) * 16].rearrange("j t c -> j c t"), op=ALU.mult),
                      reads=[r_Hg[g], r_W], writes=[r_Hg[g]])

            drain(prep(0))
            for g in range(8):
                load_hg(0, g)
            pend = list(range(8))
            for t in range(NT):
                for g in pend:
                    gate(t, g)
                pend = []
                nxt = prep(t + 1) if t + 1 < NT else None
                loaded = []
                budget = 0.0
                for c in range(128):
                    vb, r_vb = vbs.next()
                    fw.dma("sp", vb[:], S["vb"][c * 128:(c + 1) * 128, :], writes=[r_vb])
                    if c == 12 and t > 0:
                        load_hg(t, 7)
                    if c == 20 and t > 0:
                        gate(t, 7)
                    if c == 64:
                        fw.dma("sp", x2[:], S["x2"][t * 128:(t + 1) * 128, :], writes=[r_x2])
                    for cg in range(4):
                        fw.op("pe", lambda: nc.tensor.matmul(acc[cg][:], lhsT=Hgt[:, c, :], rhs=vb[:, cg * 512:(cg + 1) * 512], start=(c == 0), stop=(c == 127)),
                              reads=[r_Hg[c // 16], r_vb], writes=[r_acc[cg]])
                    budget += 1.0
                    while nxt is not None and budget > 0:
                        cost = next(nxt, None)
                        if cost is None:
                            nxt = None
                        else:
                            budget -= cost
                    if nxt is None and t + 1 < NT and loaded and budget > 0:
                        gate(t + 1, loaded.pop(0))
                        budget -= 3.0
                    if c % 16 == 9 and c >= 25 and t + 1 < NT:
                        load_hg(t + 1, (c - 25) // 16)
                        loaded.append((c - 25) // 16)
                if nxt is not None:
                    drain(nxt)
                pend = loaded
                for cg in range(4):
                    fw.op("dve", lambda: nc.vector.tensor_tensor(out=x2[:, cg * 512:(cg + 1) * 512], in0=acc[cg][:], in1=x2[:, cg * 512:(cg + 1) * 512], op=ALU.add),
                          reads=[r_acc[cg], r_x2], writes=[r_x2])
                fw.dma("sp", S["x3"][t * 128:(t + 1) * 128, :], x2[:], reads=[r_x2])
        fw.barrier()
    k.phase_E1 = phase_E1

    def phase_F():
        with ExitStack() as es:
            stg = Rot(fw, k, es, "stg", [128, 16, 256], F32, 2)
            wpg, r_wpg = load_w(es, "wpg", I["w_ple_gate"], 16, D, stg)
            wpl, r_wpl = load_w(es, "wpl", I["w_ple"], 2, D, stg)
            nt_ = norm_tiles(es, "f")
            fw.dma("sp", nt_["gB"][:], I["g_ple"], writes=[nt_["r_gB"]])
            xs = Rot(fw, k, es, "xs", [128, D], F32, 3)
            ys = Rot(fw, k, es, "ys", [128, D], F32, 2)
            xpTs = Rot(fw, k, es, "xpT", [128, 16, 128], BF16, 3)
            plf = Rot(fw, k, es, "plf", [128, 2, 128], F32, 3)
            plb = Rot(fw, k, es, "plb", [128, 2, 128], BF16, 3)
            sgs = Rot(fw, k, es, "sg", [128, 512], F32, 2)
            ps = [es.enter_context(k.pp("psF%d" % i, [128, 512], F32)) for i in range(4)]
            r_ps = [fw.res("psF%d" % i) for i in range(4)]
            pi = 0

            def f_prep(t):
                xt, r_xt = xs.next()
                fw.dma("sp", xt[:], S["x3"][t * 128:(t + 1) * 128, :], writes=[r_xt])
                pf, r_pf = plf.next()
                pb, r_pb = plb.next()
                fw.dma("act", pf[:], I["pleT"][:, t * 128:(t + 1) * 128].rearrange("(k p) t -> p k t", p=128), writes=[r_pf])
                fw.op("pool", lambda: nc.gpsimd.tensor_copy(out=pb[:], in_=pf[:]), reads=[r_pf], writes=[r_pb])
                xpT, r_xpT = xpTs.next()
                do_norm(es, nt_, xt, r_xt, xpT[:], r_xpT, t)
                return xt, r_xt, pb, r_pb, xpT, r_xpT

            fq = [f_prep(0), f_prep(1)]
            for t in range(NT):
                xt, r_xt, pb, r_pb, xpT, r_xpT = fq.pop(0)
                if t + 2 < NT:
                    fq.append(f_prep(t + 2))
                y, r_y = ys.next()
                for cg in range(4):
                    pg, r_pg = ps[pi % 4], r_ps[pi % 4]
                    pl, r_pl = ps[(pi + 1) % 4], r_ps[(pi + 1) % 4]
                    pi += 2
                    for kk in range(16):
                        fw.op("pe", lambda: nc.tensor.matmul(pg[:], lhsT=xpT[:, kk, :], rhs=wpg[:, kk, cg * 512:(cg + 1) * 512], start=(kk == 0), stop=(kk == 15)),
                              reads=[r_xpT, r_wpg[2 * cg], r_wpg[2 * cg + 1]], writes=[r_pg])
                    for kk in range(2):
                        fw.op("pe", lambda: nc.tensor.matmul(pl[:], lhsT=pb[:, kk, :], rhs=wpl[:, kk, cg * 512:(cg + 1) * 512], start=(kk == 0), stop=(kk == 1)),
                              reads=[r_pb, r_wpl[2 * cg], r_wpl[2 * cg + 1]], writes=[r_pl])
                    sg, r_sg = sgs.next()
                    fw.op("act", lambda: nc.scalar.activation(out=sg[:], in_=pg[:], func=AF.Sigmoid), reads=[r_pg], writes=[r_sg])
                    fw.op("dve", lambda: nc.vector.tensor_tensor(out=sg[:], in0=pl[:], in1=sg[:], op=ALU.mult), reads=[r_pl, r_sg], writes=[r_sg])
                    fw.op("dve", lambda: nc.vector.tensor_tensor(out=y[:, cg * 512:(cg + 1) * 512], in0=sg[:], in1=xt[:, cg * 512:(cg + 1) * 512], op=ALU.add),
                          reads=[r_sg, r_xt], writes=[r_y])
                fw.dma("sp", O["y"][t * 128:(t + 1) * 128, :], y[:], reads=[r_y])
        fw.barrier()
    k.phase_F = phase_F

    k.phase_A = phase_A
    k.I, k.S = I, S
    k.din, k.dout, k.scr = din, dout, scr
    return k


def emit_all(k, stop_after=None):
    I, S = k.I, k.S
    k.phase_A(I["x_prev"], NTP, [0, 1, 2, 3], S["zTp"], [12, 13], [NTP - 1], S["zkvp"], 3072)
    if stop_after == "A0":
        return
    fm = [g for g in range(30) if not (8 <= g <= 13)]
    k.phase_A(I["x_own"], NT, fm, S["zT"], list(range(8, 14)), list(range(NT)), S["zqkv"], 2048)
    if stop_after == "A":
        return
    k.phase_B()
    if stop_after == "B":
        return
    k.phase_C()
    if stop_after == "C":
        return
    k.phase_D()
    if stop_after == "D":
        return
    k.phase_E0()
    if stop_after == "E0":
        return
    k.phase_E1a()
    k.phase_E1()
    if stop_after == "E1":
        return
    k.phase_F()


def core_inputs(inp, c, shared):
    sq, half = c // 2, c % 2
    m = dict(shared)
    xo = np.zeros((TOK, D), np.float32)
    xo[0:2048] = inp["x_prompt"][sq, half * 2048:(half + 1) * 2048]
    xo[2048:2048 + 64] = inp["x_sample"][c * 16:(c + 1) * 16].reshape(64, D)
    m["x_own"] = xo
    if half == 1:
        m["x_prev"] = np.ascontiguousarray(inp["x_prompt"][sq, 0:2048])
    else:
        m["x_prev"] = np.zeros((2048, D), np.float32)
    m["state_convT"] = np.ascontiguousarray(inp["state_conv"][0, c * 16:(c + 1) * 16].reshape(16, 3, 8, 128).transpose(3, 2, 0, 1))
    m["state_hT"] = np.ascontiguousarray(inp["state_rglru"][0, c * 16:(c + 1) * 16].reshape(16, 8, 128).transpose(2, 1, 0))
    pos = np.zeros(TOK, np.float64)
    pos[0:2048] = half * 2048 + np.arange(2048)
    pos[2048:2048 + 64] = np.tile(16384 + np.arange(4), 16)
    m["cs_own"] = rope_table(pos)
    m["cs_prev"] = rope_table(np.arange(1920, 2048).astype(np.float64))
    tri = (np.arange(128)[:, None] <= np.arange(128)[None, :]).astype(np.float32)
    mk = np.zeros((128, 3, 128), np.float32)
    mk[:, 0] = tri
    mk[:, 1] = 1.0 - tri
    mk[:, 2] = (1.0 - tri) * float(half)
    m["masks"] = mk
    ck = inp["cache_k"][0, c * 16:(c + 1) * 16]
    cv = inp["cache_v"][0, c * 16:(c + 1) * 16]
    m["cache_kT"] = np.ascontiguousarray(ck.transpose(3, 0, 2, 1))
    m["cache_vT"] = np.ascontiguousarray(cv.transpose(1, 0, 2, 3))
    m["cache_k_nat"] = np.ascontiguousarray(ck.reshape(16, 128, 256))
    m["cache_v_nat"] = np.ascontiguousarray(cv.reshape(16, 128, 256))
    pt = np.zeros((TOK, 256), np.float32)
    pt[0:2048] = inp["p_prompt"][0, sq, half * 2048:(half + 1) * 2048]
    pt[2048:2048 + 64] = inp["p_sample"][0, c * 16:(c + 1) * 16].reshape(64, 256)
    m["pleT"] = np.ascontiguousarray(pt.T)
    fl = np.zeros((128, 2), np.float32)
    fl[:, 0] = float(half)
    fl[:, 1] = 1.0 - float(half)
    m["flags"] = fl
    return m


def rope_table(pos):
    inv = (np.float32(500000.0) ** (-np.arange(0, 16, 2, dtype=np.float32) / np.float32(16))).astype(np.float32)
    ang = pos.astype(np.float32)[:, None] * inv[None, :]
    return np.concatenate([np.cos(ang), np.sin(ang)], 1).astype(np.float32)


def shared_inputs(inp):
    sh = {}
    for nm_ in ("w_proj_rnn", "w_proj_attn", "w_out", "w_peer_q", "w_ple_gate", "w_ple", "peer_v"):
        sh[nm_] = np.ascontiguousarray(inp[nm_][0])
    sh["g_ffn"] = np.ascontiguousarray(np.broadcast_to(inp["norm_ffn"][0][None, :], (128, D)))
    sh["g_ple"] = np.ascontiguousarray(np.broadcast_to(inp["norm_ple"][0][None, :], (128, D)))
    sh["skT"] = np.ascontiguousarray(inp["peer_sub_keys"][0].reshape(16, 128, 128).transpose(2, 0, 1))
    sh["peer_uT"] = np.ascontiguousarray(inp["peer_u"][0].reshape(128, 128, 16, 128).transpose(0, 3, 2, 1).reshape(128, 128, 2048))
    sh["gqk"] = np.ascontiguousarray(np.broadcast_to(np.concatenate([np.tile(inp["q_norm"][0], 16), np.tile(inp["k_norm"][0], 4)])[None, :], (128, 1280)))
    sk = inp["attn_sinks"][0]
    sh["sinkB"] = np.ascontiguousarray(np.broadcast_to(sk[None, :], (128, 16)))
    sT = np.zeros((64, 4, 16), np.float32)
    for g in range(4):
        for h in range(4):
            sT[:, g, h * 4:(h + 1) * 4] = sk[4 * g + h]
    sh["sinkT"] = sT
    j = np.arange(128)[:, None]
    i = np.tile(np.arange(4), 4)[None, :]
    sh["smask_c"] = (j > i).astype(np.float32)
    kt = np.arange(64)
    sh["smask_n"] = ((kt[:, None] // 4 == kt[None, :] // 4) & (kt[:, None] % 4 <= kt[None, :] % 4)).astype(np.float32)
    sh["ident"] = np.eye(128, dtype=np.float32)
    sh["g_mix"] = np.ascontiguousarray(np.broadcast_to(inp["norm_mix"][0][None, :], (128, D)))
    sh["w_in"] = np.ascontiguousarray(inp["w_in"][0])
    v4 = np.stack([inp["conv_b"][0], inp["b_rgate"][0], inp["b_igate"][0], inp["lru_lambda"][0]], -1)
    sh["rnn_vec"] = np.ascontiguousarray(v4.reshape(8, 128, 4).transpose(1, 0, 2))
    sh["conv_wT"] = np.ascontiguousarray(inp["conv_w"][0].reshape(4, 8, 128).transpose(2, 1, 0))
    wg = np.stack([inp["w_rgate"][0], inp["w_igate"][0]], 0)
    sh["w_gates"] = np.ascontiguousarray(wg.transpose(2, 0, 1, 3))
    return sh


_CACHE = {}


def kernel(**inputs):
    inp = {k_: np.asarray(v) for k_, v in inputs.items()}
    if "k" not in _CACHE:
        k = build()
        emit_all(k)
        k.fw.finish()
        _CACHE["k"] = k
    k = _CACHE["k"]
    sh = shared_inputs(inp)
    in_maps = [core_inputs(inp, c, sh) for c in range(8)]
    res = run_bass_kernel_spmd(k.nc, in_maps, core_ids=list(range(8)))
    R = res.results
    f32 = np.float32
    y_prompt = np.zeros((4, 4096, D), f32)
    y_sample = np.zeros((128, 4, D), f32)
    prompt_conv = np.zeros((1, 4, 3, 1024), f32)
    prompt_rglru = np.zeros((1, 4, 1024), f32)
    prompt_k = np.zeros((1, 4, 128, 4, 64), f32)
    prompt_v = np.zeros((1, 4, 128, 4, 64), f32)
    sample_conv = np.zeros((1, 128, 3, 1024), f32)
    sample_rglru = np.zeros((1, 128, 1024), f32)
    sample_k = np.zeros((1, 128, 128, 4, 64), f32)
    sample_v = np.zeros((1, 128, 128, 4, 64), f32)
    for c in range(8):
        r = R[c]
        sq, half = c // 2, c % 2
        y = np.asarray(r["y"])
        y_prompt[sq, half * 2048:(half + 1) * 2048] = y[0:2048]
        y_sample[c * 16:(c + 1) * 16] = y[2048:2048 + 64].reshape(16, 4, D)
        if half == 1:
            prompt_conv[0, sq] = np.asarray(r["o_pconv"]).T
            prompt_rglru[0, sq] = np.asarray(r["o_prglru"])[:, 0]
            prompt_k[0, sq] = np.asarray(r["o_pk"]).reshape(128, 4, 64)
            prompt_v[0, sq] = np.asarray(r["o_pv"]).reshape(128, 4, 64)
        sl = slice(c * 16, (c + 1) * 16)
        sample_conv[0, sl] = np.asarray(r["o_sconv"]).transpose(1, 2, 0)
        sample_rglru[0, sl] = np.asarray(r["o_srglru"]).T
        sample_k[0, sl] = np.concatenate([np.asarray(r["o_sk"]), np.asarray(r["o_skn"]).reshape(16, 4, 256)], 1).reshape(16, 128, 4, 64)
        sample_v[0, sl] = np.concatenate([np.asarray(r["o_sv"]), np.asarray(r["o_svn"]).reshape(16, 4, 256)], 1).reshape(16, 128, 4, 64)
    return (y_prompt, y_sample, prompt_conv, prompt_rglru, prompt_k, prompt_v, sample_conv, sample_rglru, sample_k, sample_v)
```

```python
from contextlib import ExitStack
import numpy as np
import concourse.bass as bass
import concourse.mybir as mybir
from concourse.bass_utils import run_bass_kernel_spmd

F32 = mybir.dt.float32
BF16 = mybir.dt.bfloat16
AF = mybir.ActivationFunctionType
ALU = mybir.AluOpType
AX = mybir.AxisListType

D = 2048
NT = 17
NTP = 16
TOK = NT * 128
EPS = 1e-6


class DSem:
    __slots__ = ("sem", "tot", "gen")

    def __init__(self, sem):
        self.sem = sem
        self.tot = 0
        self.gen = -1


class Res:
    __slots__ = ("name", "lw", "rd", "ds", "lw_eng")

    def __init__(self, name):
        self.name = name
        self.lw = None
        self.rd = {}
        self.ds = {}
        self.lw_eng = None


class FW:
    def __init__(self, nc):
        self.nc = nc
        self.eng = {"pe": nc.tensor, "dve": nc.vector, "act": nc.scalar, "pool": nc.gpsimd, "sp": nc.sync}
        self.esem, self.cnt, self.seen, self._ctx = {}, {}, {}, []
        for k in self.eng:
            cm = nc.semaphore("s_" + k)
            self.esem[k] = cm.__enter__()
            self._ctx.append(cm)
            self.cnt[k] = 0
            self.seen[k] = {}
        self.ninst = 0
        self.free_ds, self.used_ds, self.all_ds = {"hw": [], "sw": []}, {"hw": [], "sw": []}, []
        self.gen = 0

    def res(self, name):
        return Res(name)

    def _wait(self, e, ev):
        if ev is None:
            return
        sem, val = ev
        key = id(sem)
        if self.seen[e].get(key, 0) >= val:
            return
        self.eng[e].wait_ge(sem, val)
        self.seen[e][key] = val

    def _deps(self, e, reads, writes):
        for r in reads:
            self._wait(e, r.lw)
        for w in writes:
            self._wait(e, w.lw)
            for ev in w.rd.values():
                self._wait(e, ev)

    def _commit(self, ev, reads, writes):
        k = id(ev[0])
        for r in reads:
            r.rd[k] = ev
        for w in writes:
            w.lw = ev
            w.rd = {}

    def op(self, e, inst_fn, reads=(), writes=()):
        self._deps(e, reads, writes)
        inst = inst_fn()
        self.cnt[e] += 1
        inst.then_inc(self.esem[e], 1)
        ev = (self.esem[e], self.cnt[e])
        if e == "pe":
            self.seen[e][id(self.esem[e])] = self.cnt[e]
        self._commit(ev, reads, writes)
        for w in writes:
            w.lw_eng = e
        self.ninst += 1
        return inst

    def _get_ds(self, owner, kind):
        ds = owner.ds.get(kind)
        if ds is None or ds.gen != self.gen:
            if self.free_ds[kind]:
                ds = self.free_ds[kind].pop()
            else:
                cm = self.nc.semaphore("d%s%d" % (kind, len(self.all_ds)))
                ds = DSem(cm.__enter__())
                self._ctx.append(cm)
                self.all_ds.append(ds)
            ds.gen = self.gen
            owner.ds[kind] = ds
            self.used_ds[kind].append(ds)
        return ds

    def dma(self, q, out, in_, reads=(), writes=(), owner=None, **kw):
        if writes:
            q = "sp"
        else:
            q = "act" if reads[0].lw_eng == "act" else "pool"
        self._deps(q, reads, writes)
        if owner is None:
            owner = writes[0] if writes else reads[0]
        ds = self._get_ds(owner, "sw" if q == "pool" else "hw")
        inst = self.eng[q].dma_start(out=out, in_=in_, **kw)
        ds.tot += 16
        inst.then_inc(ds.sem, 16)
        self._commit((ds.sem, ds.tot), reads, writes)
        for w in writes:
            w.lw_eng = None
        self.ninst += 1
        return inst

    def barrier(self):
        for e in self.eng:
            for k in self.eng:
                if k != e and self.cnt[k] > 0:
                    self._wait(e, (self.esem[k], self.cnt[k]))
            for ds in self.all_ds:
                if ds.tot > 0:
                    self._wait(e, (ds.sem, ds.tot))
        for kind in ("hw", "sw"):
            self.free_ds[kind].extend(self.used_ds[kind])
            self.used_ds[kind] = []
        self.gen += 1

    def finish(self):
        self.barrier()


class Rot:
    def __init__(self, fw, k, es, name, shape, dt, n):
        self.slots = []
        for i in range(n):
            t = es.enter_context(k.sb("%s%d" % (name, i), shape, dt))
            self.slots.append((t, fw.res("%s%d" % (name, i))))
        self.i = 0

    def next(self):
        s = self.slots[self.i % len(self.slots)]
        self.i += 1
        return s


class K:
    pass


def build(dbg=()):
    nc = bass.Bass("TRN2", target_bir_lowering=False)
    fw = FW(nc)
    k = K()
    k.nc, k.fw = nc, fw
    k.evac_i = 0
    k.dq_i = 0
    k.uid = 0

    def _sb(name, shape, dt):
        k.uid += 1
        return nc.sbuf_tensor("%s_%d" % (name, k.uid), list(shape), dt)

    def _pp(name, shape, dt):
        k.uid += 1
        return nc.psum_tensor("%s_%d" % (name, k.uid), list(shape), dt)
    k.sb, k.pp = _sb, _pp

    def din(name, shape, dt=F32):
        return nc.dram_tensor(name, list(shape), dt, kind="ExternalInput").ap()

    def dout(name, shape, dt=F32):
        return nc.dram_tensor(name, list(shape), dt, kind="ExternalOutput").ap()

    def scr(name, shape, dt=F32):
        kind = "ExternalOutput" if name in dbg else "Internal"
        return nc.dram_tensor(name, list(shape), dt, kind=kind).ap()

    I = {}
    I["x_own"] = din("x_own", [TOK, D])
    I["x_prev"] = din("x_prev", [NTP * 128, D])
    I["ident"] = din("ident", [128, 128])
    I["g_mix"] = din("g_mix", [128, D])
    I["w_in"] = din("w_in", [D, 7680])
    S = {}
    S["zT"] = scr("zT", [6144, TOK])
    S["zTp"] = scr("zTp", [1024, NTP * 128])
    S["zqkv"] = scr("zqkv", [TOK, 1536])
    S["zkvp"] = scr("zkvp", [128, 512])

    es0 = ExitStack()
    idf = es0.enter_context(k.sb("idf", [128, 128], F32))
    idb = es0.enter_context(k.sb("idb", [128, 128], BF16))
    r_idf, r_idb = fw.res("idf"), fw.res("idb")
    fw.dma("sp", idf[:], I["ident"], writes=[r_idf])
    fw.op("act", lambda: nc.scalar.copy(out=idb[:], in_=idf[:]), reads=[r_idf], writes=[r_idb])
    k.idb, k.r_idb, k.idf, k.r_idf = idb, r_idb, idf, r_idf

    def evac(out, in_, reads, writes):
        k.evac_i += 1
        if k.evac_i % 2:
            fw.op("dve", lambda: nc.vector.tensor_copy(out=out, in_=in_), reads=reads, writes=writes)
        else:
            fw.op("act", lambda: nc.scalar.copy(out=out, in_=in_), reads=reads, writes=writes)
    k.evac = evac

    def dq():
        k.dq_i += 1
        return "sp" if k.dq_i % 2 else "act"

    def rms_transpose(es, xt, r_xt, gB, r_gB, junk, r_junk, ss, r_ss, xn, r_xn, pst, r_pst, dstT, r_dstT, ncols=128, defer=False):
        fw.op("act", lambda: nc.scalar.activation(out=junk[:], in_=xt[:], func=AF.Square, accum_out=ss[:]),
              reads=[r_xt], writes=[r_junk, r_ss])
        fw.op("act", lambda: nc.scalar.activation(out=ss[:], in_=ss[:], func=AF.Sqrt, scale=1.0 / D, bias=EPS),
              reads=[r_ss], writes=[r_ss])
        fw.op("dve", lambda: nc.vector.reciprocal(out=ss[:], in_=ss[:]), reads=[r_ss], writes=[r_ss])
        fw.op("dve", lambda: nc.vector.scalar_tensor_tensor(out=xn[:], in0=xt[:], scalar=ss[:, 0:1], in1=gB[:],
                                                            op0=ALU.mult, op1=ALU.mult),
              reads=[r_xt, r_ss, r_gB], writes=[r_xn])
        for kk in range(16):
            fw.op("pe", lambda: nc.tensor.transpose(out=pst[:, kk, :], in_=xn[:, kk * 128:(kk + 1) * 128], identity=idb[:]),
                  reads=[r_xn, r_idb], writes=[r_pst])

        def _ev():
            evac(dstT, pst[:], [r_pst], [r_dstT])
        if defer:
            return _ev
        _ev()
    k.rms_transpose = rms_transpose

    def phase_A(x_dram, nt, fm_groups, fm_out, tm_groups, tm_tiles, tm_out, tm_col0):
        with ExitStack() as es:
            n1T = es.enter_context(k.sb("n1T", [128, 16, nt * 128], BF16))
            r_n1T = [fw.res("n1T%d" % i) for i in range(nt)]
            gB = es.enter_context(k.sb("gB", [128, D], F32))
            r_gB = fw.res("gB")
            fw.dma("sp", gB[:], I["g_mix"], writes=[r_gB])
            xs = Rot(fw, k, es, "xs", [128, D], F32, 2)
            junk = es.enter_context(k.sb("junk", [128, D], BF16))
            r_junk = fw.res("junk")
            sss = Rot(fw, k, es, "ss", [128, 1], F32, 2)
            xns = Rot(fw, k, es, "xn", [128, D], BF16, 2)
            pst = [es.enter_context(k.pp("pst%d" % i, [128, 16, 128], BF16)) for i in range(2)]
            r_pst = [fw.res("pst%d" % i) for i in range(2)]
            pend_ev = None
            for i in range(nt):
                xt, r_xt = xs.next()
                fw.dma(dq(), xt[:], x_dram[i * 128:(i + 1) * 128, :], writes=[r_xt])
                ss, r_ss = sss.next()
                xn, r_xn = xns.next()
                ev_ = rms_transpose(es, xt, r_xt, gB, r_gB, junk, r_junk, ss, r_ss, xn, r_xn, pst[i % 2], r_pst[i % 2],
                                    n1T[:, :, i * 128:(i + 1) * 128], r_n1T[i], defer=True)
                if pend_ev is not None:
                    pend_ev()
                pend_ev = ev_
            pend_ev()
            wst = Rot(fw, k, es, "wst", [128, 16, 256], F32, 2)
            wbs = Rot(fw, k, es, "wb", [128, 16, 256], BF16, 2)
            evs = Rot(fw, k, es, "ev", [128, 512], F32, 3)
            pso = [es.enter_context(k.pp("pso%d" % i, [128, 512], F32)) for i in range(4)]
            r_pso = [fw.res("pso%d" % i) for i in range(4)]
            pi = 0
            ntok = nt * 128
            glist = sorted(set(fm_groups) | set(tm_groups))

            def fetch_w(g):
                ws, r_ws = wst.next()
                fw.dma(dq(), ws[:], I["w_in"][:, g * 256:(g + 1) * 256].rearrange("(k p) c -> p k c", p=128), writes=[r_ws])
                wb, r_wb = wbs.next()
                fw.op("pool", lambda: nc.gpsimd.tensor_copy(out=wb[:], in_=ws[:]), reads=[r_ws], writes=[r_wb])
                return wb, r_wb

            nxt_w = fetch_w(glist[0])
            for gi_, g in enumerate(glist):
                wb, r_wb = nxt_w
                if gi_ + 1 < len(glist):
                    nxt_w = fetch_w(glist[gi_ + 1])
                if g in fm_groups:
                    for c2 in range(2):
                        cc = g * 2 + c2
                        row = (cc if cc < 16 else cc - 12) * 128
                        for t0 in range(0, ntok, 512):
                            n = min(512, ntok - t0)
                            ps, r_ps = pso[pi % 4], r_pso[pi % 4]
                            pi += 1
                            tiles = range(t0 // 128, (t0 + n) // 128)
                            for kk in range(16):
                                fw.op("pe", lambda: nc.tensor.matmul(ps[:, 0:n], lhsT=wb[:, kk, c2 * 128:(c2 + 1) * 128],
                                                                     rhs=n1T[:, kk, t0:t0 + n], start=(kk == 0), stop=(kk == 15)),
                                      reads=[r_wb] + [r_n1T[t] for t in tiles], writes=[r_ps])
                            ev, r_ev = evs.next()
                            evac(ev[:, 0:n], ps[:, 0:n], [r_ps], [r_ev])
                            fw.dma(dq(), fm_out[row:row + 128, t0:t0 + n], ev[:, 0:n], reads=[r_ev])
                            k.bg()
                if g in tm_groups:
                    for t in tm_tiles:
                        ps, r_ps = pso[pi % 4], r_pso[pi % 4]
                        pi += 1
                        for kk in range(16):
                            fw.op("pe", lambda: nc.tensor.matmul(ps[:, 0:256], lhsT=n1T[:, kk, t * 128:(t + 1) * 128],
                                                                 rhs=wb[:, kk, :], start=(kk == 0), stop=(kk == 15)),
                                  reads=[r_wb, r_n1T[t]], writes=[r_ps])
                        ev, r_ev = evs.next()
                        evac(ev[:, 0:256], ps[:, 0:256], [r_ps], [r_ev])
                        k.bg()
                        c0 = g * 256 - tm_col0
                        ti = tm_tiles.index(t) if tm_out is S["zkvp"] else t
                        fw.dma(dq(), tm_out[ti * 128:(ti + 1) * 128, c0:c0 + 256], ev[:, 0:256], reads=[r_ev])
        fw.barrier()


    I["rnn_vec"] = din("rnn_vec", [128, 8, 4])
    I["conv_wT"] = din("conv_wT", [128, 8, 4])
    I["w_gates"] = din("w_gates", [128, 2, 8, 128])
    I["state_convT"] = din("state_convT", [128, 8, 16, 3])
    I["state_hT"] = din("state_hT", [128, 8, 16])
    I["flags"] = din("flags", [128, 2])
    S["hgT"] = scr("hgT", [1024, TOK], BF16)
    O = {}
    O["o_pconv"] = dout("o_pconv", [1024, 3])
    O["o_prglru"] = dout("o_prglru", [1024, 1])
    O["o_sconv"] = dout("o_sconv", [1024, 16, 3])
    O["o_srglru"] = dout("o_srglru", [1024, 16])
    k.O = O

    def phase_B():
        TP = 4096
        with ExitStack() as es:
            def T(name, shape, dt=F32):
                return es.enter_context(k.sb(name, shape, dt)), fw.res(name)
            rv, r_rv = T("rv", [128, 8, 4])
            cw, r_cw = T("cw", [128, 8, 4])
            wgf, r_wgf = T("wgf", [128, 2, 8, 128])
            wg, r_wg = T("wg", [128, 2, 8, 128], BF16)
            flg, r_flg = T("flg", [128, 2])
            sp_, r_sp = T("sp_", [128, 8])
            c8, r_c8 = T("c8", [128, 8])
            c16, r_c16 = T("c16", [128, 8])
            fw.dma("sp", rv[:], I["rnn_vec"], writes=[r_rv])
            fw.dma("sp", cw[:], I["conv_wT"], writes=[r_cw])
            fw.dma("sp", wgf[:], I["w_gates"], writes=[r_wgf])
            fw.dma("sp", flg[:], I["flags"], writes=[r_flg])
            fw.op("act", lambda: nc.scalar.copy(out=wg[:], in_=wgf[:]), reads=[r_wgf], writes=[r_wg])
            fw.op("act", lambda: nc.scalar.activation(out=sp_[:], in_=rv[:, :, 3], func=AF.Exp, scale=-1.0), reads=[r_rv], writes=[r_sp])
            fw.op("act", lambda: nc.scalar.activation(out=sp_[:], in_=sp_[:], func=AF.Ln, bias=1.0), reads=[r_sp], writes=[r_sp])
            fw.op("dve", lambda: nc.vector.tensor_scalar(out=c8[:], in0=sp_[:], scalar1=-8.0, scalar2=None, op0=ALU.mult), reads=[r_sp], writes=[r_c8])
            fw.op("dve", lambda: nc.vector.tensor_scalar(out=c16[:], in0=sp_[:], scalar1=-16.0, scalar2=None, op0=ALU.mult), reads=[r_sp], writes=[r_c16])
            XB2 = [T("XB%d" % i, [128, TP + 3]) for i in range(2)]
            xc2 = [T("xc%d" % i, [128, TP]) for i in range(2)]
            xcb2 = [T("xcb%d" % i, [128, TP], BF16) for i in range(2)]
            rr, r_rr = T("rr", [128, TP])
            ii, r_ii = T("ii", [128, TP])
            aa, r_aa = T("aa", [128, TP])
            mt, r_mt = T("mt", [128, TP])
            hh, r_hh = T("hh", [128, TP])
            h0, r_h0 = T("h0", [128, 1])
            gr, r_gr = T("gr", [128, 2048 + 64])
            hgb, r_hgb = T("hgb", [128, 2048 + 128], BF16)
            XS2 = [T("XS%d" % i, [128, 16, 7]) for i in range(2)]
            xs2 = [T("xs_%d" % i, [128, 16, 4]) for i in range(2)]
            xsb2 = [T("xsb%d" % i, [128, 64], BF16) for i in range(2)]
            hs2 = [T("hs%d" % i, [128, 16]) for i in range(2)]
            rs, r_rs = T("rs", [128, 64])
            is_, r_is = T("is_", [128, 64])
            as_, r_as = T("as_", [128, 64])
            ms, r_ms = T("ms", [128, 64])
            ps = [es.enter_context(k.pp("psB%d" % i, [128, 512], F32)) for i in range(4)]
            r_ps = [fw.res("psB%d" % i) for i in range(4)]
            pi = 0
            for (XB_, r_XB_) in XB2:
                fw.op("dve", lambda: nc.vector.memset(XB_[:, 0:3], 0.0), writes=[r_XB_])
            fw.op("dve", lambda: nc.vector.memset(hgb[:, 2048 + 64:], 0.0), writes=[r_hgb])

            def stage1(b):
                cs = slice(b * 128, (b + 1) * 128)
                XB, r_XB = XB2[b % 2]
                xc, r_xc = xc2[b % 2]
                xcb, r_xcb = xcb2[b % 2]
                XS, r_XS = XS2[b % 2]
                xs_, r_xs = xs2[b % 2]
                xsb, r_xsb = xsb2[b % 2]
                hs, r_hs = hs2[b % 2]
                fw.dma("sp", XB[:, 3:3 + 2048], S["zTp"][cs, :], writes=[r_XB])
                fw.dma("act", XB[:, 3 + 2048:], S["zT"][cs, 0:2048], writes=[r_XB])
                fw.dma("act", XS[:, :, 0:3], I["state_convT"][:, b], writes=[r_XS])
                fw.dma("act", XS[:, :, 3:7], S["zT"][cs, 2048:2048 + 64].rearrange("p (s i) -> p s i", i=4), writes=[r_XS])
                fw.dma("sp", hs[:], I["state_hT"][:, b], writes=[r_hs])

                def conv(dst, r_dst, src_fn, r_src):
                    fw.op("dve", lambda: nc.vector.tensor_scalar(out=dst, in0=src_fn(3), scalar1=cw[:, b, 3:4], scalar2=rv[:, b, 0:1],
                                                                 op0=ALU.mult, op1=ALU.add), reads=[r_src, r_cw, r_rv], writes=[r_dst])
                    for kk in (2, 1, 0):
                        fw.op("dve", lambda: nc.vector.scalar_tensor_tensor(out=dst, in0=src_fn(kk), scalar=cw[:, b, kk:kk + 1], in1=dst,
                                                                            op0=ALU.mult, op1=ALU.add), reads=[r_src, r_cw, r_dst], writes=[r_dst])
                conv(xc[:], r_xc, lambda kk: XB[:, kk:kk + TP], r_XB)
                conv(xs_[:], r_xs, lambda kk: XS[:, :, kk:kk + 4], r_XS)

            def cast1(b):
                xc, r_xc = xc2[b % 2]
                xcb, r_xcb = xcb2[b % 2]
                xs_, r_xs = xs2[b % 2]
                xsb, r_xsb = xsb2[b % 2]
                fw.op("act", lambda: nc.scalar.copy(out=xcb[:], in_=xc[:]), reads=[r_xc], writes=[r_xcb])
                fw.op("act", lambda: nc.scalar.copy(out=xsb[:], in_=xs_[:].rearrange("p s i -> p (s i)")), reads=[r_xs], writes=[r_xsb])

            stage1(0)
            cast1(0)
            for b in range(8):
                cs = slice(b * 128, (b + 1) * 128)
                XB, r_XB = XB2[b % 2]
                xc, r_xc = xc2[b % 2]
                xcb, r_xcb = xcb2[b % 2]
                XS, r_XS = XS2[b % 2]
                xs_, r_xs = xs2[b % 2]
                xsb, r_xsb = xsb2[b % 2]
                hs, r_hs = hs2[b % 2]
                fw.dma("sp", gr[:], S["zT"][1024 + b * 128:1024 + (b + 1) * 128, 0:2048 + 64], writes=[r_gr])
                if b + 1 < 8:
                    stage1(b + 1)
                for gi, (dst, r_dst, dsts, r_dsts) in enumerate(((rr, r_rr, rs, r_rs), (ii, r_ii, is_, r_is))):
                    for c in range(TP // 512):
                        p_, r_p = ps[pi % 4], r_ps[pi % 4]
                        pi += 1
                        fw.op("pe", lambda: nc.tensor.matmul(p_[:], lhsT=wg[:, gi, b, :], rhs=xcb[:, c * 512:(c + 1) * 512], start=True, stop=True),
                              reads=[r_wg, r_xcb], writes=[r_p])
                        fw.op("act", lambda: nc.scalar.activation(out=dst[:, c * 512:(c + 1) * 512], in_=p_[:], func=AF.Sigmoid, bias=rv[:, b, 1 + gi:2 + gi]),
                              reads=[r_p, r_rv], writes=[r_dst])
                    p_, r_p = ps[pi % 4], r_ps[pi % 4]
                    pi += 1
                    fw.op("pe", lambda: nc.tensor.matmul(p_[:, 0:64], lhsT=wg[:, gi, b, :], rhs=xsb[:], start=True, stop=True),
                          reads=[r_wg, r_xsb], writes=[r_p])
                    fw.op("act", lambda: nc.scalar.activation(out=dsts[:], in_=p_[:, 0:64], func=AF.Sigmoid, bias=rv[:, b, 1 + gi:2 + gi]),
                          reads=[r_p, r_rv], writes=[r_dsts])
                for (a_, r_a, m_, r_m, r__, r_r) in ((aa, r_aa, mt, r_mt, rr, r_rr), (as_, r_as, ms, r_ms, rs, r_rs)):
                    fw.op("act", lambda: nc.scalar.activation(out=a_[:], in_=r__[:], func=AF.Exp, scale=c8[:, b:b + 1]), reads=[r_r, r_c8], writes=[r_a])
                    fw.op("act", lambda: nc.scalar.activation(out=m_[:], in_=r__[:], func=AF.Exp, scale=c16[:, b:b + 1]), reads=[r_r, r_c16], writes=[r_m])
                    fw.op("act", lambda: nc.scalar.activation(out=m_[:], in_=m_[:], func=AF.Sqrt, scale=-1.0, bias=1.0), reads=[r_m], writes=[r_m])
                fw.op("dve", lambda: nc.vector.memset(mt[:, 0:1], 1.0), writes=[r_mt])
                fw.op("dve", lambda: nc.vector.tensor_scalar(out=mt[:, 2048:2049], in0=mt[:, 2048:2049], scalar1=flg[:, 0:1], scalar2=flg[:, 1:2],
                                                             op0=ALU.mult, op1=ALU.add), reads=[r_mt, r_flg], writes=[r_mt])
                if b + 1 < 8:
                    cast1(b + 1)
                fw.op("dve", lambda: nc.vector.tensor_tensor(out=mt[:], in0=mt[:], in1=ii[:], op=ALU.mult), reads=[r_mt, r_ii], writes=[r_mt])
                fw.op("dve", lambda: nc.vector.tensor_tensor(out=mt[:], in0=mt[:], in1=xc[:], op=ALU.mult), reads=[r_mt, r_xc], writes=[r_mt])
                fw.op("dve", lambda: nc.vector.tensor_tensor(out=ms[:], in0=ms[:], in1=is_[:], op=ALU.mult), reads=[r_ms, r_is], writes=[r_ms])
                fw.op("dve", lambda: nc.vector.tensor_tensor(out=ms[:], in0=ms[:], in1=xs_[:].rearrange("p s i -> p (s i)"), op=ALU.mult), reads=[r_ms, r_xs], writes=[r_ms])
                fw.op("dve", lambda: nc.vector.tensor_tensor_scan(out=hh[:, 0:2048], data0=aa[:, 0:2048], data1=mt[:, 0:2048], initial=0.0,
                                                                  op0=ALU.mult, op1=ALU.add), reads=[r_aa, r_mt], writes=[r_hh])
                fw.op("dve", lambda: nc.vector.tensor_scalar(out=h0[:], in0=hh[:, 2047:2048], scalar1=flg[:, 0:1], scalar2=None, op0=ALU.mult),
                      reads=[r_hh, r_flg], writes=[r_h0])
                fw.op("dve", lambda: nc.vector.tensor_tensor_scan(out=hh[:, 2048:], data0=aa[:, 2048:], data1=mt[:, 2048:], initial=h0[:, 0:1],
                                                                  op0=ALU.mult, op1=ALU.add), reads=[r_aa, r_mt, r_h0], writes=[r_hh])
                a3 = as_[:].rearrange("p (s i) -> p s i", i=4)
                b3 = ms[:].rearrange("p (s i) -> p s i", i=4)
                hq = rs[:].rearrange("p (s i) -> p s i", i=4)
                for i4 in range(4):
                    prev = hs[:] if i4 == 0 else hq[:, :, i4 - 1]
                    fw.op("dve", lambda: nc.vector.tensor_tensor(out=hq[:, :, i4], in0=a3[:, :, i4], in1=prev, op=ALU.mult),
                          reads=[r_as, r_hs, r_rs], writes=[r_rs])
                    fw.op("dve", lambda: nc.vector.tensor_tensor(out=hq[:, :, i4], in0=hq[:, :, i4], in1=b3[:, :, i4], op=ALU.add),
                          reads=[r_ms, r_rs], writes=[r_rs])
                fw.op("act", lambda: nc.scalar.activation(out=gr[:], in_=gr[:], func=AF.Gelu_apprx_tanh), reads=[r_gr], writes=[r_gr])
                fw.op("dve", lambda: nc.vector.tensor_tensor(out=hgb[:, 0:2048], in0=hh[:, 2048:], in1=gr[:, 0:2048], op=ALU.mult),
                      reads=[r_hh, r_gr], writes=[r_hgb])
                fw.op("dve", lambda: nc.vector.tensor_tensor(out=hgb[:, 2048:2048 + 64], in0=rs[:], in1=gr[:, 2048:], op=ALU.mult),
                      reads=[r_rs, r_gr], writes=[r_hgb])
                fw.dma("sp", S["hgT"][cs, :], hgb[:], reads=[r_hgb])
                fw.dma("act", O["o_pconv"][cs, :], XB[:, TP:TP + 3], reads=[r_XB])
                fw.dma("act", O["o_prglru"][cs, :], hh[:, TP - 1:TP], reads=[r_hh])
                fw.dma("act", O["o_sconv"][cs], XS[:, :, 4:7], reads=[r_XS])
                fw.op("dve", lambda: nc.vector.tensor_copy(out=hs[:], in_=hq[:, :, 3]), reads=[r_rs], writes=[r_hs])
                fw.dma("act", O["o_srglru"][cs, :], hs[:], reads=[r_hs])
        fw.barrier()
    k.phase_B = phase_B

    I["gqk"] = din("gqk", [128, 1280])
    I["cs_own"] = din("cs_own", [TOK, 16])
    I["cs_prev"] = din("cs_prev", [128, 16])
    I["masks"] = din("masks", [128, 3, 128])
    I["sinkB"] = din("sinkB", [128, 16])
    I["sinkT"] = din("sinkT", [64, 4, 16])
    I["cache_kT"] = din("cache_kT", [64, 16, 4, 128])
    I["cache_vT"] = din("cache_vT", [128, 16, 4, 64])
    I["cache_k_nat"] = din("cache_k_nat", [16, 128, 256])
    I["cache_v_nat"] = din("cache_v_nat", [16, 128, 256])
    I["smask_c"] = din("smask_c", [128, 16])
    I["smask_n"] = din("smask_n", [64, 64])
    S["oT"] = scr("oT", [1024, TOK], BF16)
    O["o_pk"] = dout("o_pk", [128, 256])
    O["o_pv"] = dout("o_pv", [128, 256])
    O["o_sk"] = dout("o_sk", [16, 124, 256])
    O["o_sv"] = dout("o_sv", [16, 124, 256])
    O["o_skn"] = dout("o_skn", [64, 256])
    O["o_svn"] = dout("o_svn", [64, 256])

    def phase_C():
        with ExitStack() as es:
            def T(name, shape, dt=F32):
                return es.enter_context(k.sb(name, shape, dt)), fw.res(name)
            gqk, r_gqk = T("gqk", [128, 1280])
            mkf, r_mkf = T("mkf", [128, 3, 128])
            mk, r_mk = T("mk", [128, 3, 128], BF16)
            esk, r_esk = T("esk", [128, 16])
            eskT, r_eskT = T("eskT", [64, 4, 16])
            smc_f, r_smcf = T("smc_f", [128, 16])
            smc, r_smc = T("smc", [128, 16], BF16)
            smn_f, r_smnf = T("smn_f", [64, 64])
            smn, r_smn = T("smn", [64, 64], BF16)
            ones, r_ones = T("ones", [128, 64], BF16)
            fw.dma("sp", gqk[:], I["gqk"], writes=[r_gqk])
            fw.dma("sp", mkf[:], I["masks"], writes=[r_mkf])
            fw.dma("sp", esk[:], I["sinkB"], writes=[r_esk])
            fw.dma("sp", eskT[:], I["sinkT"], writes=[r_eskT])
            fw.dma("sp", smc_f[:], I["smask_c"], writes=[r_smcf])
            fw.dma("sp", smn_f[:], I["smask_n"], writes=[r_smnf])
            fw.op("act", lambda: nc.scalar.copy(out=mk[:], in_=mkf[:]), reads=[r_mkf], writes=[r_mk])
            fw.op("act", lambda: nc.scalar.copy(out=smc[:], in_=smc_f[:]), reads=[r_smcf], writes=[r_smc])
            fw.op("act", lambda: nc.scalar.copy(out=smn[:], in_=smn_f[:]), reads=[r_smnf], writes=[r_smn])
            fw.op("act", lambda: nc.scalar.activation(out=esk[:], in_=esk[:], func=AF.Exp), reads=[r_esk], writes=[r_esk])
            fw.op("act", lambda: nc.scalar.activation(out=eskT[:], in_=eskT[:], func=AF.Exp), reads=[r_eskT], writes=[r_eskT])
            fw.op("dve", lambda: nc.vector.memset(ones[:], 1.0), writes=[r_ones])
            ckf, r_ckf = T("ckf", [64, 16, 4, 128])
            ck, r_ck = T("ck", [64, 16, 4, 128], BF16)
            cvf, r_cvf = T("cvf", [128, 16, 4, 64])
            cv, r_cv = T("cv", [128, 16, 4, 64], BF16)
            fw.dma("sp", ckf[:], I["cache_kT"], writes=[r_ckf])
            fw.dma("act", cvf[:], I["cache_vT"], writes=[r_cvf])
            fw.op("pool", lambda: nc.gpsimd.tensor_copy(out=ck[:], in_=ckf[:]), reads=[r_ckf], writes=[r_ck])
            fw.op("pool", lambda: nc.gpsimd.tensor_copy(out=cv[:], in_=cvf[:]), reads=[r_cvf], writes=[r_cv])
            r_cp = fw.res("cachecp")
            fw.dma("sp", O["o_sk"], I["cache_k_nat"][:, 4:128, :], writes=[r_cp])
            fw.dma("act", O["o_sv"], I["cache_v_nat"][:, 4:128, :], writes=[r_cp])

            qkvs = Rot(fw, k, es, "qkv", [128, 1536], F32, 2)
            css = Rot(fw, k, es, "cs", [128, 16], F32, 2)
            sq, r_sq = T("sq", [128, 1280])
            ssq, r_ssq = T("ssq", [128, 20])
            tmp, r_tmp = T("tmp", [128, 4, 20, 8])
            qkb, r_qkb = T("qkb", [128, 1280], BF16)
            QTs = Rot(fw, k, es, "QT", [64, 16, 128], BF16, 2)
            KTs = Rot(fw, k, es, "KT", [64, 4, 128], BF16, 3)
            VAs = Rot(fw, k, es, "VA", [128, 4, 65], BF16, 3)
            for (va, r_va) in VAs.slots:
                fw.op("dve", lambda: nc.vector.memset(va[:, :, 64:65], 1.0), writes=[r_va])
            PTs = Rot(fw, k, es, "PT", [128, 512], BF16, 4)
            ots = Rot(fw, k, es, "ot", [128, 16, 64], BF16, 2)
            oTts = Rot(fw, k, es, "oTt", [128, 8, 128], BF16, 2)
            den, r_den = T("den", [128, 4])
            psT = es.enter_context(k.pp("psT", [64, 24, 128], BF16))
            r_psT = fw.res("psT")
            psS = [es.enter_context(k.pp("psS%d" % i, [128, 512], F32)) for i in range(2)]
            r_psS = [fw.res("psS%d" % i) for i in range(2)]
            psO = [es.enter_context(k.pp("psO%d" % i, [128, 4, 128], F32)) for i in range(2)]
            r_psO = [fw.res("psO%d" % i) for i in range(2)]
            psOT = es.enter_context(k.pp("psOT", [128, 8, 128], BF16))
            r_psOT = fw.res("psOT")
            si = 0
            oi = 0

            def prep(qkv, r_qkv, cs, r_cs, nheads0):
                hs = slice(nheads0 * 64, 1280)
                nh = 20 - nheads0
                v3 = qkv[:, hs].rearrange("p (h d) -> p h d", d=64)
                fw.op("dve", lambda: nc.vector.tensor_tensor(out=sq[:, hs], in0=qkv[:, hs], in1=qkv[:, hs], op=ALU.mult), reads=[r_qkv], writes=[r_sq])
                fw.op("dve", lambda: nc.vector.tensor_reduce(out=ssq[:, nheads0:], in_=sq[:, hs].rearrange("p (h d) -> p h d", d=64), axis=AX.X, op=ALU.add),
                      reads=[r_sq], writes=[r_ssq])
                fw.op("act", lambda: nc.scalar.activation(out=ssq[:, nheads0:], in_=ssq[:, nheads0:], func=AF.Sqrt, scale=1.0 / 64, bias=EPS), reads=[r_ssq], writes=[r_ssq])
                fw.op("dve", lambda: nc.vector.reciprocal(out=ssq[:, nheads0:], in_=ssq[:, nheads0:]), reads=[r_ssq], writes=[r_ssq])
                fw.op("dve", lambda: nc.vector.tensor_tensor(out=v3, in0=v3, in1=ssq[:, nheads0:].unsqueeze(2).to_broadcast([128, nh, 64]), op=ALU.mult),
                      reads=[r_qkv, r_ssq], writes=[r_qkv])
                fw.op("dve", lambda: nc.vector.tensor_tensor(out=qkv[:, hs], in0=qkv[:, hs], in1=gqk[:, hs], op=ALU.mult), reads=[r_qkv, r_gqk], writes=[r_qkv])
                x1, x2 = v3[:, :, 0:8], v3[:, :, 8:16]
                cosb = cs[:, 0:8].unsqueeze(1).to_broadcast([128, nh, 8])
                sinb = cs[:, 8:16].unsqueeze(1).to_broadcast([128, nh, 8])
                tt = [tmp[:, j, nheads0:, :] for j in range(4)]
                for (o_, a_, b_) in ((tt[0], x1, cosb), (tt[1], x2, sinb), (tt[2], x2, cosb), (tt[3], x1, sinb)):
                    fw.op("dve", lambda: nc.vector.tensor_tensor(out=o_, in0=a_, in1=b_, op=ALU.mult), reads=[r_qkv, r_cs], writes=[r_tmp])
                fw.op("dve", lambda: nc.vector.tensor_tensor(out=x1, in0=tt[0], in1=tt[1], op=ALU.subtract), reads=[r_tmp], writes=[r_qkv])
                fw.op("dve", lambda: nc.vector.tensor_tensor(out=x2, in0=tt[2], in1=tt[3], op=ALU.add), reads=[r_tmp], writes=[r_qkv])
                fw.op("act", lambda: nc.scalar.copy(out=qkb[:, hs], in_=qkv[:, hs]), reads=[r_qkv], writes=[r_qkb])
                for h in range(nheads0, 20):
                    fw.op("pe", lambda: nc.tensor.transpose(out=psT[:, h, :], in_=qkb[:, h * 64:(h + 1) * 64], identity=idb[:]),
                          reads=[r_qkb, r_idb], writes=[r_psT])
                QT = r_QT = None
                if nheads0 == 0:
                    QT, r_QT = QTs.next()
                    evac(QT[:], psT[:, 0:16, :], [r_psT], [r_QT])
                KT, r_KT = KTs.next()
                evac(KT[:], psT[:, 16:20, :], [r_psT], [r_KT])
                VA, r_VA = VAs.next()
                fw.op("act", lambda: nc.scalar.copy(out=VA[:, :, 0:64], in_=qkv[:, 1280:1536].rearrange("p (g d) -> p g d", d=64)), reads=[r_qkv], writes=[r_VA])
                return QT, r_QT, KT, r_KT, VA, r_VA

            qkv, r_qkv = qkvs.next()
            cs, r_cs = css.next()
            fw.op("dve", lambda: nc.vector.memset(qkv[:, 0:1024], 0.0), writes=[r_qkv])
            fw.dma("sp", qkv[:, 1024:1536], S["zkvp"], writes=[r_qkv])
            fw.dma("act", cs[:], I["cs_prev"], writes=[r_cs])
            _, _, KTp, r_KTp, VAp, r_VAp = prep(qkv, r_qkv, cs, r_cs, 16)
            for t in range(NT):
                qkv, r_qkv = qkvs.next()
                cs, r_cs = css.next()
                fw.dma("sp", qkv[:], S["zqkv"][t * 128:(t + 1) * 128, :], writes=[r_qkv])
                fw.dma("act", cs[:], I["cs_own"][t * 128:(t + 1) * 128, :], writes=[r_cs])
                QT, r_QT, KT, r_KT, VA, r_VA = prep(qkv, r_qkv, cs, r_cs, 0)
                if t == NT - 2:
                    fw.dma("sp", O["o_pk"], qkv[:, 1024:1280], reads=[r_qkv])
                    fw.dma("sp", O["o_pv"], qkv[:, 1280:1536], reads=[r_qkv])
                if t == NT - 1:
                    fw.dma("sp", O["o_skn"], qkv[0:64, 1024:1280], reads=[r_qkv])
                    fw.dma("sp", O["o_svn"], qkv[0:64, 1280:1536], reads=[r_qkv])
                oTt, r_oTt = oTts.next()
                if t < NT - 1:
                    ot, r_ot = ots.next()
                    for g in range(4):
                        pts = []
                        for (KTx, r_KTx, mi) in ((KTp, r_KTp, 2 if t == 0 else 1), (KT, r_KT, 0)):
                            pS, r_pS = psS[si % 2], r_psS[si % 2]
                            si += 1
                            fw.op("pe", lambda: nc.tensor.matmul(pS[:], lhsT=KTx[:, g, :], rhs=QT[:, 4 * g:4 * g + 4, :], start=True, stop=True),
                                  reads=[r_KTx, r_QT], writes=[r_pS])
                            PT, r_PT = PTs.next()
                            fw.op("act", lambda: nc.scalar.activation(out=PT[:], in_=pS[:], func=AF.Exp, scale=0.125), reads=[r_pS], writes=[r_PT])
                            pv = PT[:].rearrange("p (h q) -> p h q", q=128)
                            fw.op("dve", lambda: nc.vector.tensor_tensor(out=pv, in0=pv, in1=mk[:, mi, :].unsqueeze(1).to_broadcast([128, 4, 128]), op=ALU.mult),
                                  reads=[r_PT, r_mk], writes=[r_PT])
                            pts.append((PT, r_PT))
                        pO, r_pO = psO[oi % 2], r_psO[oi % 2]
                        oi += 1
                        for h in range(4):
                            for j, (VAx, r_VAx) in enumerate(((VAp, r_VAp), (VA, r_VA))):
                                PT, r_PT = pts[j]
                                fw.op("pe", lambda: nc.tensor.matmul(pO[:, h, 0:65], lhsT=PT[:, h * 128:(h + 1) * 128], rhs=VAx[:, g, :], start=(j == 0), stop=(j == 1)),
                                      reads=[r_PT, r_VAx], writes=[r_pO])
                        fw.op("dve", lambda: nc.vector.tensor_tensor(out=den[:], in0=pO[:, :, 64], in1=esk[:, 4 * g:4 * g + 4], op=ALU.add), reads=[r_pO, r_esk], writes=[r_den])
                        fw.op("dve", lambda: nc.vector.reciprocal(out=den[:], in_=den[:]), reads=[r_den], writes=[r_den])
                        fw.op("dve", lambda: nc.vector.tensor_tensor(out=ot[:, 4 * g:4 * g + 4, :], in0=pO[:, :, 0:64], in1=den[:].unsqueeze(2).to_broadcast([128, 4, 64]), op=ALU.mult),
                              reads=[r_pO, r_den], writes=[r_ot])
                    otf = ot[:].rearrange("p h d -> p (h d)")
                    for kc in range(8):
                        fw.op("pe", lambda: nc.tensor.transpose(out=psOT[:, kc, :], in_=otf[:, kc * 128:(kc + 1) * 128], identity=idb[:]), reads=[r_ot, r_idb], writes=[r_psOT])
                    evac(oTt[:], psOT[:], [r_psOT], [r_oTt])
                    fw.dma("sp", S["oT"][:, t * 128:(t + 1) * 128].rearrange("(k p) t -> p k t", p=128), oTt[:], reads=[r_oTt])
                    KTp, r_KTp, VAp, r_VAp = KT, r_KT, VA, r_VA
                else:
                    osm, r_osm = T("osm", [64, 4, 4, 16, 4], BF16)
                    dn, r_dn = T("dn", [64, 16, 16])
                    PN, r_PN = T("PN", [64, 4, 64], BF16)
                    fw.op("dve", lambda: nc.vector.memset(oTt[:], 0.0), writes=[r_oTt])
                    fw.dma("sp", S["oT"][:, t * 128 + 64:(t + 1) * 128].rearrange("(k p) t -> p k t", p=128), oTt[:, :, 0:64], reads=[r_oTt])
                    for g in range(4):
                        pS, r_pS = psS[si % 2], r_psS[si % 2]
                        si += 1
                        fw.op("pe", lambda: nc.tensor.matmul(pS[0:64, 0:256], lhsT=KT[:, g, 0:64], rhs=QT[:, 4 * g:4 * g + 4, 0:64], start=True, stop=True),
                              reads=[r_KT, r_QT], writes=[r_pS])
                        fw.op("act", lambda: nc.scalar.activation(out=PN[:].rearrange("p h q -> p (h q)"), in_=pS[0:64, 0:256], func=AF.Exp, scale=0.125), reads=[r_pS], writes=[r_PN])
                        fw.op("dve", lambda: nc.vector.tensor_tensor(out=PN[:], in0=PN[:], in1=smn[:].unsqueeze(1).to_broadcast([64, 4, 64]), op=ALU.mult), reads=[r_PN, r_smn], writes=[r_PN])
                        pS, r_pS = psS[si % 2], r_psS[si % 2]
                        si += 1
                        for s_ in range(16):
                            fw.op("pe", lambda: nc.tensor.matmul(pS[:, s_ * 16:(s_ + 1) * 16], lhsT=ck[:, s_, g, :], rhs=QT[:, 4 * g:4 * g + 4, 4 * s_:4 * s_ + 4], start=True, stop=True),
                                  reads=[r_ck, r_QT], writes=[r_pS])
                        PT, r_PT = PTs.next()
                        fw.op("act", lambda: nc.scalar.activation(out=PT[:, 0:256], in_=pS[:, 0:256], func=AF.Exp, scale=0.125), reads=[r_pS], writes=[r_PT])
                        pv = PT[:, 0:256].rearrange("p (s q) -> p s q", q=16)
                        fw.op("dve", lambda: nc.vector.tensor_tensor(out=pv, in0=pv, in1=smc[:].unsqueeze(1).to_broadcast([128, 16, 16]), op=ALU.mult), reads=[r_PT, r_smc], writes=[r_PT])
                        pO, r_pO = psO[oi % 2], r_psO[oi % 2]
                        oi += 1
                        pOv = pO[0:64].rearrange("p a b -> p (a b)")
                        for s_ in range(16):
                            pn = PN[:, :, 4 * s_:4 * s_ + 4]
                            fw.op("pe", lambda: nc.tensor.matmul(pOv[:, s_ * 16:(s_ + 1) * 16], lhsT=cv[:, s_, g, :], rhs=PT[:, s_ * 16:(s_ + 1) * 16], start=True, stop=False),
                                  reads=[r_cv, r_PT], writes=[r_pO])
                            fw.op("pe", lambda: nc.tensor.matmul(pOv[:, s_ * 16:(s_ + 1) * 16], lhsT=VA[0:64, g, 0:64], rhs=pn, start=False, stop=True),
                                  reads=[r_VA, r_PN], writes=[r_pO])
                        for s_ in range(16):
                            pn = PN[:, :, 4 * s_:4 * s_ + 4]
                            fw.op("pe", lambda: nc.tensor.matmul(pOv[:, 256 + s_ * 16:256 + (s_ + 1) * 16], lhsT=ones[:, 0:64], rhs=PT[:, s_ * 16:(s_ + 1) * 16], start=True, stop=False),
                                  reads=[r_ones, r_PT], writes=[r_pO])
                            fw.op("pe", lambda: nc.tensor.matmul(pOv[:, 256 + s_ * 16:256 + (s_ + 1) * 16], lhsT=ones[0:64, 0:64], rhs=pn, start=False, stop=True),
                                  reads=[r_ones, r_PN], writes=[r_pO])
                        fw.op("dve", lambda: nc.vector.tensor_tensor(out=dn[:], in0=pOv[:, 256:512].rearrange("p (s q) -> p s q", q=16),
                                                                     in1=eskT[:, g, :].unsqueeze(1).to_broadcast([64, 16, 16]), op=ALU.add), reads=[r_pO, r_eskT], writes=[r_dn])
                        fw.op("dve", lambda: nc.vector.reciprocal(out=dn[:], in_=dn[:]), reads=[r_dn], writes=[r_dn])
                        fw.op("dve", lambda: nc.vector.tensor_tensor(out=osm[:, g].rearrange("p h s i -> p s h i"),
                                                                     in0=pOv[:, 0:256].rearrange("p (s h i) -> p s h i", h=4, i=4),
                                                                     in1=dn[:].rearrange("p s (h i) -> p s h i", i=4), op=ALU.mult), reads=[r_pO, r_dn], writes=[r_osm])
                    for g in range(4):
                        for h in range(4):
                            hd = 4 * g + h
                            fw.dma("sp" if h % 2 else "act", S["oT"][hd * 64:(hd + 1) * 64, t * 128:t * 128 + 64], osm[:, g, h].rearrange("p s i -> p (s i)"), reads=[r_osm], owner=r_osm)
        fw.barrier()
    k.phase_C = phase_C

    for nm_, shp in (("w_proj_rnn", [1024, D]), ("w_proj_attn", [1024, D]), ("w_out", [D, D]), ("w_peer_q", [D, D]), ("w_ple_gate", [D, D]), ("w_ple", [256, D])):
        I[nm_] = din(nm_, shp)
    I["g_ffn"] = din("g_ffn", [128, D])
    I["g_ple"] = din("g_ple", [128, D])
    I["skT"] = din("skT", [128, 16, 128])
    I["pleT"] = din("pleT", [256, TOK])
    I["peer_uT"] = din("peer_uT", [128, 128, 2048])
    I["peer_v"] = din("peer_v", [16384, D])
    S["mT"] = scr("mT", [D, TOK], BF16)
    S["x2"] = scr("x2", [TOK, D])
    S["x3"] = scr("x3", [TOK, D])
    S["xnT"] = scr("xnT", [D, TOK], BF16)
    S["s12"] = scr("s12", [TOK, 2048])
    S["ub"] = scr("ub", [128, 128, 2048], BF16)
    S["Hg"] = scr("Hg", [NT, 128, 128, 128], BF16)
    S["vb"] = scr("vb", [16384, D], BF16)
    O["y"] = dout("y", [TOK, D])

    def load_w(es, name, dram, K, N, stg, w=None):
        if w is None:
            w = es.enter_context(k.sb(name, [128, K, N], BF16))
        r_w = []
        for ci, c0 in enumerate(range(0, N, 256)):
            r_c = fw.res("%s_%d" % (name, ci))
            st, r_st = stg.next()
            fw.dma(dq(), st[:, 0:K, :], dram[:, c0:c0 + 256].rearrange("(k p) c -> p k c", p=128), writes=[r_st])
            k.evac_i += 1
            if k.evac_i % 2:
                fw.op("act", lambda: nc.scalar.copy(out=w[:, :, c0:c0 + 256], in_=st[:, 0:K, :]), reads=[r_st], writes=[r_c])
            else:
                fw.op("dve", lambda: nc.vector.tensor_copy(out=w[:, :, c0:c0 + 256], in_=st[:, 0:K, :]), reads=[r_st], writes=[r_c])
            r_w.append(r_c)
        return w, r_w

    def load_w_multi(es, specs, N, stg):
        ws = [es.enter_context(k.sb(name, [128, K, N], BF16)) for (name, dram, K) in specs]
        rs = [[] for _ in specs]
        for ci, c0 in enumerate(range(0, N, 256)):
            for si, (name, dram, K) in enumerate(specs):
                w = ws[si]
                r_c = fw.res("%s_%d" % (name, ci))
                st, r_st = stg.next()
                fw.dma(dq(), st[:, 0:K, :], dram[:, c0:c0 + 256].rearrange("(k p) c -> p k c", p=128), writes=[r_st])
                k.evac_i += 1
                if k.evac_i % 2:
                    fw.op("act", lambda: nc.scalar.copy(out=w[:, :, c0:c0 + 256], in_=st[:, 0:K, :]), reads=[r_st], writes=[r_c])
                else:
                    fw.op("dve", lambda: nc.vector.tensor_copy(out=w[:, :, c0:c0 + 256], in_=st[:, 0:K, :]), reads=[r_st], writes=[r_c])
                rs[si].append(r_c)
        return list(zip(ws, rs))

    def phase_D():
        with ExitStack() as es:
            stg = Rot(fw, k, es, "stg", [128, 16, 256], F32, 2)
            (wpr, r_wpr), (wpa, r_wpa) = load_w_multi(es, [("wpr", I["w_proj_rnn"], 8), ("wpa", I["w_proj_attn"], 8)], D, stg)
            hgs = Rot(fw, k, es, "hgs", [128, 8, 512], BF16, 2)
            obs = Rot(fw, k, es, "obs", [128, 8, 512], BF16, 2)
            gas = Rot(fw, k, es, "gas", [128, 512], F32, 2)
            gbs = Rot(fw, k, es, "gbs", [128, 512], F32, 2)
            mTs = Rot(fw, k, es, "mTs", [128, 16, 512], BF16, 2)
            t1s = Rot(fw, k, es, "t1s", [128, 512], F32, 2)
            ps = [es.enter_context(k.pp("psD%d" % i, [128, 512], F32)) for i in range(4)]
            r_ps = [fw.res("psD%d" % i) for i in range(4)]
            pi = 0
            for t0 in range(0, TOK, 512):
                n = min(512, TOK - t0)
                hg, r_hg = hgs.next()
                ob, r_ob = obs.next()
                fw.dma("sp", hg[:, :, 0:n], S["hgT"][:, t0:t0 + n].rearrange("(k p) t -> p k t", p=128), writes=[r_hg])
                fw.dma("act", ob[:, :, 0:n], S["oT"][:, t0:t0 + n].rearrange("(k p) t -> p k t", p=128), writes=[r_ob])
                mT, r_mT = mTs.next()
                for cc in range(16):
                    ga, r_ga = gas.next()
                    gb, r_gb = gbs.next()
                    fw.dma("sp", ga[:, 0:n], S["zT"][2048 + cc * 128:2048 + (cc + 1) * 128, t0:t0 + n], writes=[r_ga])
                    fw.dma("act", gb[:, 0:n], S["zT"][4096 + cc * 128:4096 + (cc + 1) * 128, t0:t0 + n], writes=[r_gb])
                    fw.op("act", lambda: nc.scalar.activation(out=ga[:, 0:n], in_=ga[:, 0:n], func=AF.Sigmoid), reads=[r_ga], writes=[r_ga])
                    fw.op("act", lambda: nc.scalar.activation(out=gb[:, 0:n], in_=gb[:, 0:n], func=AF.Sigmoid), reads=[r_gb], writes=[r_gb])
                    pa, r_pa = ps[pi % 4], r_ps[pi % 4]
                    pb, r_pb = ps[(pi + 1) % 4], r_ps[(pi + 1) % 4]
                    pi += 2
                    for kk in range(8):
                        fw.op("pe", lambda: nc.tensor.matmul(pa[:, 0:n], lhsT=wpr[:, kk, cc * 128:(cc + 1) * 128], rhs=hg[:, kk, 0:n], start=(kk == 0), stop=(kk == 7)),
                              reads=[r_wpr[cc // 2], r_hg], writes=[r_pa])
                    for kk in range(8):
                        fw.op("pe", lambda: nc.tensor.matmul(pb[:, 0:n], lhsT=wpa[:, kk, cc * 128:(cc + 1) * 128], rhs=ob[:, kk, 0:n], start=(kk == 0), stop=(kk == 7)),
                              reads=[r_wpa[cc // 2], r_ob], writes=[r_pb])
                    t1, r_t1 = t1s.next()
                    fw.op("dve", lambda: nc.vector.tensor_tensor(out=t1[:, 0:n], in0=pa[:, 0:n], in1=ga[:, 0:n], op=ALU.mult), reads=[r_pa, r_ga], writes=[r_t1])
                    fw.op("dve", lambda: nc.vector.tensor_tensor(out=gb[:, 0:n], in0=pb[:, 0:n], in1=gb[:, 0:n], op=ALU.mult), reads=[r_pb, r_gb], writes=[r_gb])
                    fw.op("dve", lambda: nc.vector.tensor_tensor(out=mT[:, cc, 0:n], in0=t1[:, 0:n], in1=gb[:, 0:n], op=ALU.add), reads=[r_t1, r_gb], writes=[r_mT])
                    k.bg()
                fw.dma("sp", S["mT"][:, t0:t0 + n].rearrange("(k p) t -> p k t", p=128), mT[:, :, 0:n], reads=[r_mT])
        fw.barrier()
        k.es_wq = ExitStack()
        k.wq_hold = k.es_wq.enter_context(k.sb("wq_pre", [128, 16, D], BF16))
        with ExitStack() as es:
            stg = Rot(fw, k, es, "stg", [128, 16, 256], F32, 2)
            wo, r_wo = load_w(es, "wo", I["w_out"], 16, D, stg)
            k.wq, k.r_wq = load_w(None, "wq", I["w_peer_q"], 16, D, stg, w=k.wq_hold)
            mts = Rot(fw, k, es, "mts", [128, 16, 128], BF16, 2)
            xs = Rot(fw, k, es, "xs", [128, D], F32, 2)
            x2s = Rot(fw, k, es, "x2s", [128, D], F32, 2)
            ps = [es.enter_context(k.pp("psE%d" % i, [128, 512], F32)) for i in range(4)]
            r_ps = [fw.res("psE%d" % i) for i in range(4)]
            pi = 0
            for t in range(NT):
                mt, r_mt = mts.next()
                xt, r_xt = xs.next()
                x2, r_x2 = x2s.next()
                fw.dma("sp", mt[:], S["mT"][:, t * 128:(t + 1) * 128].rearrange("(k p) t -> p k t", p=128), writes=[r_mt])
                fw.dma("act", xt[:], I["x_own"][t * 128:(t + 1) * 128, :], writes=[r_xt])
                for cg in range(4):
                    p_, r_p = ps[pi % 4], r_ps[pi % 4]
                    pi += 1
                    for kk in range(16):
                        fw.op("pe", lambda: nc.tensor.matmul(p_[:], lhsT=mt[:, kk, :], rhs=wo[:, kk, cg * 512:(cg + 1) * 512], start=(kk == 0), stop=(kk == 15)),
                              reads=[r_mt, r_wo[2 * cg], r_wo[2 * cg + 1]], writes=[r_p])
                    fw.op("dve", lambda: nc.vector.tensor_tensor(out=x2[:, cg * 512:(cg + 1) * 512], in0=p_[:], in1=xt[:, cg * 512:(cg + 1) * 512], op=ALU.add),
                          reads=[r_p, r_xt], writes=[r_x2])
                fw.dma("sp", S["x2"][t * 128:(t + 1) * 128, :], x2[:], reads=[r_x2])
        fw.barrier()
    k.phase_D = phase_D

    def phase_P_gen(es):
        sts = Rot(fw, k, es, "pst", [128, 2048], F32, 2)
        sbs = Rot(fw, k, es, "psb", [128, 2048], BF16, 2)
        return phase_P_run(sts, sbs)

    def phase_P_run(sts, sbs):
        ei = 0
        for c in range(128):
            for (src, dst) in ((I["peer_uT"][c], S["ub"][c]), (I["peer_v"][c * 128:(c + 1) * 128, :], S["vb"][c * 128:(c + 1) * 128, :])):
                st_, r_st = sts.next()
                sb_, r_sb = sbs.next()
                fw.dma("sp", st_[:], src, writes=[r_st])
                ei += 1
                if ei % 2:
                    fw.op("act", lambda: nc.scalar.copy(out=sb_[:], in_=st_[:]), reads=[r_st], writes=[r_sb])
                else:
                    fw.op("dve", lambda: nc.vector.tensor_copy(out=sb_[:], in_=st_[:]), reads=[r_st], writes=[r_sb])
                fw.dma("sp", dst, sb_[:], reads=[r_sb])
                yield
    k.bg_gen = None

    def bg():
        if k.bg_gen is not None:
            try:
                next(k.bg_gen)
            except StopIteration:
                k.bg_gen = None
    k.bg = bg

    def phase_P():
        pass
    k.phase_P = phase_P
    k.phase_P_gen = phase_P_gen

    def norm_tiles(es, pfx):
        gB = es.enter_context(k.sb(pfx + "gB", [128, D], F32))
        junk = es.enter_context(k.sb(pfx + "junk", [128, D], BF16))
        d_ = dict(gB=gB, r_gB=fw.res("gB"), junk=junk, r_junk=fw.res("junk"),
                  sss=Rot(fw, k, es, pfx + "ss", [128, 1], F32, 2), xns=Rot(fw, k, es, pfx + "xn", [128, D], BF16, 2),
                  pst=[es.enter_context(k.pp(pfx + "pst%d" % i, [128, 16, 128], BF16)) for i in range(1)],
                  r_pst=[fw.res("pst%d" % i) for i in range(1)])
        return d_

    def do_norm(es, nt_, xt, r_xt, dstT, r_dstT, i):
        ss, r_ss = nt_["sss"].next()
        xn, r_xn = nt_["xns"].next()
        rms_transpose(es, xt, r_xt, nt_["gB"], nt_["r_gB"], nt_["junk"], nt_["r_junk"], ss, r_ss, xn, r_xn, nt_["pst"][0], nt_["r_pst"][0], dstT, r_dstT)

    def phase_E0():
        with ExitStack() as es:
            wq, r_wq = k.wq, k.r_wq
            skf = es.enter_context(k.sb("skf", [128, 16, 128], F32))
            sk = es.enter_context(k.sb("sk", [128, 16, 128], BF16))
            r_skf, r_sk = fw.res("skf"), fw.res("sk")
            fw.dma("sp", skf[:], I["skT"], writes=[r_skf])
            fw.op("act", lambda: nc.scalar.copy(out=sk[:], in_=skf[:]), reads=[r_skf], writes=[r_sk])
            nt_ = norm_tiles(es, "e0")
            fw.dma("sp", nt_["gB"][:], I["g_ffn"], writes=[nt_["r_gB"]])
            xs = Rot(fw, k, es, "xs", [128, D], F32, 2)
            xnTs = Rot(fw, k, es, "xnTt", [128, 16, 128], BF16, 3)
            qTs = Rot(fw, k, es, "qT", [128, 16, 128], BF16, 2)
            s12s = Rot(fw, k, es, "s12", [128, 2048], F32, 2)
            psq = [es.enter_context(k.pp("psq%d" % i, [128, 4, 128], F32)) for i in range(2)]
            r_psq = [fw.res("psq%d" % i) for i in range(2)]
            pss = [es.enter_context(k.pp("pss%d" % i, [128, 512], F32)) for i in range(4)]
            r_pss = [fw.res("pss%d" % i) for i in range(4)]
            qi = 0

            def e0_prep(t):
                xt, r_xt = xs.next()
                fw.dma("sp", xt[:], S["x2"][t * 128:(t + 1) * 128, :], writes=[r_xt])
                xnT, r_xnT = xnTs.next()
                do_norm(es, nt_, xt, r_xt, xnT[:], r_xnT, t)
                fw.dma("act", S["xnT"][:, t * 128:(t + 1) * 128].rearrange("(k p) t -> p k t", p=128), xnT[:], reads=[r_xnT])
                return xnT, r_xnT

            nxt_x = e0_prep(0)
            for t in range(NT):
                xnT, r_xnT = nxt_x
                if t + 1 < NT:
                    nxt_x = e0_prep(t + 1)
                qT, r_qT = qTs.next()
                for h4 in range(4):
                    pq, r_pq = psq[qi % 2], r_psq[qi % 2]
                    qi += 1
                    for j in range(4):
                        hp = h4 * 4 + j
                        for kk in range(16):
                            fw.op("pe", lambda: nc.tensor.matmul(pq[:, j, :], lhsT=wq[:, kk, hp * 128:(hp + 1) * 128], rhs=xnT[:, kk, :], start=(kk == 0), stop=(kk == 15)),
                                  reads=[r_wq[hp // 2], r_xnT], writes=[r_pq])
                    evac(qT[:, h4 * 4:(h4 + 1) * 4, :], pq[:], [r_pq], [r_qT])
                s12, r_s12 = s12s.next()
                for h4 in range(4):
                    p_, r_p = pss[h4], r_pss[h4]
                    for j in range(4):
                        hp = h4 * 4 + j
                        fw.op("pe", lambda: nc.tensor.matmul(p_[:, j * 128:(j + 1) * 128], lhsT=qT[:, hp, :], rhs=sk[:, hp, :], start=True, stop=True),
                              reads=[r_qT, r_sk], writes=[r_p])
                    evac(s12[:, h4 * 512:(h4 + 1) * 512], p_[:], [r_p], [r_s12])
                fw.dma("sp", S["s12"][t * 128:(t + 1) * 128, :], s12[:], reads=[r_s12])
        fw.barrier()
        k.es_wq.close()
    k.phase_E0 = phase_E0

    def phase_E1a():
        with ExitStack() as es:
            xa = es.enter_context(k.sb("xnTall", [128, 16, TOK], BF16))
            r_xa = fw.res("xnTall")
            for k0 in range(0, 16, 4):
                fw.dma("sp", xa[:, k0:k0 + 4, :], S["xnT"][k0 * 128:(k0 + 4) * 128, :].rearrange("(k p) t -> p k t", p=128), writes=[r_xa])
            ufs = Rot(fw, k, es, "uft", [128, 2048], F32, 2)
            ubs = Rot(fw, k, es, "ubt", [128, 16, 128], BF16, 3)
            vfs = Rot(fw, k, es, "vft", [128, 2048], F32, 2)
            vbs = Rot(fw, k, es, "vbt", [128, 2048], BF16, 2)
            gbs = Rot(fw, k, es, "gbt", [128, TOK], BF16, 3)
            ps = [es.enter_context(k.pp("psG%d" % i, [128, 512], F32)) for i in range(6)]
            r_ps = [fw.res("psG%d" % i) for i in range(6)]
            pi = 0
            def fetch_u(c):
                uf, r_uf = ufs.next()
                ub, r_ub = ubs.next()
                fw.dma("sp", uf[:], I["peer_uT"][c], writes=[r_uf])
                fw.op("act", lambda: nc.scalar.copy(out=ub[:].rearrange("p k e -> p (k e)"), in_=uf[:]), reads=[r_uf], writes=[r_ub])
                return ub, r_ub

            nxt_u = fetch_u(0)
            for c in range(128):
                ub, r_ub = nxt_u
                if c + 1 < 128:
                    nxt_u = fetch_u(c + 1)
                vf, r_vf = vfs.next()
                vb, r_vb = vbs.next()
                fw.dma("sp", vf[:], I["peer_v"][c * 128:(c + 1) * 128, :], writes=[r_vf])
                fw.op("dve", lambda: nc.vector.tensor_copy(out=vb[:], in_=vf[:]), reads=[r_vf], writes=[r_vb])
                fw.dma("sp", S["vb"][c * 128:(c + 1) * 128, :], vb[:], reads=[r_vb])
                gb, r_gb = gbs.next()
                for t0 in range(0, TOK, 512):
                    n = min(512, TOK - t0)
                    p_, r_p = ps[pi % 6], r_ps[pi % 6]
                    pi += 1
                    for kk in range(16):
                        fw.op("pe", lambda: nc.tensor.matmul(p_[:, 0:n], lhsT=ub[:, kk, :], rhs=xa[:, kk, t0:t0 + n], start=(kk == 0), stop=(kk == 15)),
                              reads=[r_ub, r_xa], writes=[r_p])
                    fw.op("act", lambda: nc.scalar.activation(out=gb[:, t0:t0 + n], in_=p_[:, 0:n], func=AF.Gelu_apprx_tanh), reads=[r_p], writes=[r_gb])
                fw.dma("sp", S["Hg"][:, :, c, :].rearrange("n e t -> e n t"), gb[:].rearrange("e (n t) -> e n t", t=128), reads=[r_gb])
        fw.barrier()
    k.phase_E1a = phase_E1a

    def phase_E1():
        NEG = -3.0e38
        DELTA = 4.0e-6
        with ExitStack() as es:
            def T(name, shape, dt=F32):
                return es.enter_context(k.sb(name, shape, dt)), fw.res(name)
            s12, r_s12 = T("s12m", [128, 8, 2, 128])
            Hgt = es.enter_context(k.sb("Hgt", [128, 128, 128], BF16))
            r_Hg = [fw.res("Hgt%d" % i) for i in range(8)]
            x2, r_x2 = T("x2m", [128, D])
            wrk, r_wrk = T("wrk", [128, 256])
            tops, r_tops = T("tops", [128, 8, 2, 16])
            cand, r_cand = T("cand", [128, 16, 16])
            c16, r_c16 = T("c16m", [128, 8, 16])
            e16, r_e16 = T("e16", [128, 8, 16])
            Z, r_Z = T("Z", [128, 8])
            tauD, r_tauD = T("tauD", [128, 8])
            thp, r_thp = T("thp", [128, 8, 16])
            e1n, r_e1n = T("e1n", [128, 8, 16])
            s2m, r_s2m = T("s2m", [128, 8, 128])
            E2b, r_E2b = T("E2b", [128, 8, 128], BF16)
            Rms = Rot(fw, k, es, "Rm", [128, 8, 128], BF16, 4)
            e1b, r_e1b = T("e1b", [128, 8, 16], BF16)
            As_ = Rot(fw, k, es, "As", [128, 8, 128], BF16, 4)
            AT, r_AT = T("ATf", [128, 128, 128], BF16)
            RT, r_RT = T("RT", [128, 128, 128], BF16)
            Wall = [T("Wall%d" % i, [128, 128, 128], BF16) for i in range(1)]
            vbs = Rot(fw, k, es, "vbt", [128, D], BF16, 8)
            acc = [es.enter_context(k.pp("acc%d" % i, [128, 512], F32)) for i in range(4)]
            r_acc = [fw.res("acc%d" % i) for i in range(4)]
            psX = [es.enter_context(k.pp("psX%d" % i, [128, 8, 128], BF16)) for i in range(2)]
            r_psX = [fw.res("psX%d" % i) for i in range(2)]
            psW = [es.enter_context(k.pp("psW%d" % i, [128, 4, 128], F32)) for i in range(2)]
            r_psW = [fw.res("psW%d" % i) for i in range(2)]
            st = {"x": 0, "w": 0}

            def prep(t):
                W, r_W = Wall[0]
                fw.dma("sp", s12[:].rearrange("p h q j -> p (h q j)"), S["s12"][t * 128:(t + 1) * 128, :], writes=[r_s12])
                for h in range(8):
                    for q in range(2):
                        fw.op("dve", lambda: nc.vector.max(out=tops[:, h, q, 0:8], in_=s12[:, h, q, :]), reads=[r_s12], writes=[r_tops])
                        fw.op("dve", lambda: nc.vector.match_replace(out=wrk[:, 0:128], in_to_replace=tops[:, h, q, 0:8], in_values=s12[:, h, q, :], imm_value=NEG),
                              reads=[r_s12, r_tops], writes=[r_wrk])
                        fw.op("dve", lambda: nc.vector.max(out=tops[:, h, q, 8:16], in_=wrk[:, 0:128]), reads=[r_wrk], writes=[r_tops])
                        yield 0.5
                for h in range(8):
                    fw.op("dve", lambda: nc.vector.tensor_tensor(out=cand[:], in0=tops[:, h, 0, :].unsqueeze(2).to_broadcast([128, 16, 16]),
                                                                 in1=tops[:, h, 1, :].unsqueeze(1).to_broadcast([128, 16, 16]), op=ALU.add), reads=[r_tops], writes=[r_cand])
                    cv_ = cand[:].rearrange("p a b -> p (a b)")
                    fw.op("dve", lambda: nc.vector.max(out=c16[:, h, 0:8], in_=cv_), reads=[r_cand], writes=[r_c16])
                    fw.op("dve", lambda: nc.vector.match_replace(out=wrk[:], in_to_replace=c16[:, h, 0:8], in_values=cv_, imm_value=NEG), reads=[r_cand, r_c16], writes=[r_wrk])
                    fw.op("dve", lambda: nc.vector.max(out=c16[:, h, 8:16], in_=wrk[:]), reads=[r_wrk], writes=[r_c16])
                    yield 1.0
                fw.op("dve", lambda: nc.vector.tensor_tensor(out=e16[:], in0=c16[:], in1=c16[:, :, 0:1].to_broadcast([128, 8, 16]), op=ALU.subtract), reads=[r_c16], writes=[r_e16])
                fw.op("act", lambda: nc.scalar.activation(out=e16[:], in_=e16[:], func=AF.Exp), reads=[r_e16], writes=[r_e16])
                fw.op("dve", lambda: nc.vector.tensor_reduce(out=Z[:], in_=e16[:], axis=AX.X, op=ALU.add), reads=[r_e16], writes=[r_Z])
                fw.op("dve", lambda: nc.vector.reciprocal(out=Z[:], in_=Z[:]), reads=[r_Z], writes=[r_Z])
                yield 1.0
                fw.op("dve", lambda: nc.vector.tensor_tensor(out=e1n[:], in0=tops[:, :, 0, :], in1=tops[:, :, 0, 0:1].to_broadcast([128, 8, 16]), op=ALU.subtract), reads=[r_tops], writes=[r_e1n])
                fw.op("act", lambda: nc.scalar.activation(out=e1n[:], in_=e1n[:], func=AF.Exp), reads=[r_e1n], writes=[r_e1n])
                fw.op("dve", lambda: nc.vector.tensor_tensor(out=e1b[:], in0=e1n[:], in1=Z[:].unsqueeze(2).to_broadcast([128, 8, 16]), op=ALU.mult), reads=[r_e1n, r_Z], writes=[r_e1b])
                fw.op("dve", lambda: nc.vector.tensor_scalar(out=tauD[:], in0=c16[:, :, 15], scalar1=-DELTA, scalar2=None, op0=ALU.add), reads=[r_c16], writes=[r_tauD])
                fw.op("dve", lambda: nc.vector.tensor_tensor(out=thp[:], in0=tauD[:].unsqueeze(2).to_broadcast([128, 8, 16]), in1=tops[:, :, 0, :], op=ALU.subtract), reads=[r_tauD, r_tops], writes=[r_thp])
                yield 1.0
                fw.op("dve", lambda: nc.vector.tensor_tensor(out=s2m[:], in0=s12[:, :, 1, :], in1=tops[:, :, 1, 0:1].to_broadcast([128, 8, 128]), op=ALU.subtract), reads=[r_s12, r_tops], writes=[r_s2m])
                fw.op("act", lambda: nc.scalar.activation(out=E2b[:], in_=s2m[:], func=AF.Exp), reads=[r_s2m], writes=[r_E2b])
                yield 1.0

                def r_front(j0):
                    Rm, r_Rm = Rms.next()
                    o_m = Rm[:].rearrange("p j (h k) -> p j h k", k=16)
                    s2b = s12[:, :, 1, j0:j0 + 8].rearrange("p h j -> p j h").unsqueeze(3).to_broadcast([128, 8, 8, 16])
                    e2b = E2b[:, :, j0:j0 + 8].rearrange("p h j -> p j h").unsqueeze(3).to_broadcast([128, 8, 8, 16])
                    fw.op("dve", lambda: nc.vector.tensor_tensor(out=o_m, in0=s2b, in1=thp[:].unsqueeze(1).to_broadcast([128, 8, 8, 16]), op=ALU.is_ge),
                          reads=[r_s12, r_thp], writes=[r_Rm])
                    fw.op("dve", lambda: nc.vector.tensor_tensor(out=o_m, in0=o_m, in1=e2b, op=ALU.mult), reads=[r_Rm, r_E2b], writes=[r_Rm])
                    return (j0, Rm, r_Rm)

                def r_back(a):
                    j0, Rm, r_Rm = a
                    pX, r_pX = psX[st["x"] % 2], r_psX[st["x"] % 2]
                    st["x"] += 1
                    for jj in range(8):
                        fw.op("pe", lambda: nc.tensor.transpose(out=pX[:, jj, :], in_=Rm[:, jj, :], identity=idb[:]), reads=[r_Rm, r_idb], writes=[r_pX])
                    fw.op("act", lambda: nc.scalar.copy(out=RT[:, :, j0:j0 + 8], in_=pX[:].rearrange("p j t -> p t j")), reads=[r_pX], writes=[r_RT])

                pend_r = []
                for j0 in range(0, 128, 8):
                    pend_r.append(r_front(j0))
                    if len(pend_r) > 2:
                        r_back(pend_r.pop(0))
                    yield 1.9
                while pend_r:
                    r_back(pend_r.pop(0))
                    yield 0.5

                def a_front(c1):
                    A_, r_A = As_.next()
                    o_a = A_[:].rearrange("p c (h k) -> p c h k", k=16)
                    fw.op("dve", lambda: nc.vector.tensor_tensor(out=o_a, in0=s12[:, :, 0, c1:c1 + 8].rearrange("p h c -> p c h").unsqueeze(3).to_broadcast([128, 8, 8, 16]),
                                                                 in1=tops[:, :, 0, :].unsqueeze(1).to_broadcast([128, 8, 8, 16]), op=ALU.is_equal),
                          reads=[r_s12, r_tops], writes=[r_A])
                    fw.op("dve", lambda: nc.vector.tensor_tensor(out=o_a, in0=o_a, in1=e1b[:].unsqueeze(1).to_broadcast([128, 8, 8, 16]), op=ALU.mult),
                          reads=[r_A, r_e1b], writes=[r_A])
                    return (c1, A_, r_A)

                def a_back(a):
                    c1, A_, r_A = a
                    pX, r_pX = psX[st["x"] % 2], r_psX[st["x"] % 2]
                    st["x"] += 1
                    for cc in range(8):
                        fw.op("pe", lambda: nc.tensor.transpose(out=pX[:, cc, :], in_=A_[:, cc, :], identity=idb[:]), reads=[r_A, r_idb], writes=[r_pX])
                    fw.op("act", lambda: nc.scalar.copy(out=AT[:, :, c1:c1 + 8], in_=pX[:].rearrange("p c t -> p t c")), reads=[r_pX], writes=[r_AT])

                pend_a = []
                for c1 in range(0, 128, 8):
                    pend_a.append(a_front(c1))
                    if len(pend_a) > 2:
                        a_back(pend_a.pop(0))
                    yield 1.9
                while pend_a:
                    a_back(pend_a.pop(0))
                    yield 0.5
                for t0 in range(0, 128, 4):
                    pW, r_pW = psW[st["w"] % 2], r_psW[st["w"] % 2]
                    st["w"] += 1
                    for tt in range(4):
                        tk = t0 + tt
                        fw.op("pe", lambda: nc.tensor.matmul(pW[:, tt, :], lhsT=RT[:, tk, :], rhs=AT[:, tk, :], start=True, stop=True),
                              reads=[r_RT, r_AT], writes=[r_pW])
                    fw.op("act", lambda: nc.scalar.copy(out=W[:, t0:t0 + 4, :], in_=pW[:]), reads=[r_pW], writes=[r_W])
                    yield 0.6

            def drain(g):
                for _ in g:
                    pass

            def load_hg(t, g):
                fw.dma("sp", Hgt[:, g * 16:(g + 1) * 16, :], S["Hg"][t, :, g * 16:(g + 1) * 16, :], writes=[r_Hg[g]])

            def gate(t, g):
                W, r_W = Wall[0]
                fw.op("dve", lambda: nc.vector.tensor_tensor(out=Hgt[:, g * 16:(g + 1) * 16, :], in0=Hgt[:, g * 16:(g + 1) * 16, :],
                                                             in1=W[:, :, g * 16:(g + 1) * 16].rearrange("j t c -> j c t"), op=ALU.mult),
                      reads=[r_Hg[g], r_W], writes=[r_Hg[g]])

            drain(prep(0))
            for g in range(8):
                load_hg(0, g)
            pend = list(range(8))
            for t in range(NT):
                for g in pend:
                    gate(t, g)
                pend = []
                nxt = prep(t + 1) if t + 1 < NT else None
                loaded = []
                budget = 0.0
                for c in range(128):
                    vb, r_vb = vbs.next()
                    fw.dma("sp", vb[:], S["vb"][c * 128:(c + 1) * 128, :], writes=[r_vb])
                    if c == 12 and t > 0:
                        load_hg(t, 7)
                    if c == 20 and t > 0:
                        gate(t, 7)
                    if c == 64:
                        fw.dma("sp", x2[:], S["x2"][t * 128:(t + 1) * 128, :], writes=[r_x2])
                    for cg in range(4):
                        fw.op("pe", lambda: nc.tensor.matmul(acc[cg][:], lhsT=Hgt[:, c, :], rhs=vb[:, cg * 512:(cg + 1) * 512], start=(c == 0), stop=(c == 127)),
                              reads=[r_Hg[c // 16], r_vb], writes=[r_acc[cg]])
                    budget += 1.0
                    while nxt is not None and budget > 0:
                        cost = next(nxt, None)
                        if cost is None:
                            nxt = None
                        else:
                            budget -= cost
                    if nxt is None and t + 1 < NT and loaded and budget > 0:
                        gate(t + 1, loaded.pop(0))
                        budget -= 3.0
                    if c % 16 == 9 and c >= 25 and t + 1 < NT:
                        load_hg(t + 1, (c - 25) // 16)
                        loaded.append((c - 25) // 16)
                if nxt is not None:
                    drain(nxt)
                pend = loaded
                for cg in range(4):
                    fw.op("dve", lambda: nc.vector.tensor_tensor(out=x2[:, cg * 512:(cg + 1) * 512], in0=acc[cg][:], in1=x2[:, cg * 512:(cg + 1) * 512], op=ALU.add),
                          reads=[r_acc[cg], r_x2], writes=[r_x2])
                fw.dma("sp", S["x3"][t * 128:(t + 1) * 128, :], x2[:], reads=[r_x2])
        fw.barrier()
    k.phase_E1 = phase_E1

    def phase_F():
        with ExitStack() as es:
            stg = Rot(fw, k, es, "stg", [128, 16, 256], F32, 2)
            (wpg, r_wpg), (wpl, r_wpl) = load_w_multi(es, [("wpg", I["w_ple_gate"], 16), ("wpl", I["w_ple"], 2)], D, stg)
            nt_ = norm_tiles(es, "f")
            fw.dma("sp", nt_["gB"][:], I["g_ple"], writes=[nt_["r_gB"]])
            xs = Rot(fw, k, es, "xs", [128, D], F32, 3)
            ys = Rot(fw, k, es, "ys", [128, D], F32, 2)
            xpTs = Rot(fw, k, es, "xpT", [128, 16, 128], BF16, 3)
            plf = Rot(fw, k, es, "plf", [128, 2, 128], F32, 3)
            plb = Rot(fw, k, es, "plb", [128, 2, 128], BF16, 3)
            sgs = Rot(fw, k, es, "sg", [128, 512], F32, 2)
            ps = [es.enter_context(k.pp("psF%d" % i, [128, 512], F32)) for i in range(4)]
            r_ps = [fw.res("psF%d" % i) for i in range(4)]
            pi = 0

            def f_prep(t):
                xt, r_xt = xs.next()
                fw.dma("sp", xt[:], S["x3"][t * 128:(t + 1) * 128, :], writes=[r_xt])
                pf, r_pf = plf.next()
                pb, r_pb = plb.next()
                fw.dma("act", pf[:], I["pleT"][:, t * 128:(t + 1) * 128].rearrange("(k p) t -> p k t", p=128), writes=[r_pf])
                fw.op("pool", lambda: nc.gpsimd.tensor_copy(out=pb[:], in_=pf[:]), reads=[r_pf], writes=[r_pb])
                xpT, r_xpT = xpTs.next()
                do_norm(es, nt_, xt, r_xt, xpT[:], r_xpT, t)
                return xt, r_xt, pb, r_pb, xpT, r_xpT

            nxt_f = f_prep(0)
            for t in range(NT):
                xt, r_xt, pb, r_pb, xpT, r_xpT = nxt_f
                if t + 1 < NT:
                    nxt_f = f_prep(t + 1)
                y, r_y = ys.next()
                for cg in range(4):
                    pg, r_pg = ps[pi % 4], r_ps[pi % 4]
                    pl, r_pl = ps[(pi + 1) % 4], r_ps[(pi + 1) % 4]
                    pi += 2
                    for kk in range(16):
                        fw.op("pe", lambda: nc.tensor.matmul(pg[:], lhsT=xpT[:, kk, :], rhs=wpg[:, kk, cg * 512:(cg + 1) * 512], start=(kk == 0), stop=(kk == 15)),
                              reads=[r_xpT, r_wpg[2 * cg], r_wpg[2 * cg + 1]], writes=[r_pg])
                    for kk in range(2):
                        fw.op("pe", lambda: nc.tensor.matmul(pl[:], lhsT=pb[:, kk, :], rhs=wpl[:, kk, cg * 512:(cg + 1) * 512], start=(kk == 0), stop=(kk == 1)),
                              reads=[r_pb, r_wpl[2 * cg], r_wpl[2 * cg + 1]], writes=[r_pl])
                    sg, r_sg = sgs.next()
                    fw.op("act", lambda: nc.scalar.activation(out=sg[:], in_=pg[:], func=AF.Sigmoid), reads=[r_pg], writes=[r_sg])
                    fw.op("dve", lambda: nc.vector.tensor_tensor(out=sg[:], in0=pl[:], in1=sg[:], op=ALU.mult), reads=[r_pl, r_sg], writes=[r_sg])
                    fw.op("dve", lambda: nc.vector.tensor_tensor(out=y[:, cg * 512:(cg + 1) * 512], in0=sg[:], in1=xt[:, cg * 512:(cg + 1) * 512], op=ALU.add),
                          reads=[r_sg, r_xt], writes=[r_y])
                fw.dma("sp", O["y"][t * 128:(t + 1) * 128, :], y[:], reads=[r_y])
        fw.barrier()
    k.phase_F = phase_F

    k.phase_A = phase_A
    k.I, k.S = I, S
    k.din, k.dout, k.scr = din, dout, scr
    return k


def emit_all(k, stop_after=None):
    I, S = k.I, k.S
    k.phase_A(I["x_prev"], NTP, [0, 1, 2, 3], S["zTp"], [12, 13], [NTP - 1], S["zkvp"], 3072)
    if stop_after == "A0":
        return
    fm = [g for g in range(30) if not (8 <= g <= 13)]
    k.phase_A(I["x_own"], NT, fm, S["zT"], list(range(8, 14)), list(range(NT)), S["zqkv"], 2048)
    if stop_after == "A":
        return
    k.phase_B()
    if stop_after == "B":
        return
    k.phase_C()
    if stop_after == "C":
        return
    k.phase_D()
    if stop_after == "D":
        return
    k.phase_E0()
    if stop_after == "E0":
        return
    k.phase_E1a()
    k.phase_E1()
    if stop_after == "E1":
        return
    k.phase_F()


def core_inputs(inp, c, shared):
    sq, half = c // 2, c % 2
    m = dict(shared)
    xo = np.zeros((TOK, D), np.float32)
    xo[0:2048] = inp["x_prompt"][sq, half * 2048:(half + 1) * 2048]
    xo[2048:2048 + 64] = inp["x_sample"][c * 16:(c + 1) * 16].reshape(64, D)
    m["x_own"] = xo
    if half == 1:
        m["x_prev"] = np.ascontiguousarray(inp["x_prompt"][sq, 0:2048])
    else:
        m["x_prev"] = np.zeros((2048, D), np.float32)
    m["state_convT"] = np.ascontiguousarray(inp["state_conv"][0, c * 16:(c + 1) * 16].reshape(16, 3, 8, 128).transpose(3, 2, 0, 1))
    m["state_hT"] = np.ascontiguousarray(inp["state_rglru"][0, c * 16:(c + 1) * 16].reshape(16, 8, 128).transpose(2, 1, 0))
    pos = np.zeros(TOK, np.float64)
    pos[0:2048] = half * 2048 + np.arange(2048)
    pos[2048:2048 + 64] = np.tile(16384 + np.arange(4), 16)
    m["cs_own"] = rope_table(pos)
    m["cs_prev"] = rope_table(np.arange(1920, 2048).astype(np.float64))
    tri = (np.arange(128)[:, None] <= np.arange(128)[None, :]).astype(np.float32)
    mk = np.zeros((128, 3, 128), np.float32)
    mk[:, 0] = tri
    mk[:, 1] = 1.0 - tri
    mk[:, 2] = (1.0 - tri) * float(half)
    m["masks"] = mk
    ck = inp["cache_k"][0, c * 16:(c + 1) * 16]
    cv = inp["cache_v"][0, c * 16:(c + 1) * 16]
    m["cache_kT"] = np.ascontiguousarray(ck.transpose(3, 0, 2, 1))
    m["cache_vT"] = np.ascontiguousarray(cv.transpose(1, 0, 2, 3))
    m["cache_k_nat"] = np.ascontiguousarray(ck.reshape(16, 128, 256))
    m["cache_v_nat"] = np.ascontiguousarray(cv.reshape(16, 128, 256))
    pt = np.zeros((TOK, 256), np.float32)
    pt[0:2048] = inp["p_prompt"][0, sq, half * 2048:(half + 1) * 2048]
    pt[2048:2048 + 64] = inp["p_sample"][0, c * 16:(c + 1) * 16].reshape(64, 256)
    m["pleT"] = np.ascontiguousarray(pt.T)
    fl = np.zeros((128, 2), np.float32)
    fl[:, 0] = float(half)
    fl[:, 1] = 1.0 - float(half)
    m["flags"] = fl
    return m


def rope_table(pos):
    inv = (np.float32(500000.0) ** (-np.arange(0, 16, 2, dtype=np.float32) / np.float32(16))).astype(np.float32)
    ang = pos.astype(np.float32)[:, None] * inv[None, :]
    return np.concatenate([np.cos(ang), np.sin(ang)], 1).astype(np.float32)


def shared_inputs(inp):
    sh = {}
    for nm_ in ("w_proj_rnn", "w_proj_attn", "w_out", "w_peer_q", "w_ple_gate", "w_ple", "peer_v"):
        sh[nm_] = np.ascontiguousarray(inp[nm_][0])
    sh["g_ffn"] = np.ascontiguousarray(np.broadcast_to(inp["norm_ffn"][0][None, :], (128, D)))
    sh["g_ple"] = np.ascontiguousarray(np.broadcast_to(inp["norm_ple"][0][None, :], (128, D)))
    sh["skT"] = np.ascontiguousarray(inp["peer_sub_keys"][0].reshape(16, 128, 128).transpose(2, 0, 1))
    sh["peer_uT"] = np.ascontiguousarray(inp["peer_u"][0].reshape(128, 128, 16, 128).transpose(0, 3, 2, 1).reshape(128, 128, 2048))
    sh["gqk"] = np.ascontiguousarray(np.broadcast_to(np.concatenate([np.tile(inp["q_norm"][0], 16), np.tile(inp["k_norm"][0], 4)])[None, :], (128, 1280)))
    sk = inp["attn_sinks"][0]
    sh["sinkB"] = np.ascontiguousarray(np.broadcast_to(sk[None, :], (128, 16)))
    sT = np.zeros((64, 4, 16), np.float32)
    for g in range(4):
        for h in range(4):
            sT[:, g, h * 4:(h + 1) * 4] = sk[4 * g + h]
    sh["sinkT"] = sT
    j = np.arange(128)[:, None]
    i = np.tile(np.arange(4), 4)[None, :]
    sh["smask_c"] = (j > i).astype(np.float32)
    kt = np.arange(64)
    sh["smask_n"] = ((kt[:, None] // 4 == kt[None, :] // 4) & (kt[:, None] % 4 <= kt[None, :] % 4)).astype(np.float32)
    sh["ident"] = np.eye(128, dtype=np.float32)
    sh["g_mix"] = np.ascontiguousarray(np.broadcast_to(inp["norm_mix"][0][None, :], (128, D)))
    sh["w_in"] = np.ascontiguousarray(inp["w_in"][0])
    v4 = np.stack([inp["conv_b"][0], inp["b_rgate"][0], inp["b_igate"][0], inp["lru_lambda"][0]], -1)
    sh["rnn_vec"] = np.ascontiguousarray(v4.reshape(8, 128, 4).transpose(1, 0, 2))
    sh["conv_wT"] = np.ascontiguousarray(inp["conv_w"][0].reshape(4, 8, 128).transpose(2, 1, 0))
    wg = np.stack([inp["w_rgate"][0], inp["w_igate"][0]], 0)
    sh["w_gates"] = np.ascontiguousarray(wg.transpose(2, 0, 1, 3))
    return sh


_CACHE = {}


def kernel(**inputs):
    inp = {k_: np.asarray(v) for k_, v in inputs.items()}
    if "k" not in _CACHE:
        k = build()
        emit_all(k)
        k.fw.finish()
        _CACHE["k"] = k
    k = _CACHE["k"]
    sh = shared_inputs(inp)
    in_maps = [core_inputs(inp, c, sh) for c in range(8)]
    res = run_bass_kernel_spmd(k.nc, in_maps, core_ids=list(range(8)))
    R = res.results
    f32 = np.float32
    y_prompt = np.zeros((4, 4096, D), f32)
    y_sample = np.zeros((128, 4, D), f32)
    prompt_conv = np.zeros((1, 4, 3, 1024), f32)
    prompt_rglru = np.zeros((1, 4, 1024), f32)
    prompt_k = np.zeros((1, 4, 128, 4, 64), f32)
    prompt_v = np.zeros((1, 4, 128, 4, 64), f32)
    sample_conv = np.zeros((1, 128, 3, 1024), f32)
    sample_rglru = np.zeros((1, 128, 1024), f32)
    sample_k = np.zeros((1, 128, 128, 4, 64), f32)
    sample_v = np.zeros((1, 128, 128, 4, 64), f32)
    for c in range(8):
        r = R[c]
        sq, half = c // 2, c % 2
        y = np.asarray(r["y"])
        y_prompt[sq, half * 2048:(half + 1) * 2048] = y[0:2048]
        y_sample[c * 16:(c + 1) * 16] = y[2048:2048 + 64].reshape(16, 4, D)
        if half == 1:
            prompt_conv[0, sq] = np.asarray(r["o_pconv"]).T
            prompt_rglru[0, sq] = np.asarray(r["o_prglru"])[:, 0]
            prompt_k[0, sq] = np.asarray(r["o_pk"]).reshape(128, 4, 64)
            prompt_v[0, sq] = np.asarray(r["o_pv"]).reshape(128, 4, 64)
        sl = slice(c * 16, (c + 1) * 16)
        sample_conv[0, sl] = np.asarray(r["o_sconv"]).transpose(1, 2, 0)
        sample_rglru[0, sl] = np.asarray(r["o_srglru"]).T
        sample_k[0, sl] = np.concatenate([np.asarray(r["o_sk"]), np.asarray(r["o_skn"]).reshape(16, 4, 256)], 1).reshape(16, 128, 4, 64)
        sample_v[0, sl] = np.concatenate([np.asarray(r["o_sv"]), np.asarray(r["o_svn"]).reshape(16, 4, 256)], 1).reshape(16, 128, 4, 64)
    return (y_prompt, y_sample, prompt_conv, prompt_rglru, prompt_k, prompt_v, sample_conv, sample_rglru, sample_k, sample_v)
```
